# Optimizing a Trainium2 kernel written in Bass

```python
import math
import jax, jax.numpy as jnp
from jax import lax
import numpy as np

D_MODEL = 1024
BATCH = 8
SEQ = 8192
DEPTH = 1
DEC_BATCH = 1
DEC_SEQ = 16384
PAST_LEN = 128

N_HEADS = 4
HEAD_DIM = 64
V_DIM = 2 * HEAD_DIM
ATTN_WIDTH = N_HEADS * 2 * HEAD_DIM
CONV_WIDTH = 512
CONV_KERNEL = 31
D_FF = 2816
NUM_BUCKETS = 32
MAX_DISTANCE = 128
Q_BLOCK = 128
ALPHA = (2.0 * DEPTH) ** 0.25
BETA = (8.0 * DEPTH) ** -0.25
LN_EPS = 1e-5
ATTN_SCALE = HEAD_DIM ** -0.5
COL_U = 2 * CONV_WIDTH
COL_Q = ATTN_WIDTH
COL_K = ATTN_WIDTH
COL_V = N_HEADS * V_DIM
COL_G = 2 * D_MODEL
PROJ_COLS = COL_U + COL_Q + COL_K + COL_V + COL_G

kernel_name = "hybrid_conv_diffattn_encoder"


def layer_norm(x, g, b):
    xf = x.astype(jnp.float32)
    mu = jnp.mean(xf, -1, keepdims=True)
    var = jnp.mean(jnp.square(xf - mu), -1, keepdims=True)
    y = (xf - mu) * lax.rsqrt(var + LN_EPS) * g.astype(jnp.float32) + b.astype(jnp.float32)
    return y.astype(x.dtype)


def rms_norm(x, g):
    xf = x.astype(jnp.float32)
    y = xf * lax.rsqrt(jnp.mean(jnp.square(xf), -1, keepdims=True) + LN_EPS) * g.astype(jnp.float32)
    return y.astype(x.dtype)


def swiglu_ffn(x, w_gu, w_down):
    h = x @ w_gu
    a, g = h[..., :D_FF], h[..., D_FF:]
    return (jax.nn.silu(a) * g) @ w_down


def t5_bucket(rel):
    nb = NUM_BUCKETS // 2
    ret = jnp.where(rel > 0, nb, 0)
    n = jnp.abs(rel)
    max_exact = nb // 2
    nf = jnp.maximum(n, 1).astype(jnp.float32)
    large = max_exact + (jnp.log(nf / max_exact) / math.log(MAX_DISTANCE / max_exact)
                         * (nb - max_exact)).astype(jnp.int32)
    large = jnp.minimum(large, nb - 1)
    return ret + jnp.where(n < max_exact, n, large)


def conv_branch(u, w_dw, b_dw, ln_g, ln_b, w_out):
    h = u[..., :CONV_WIDTH] * jax.nn.sigmoid(u[..., CONV_WIDTH:])
    h = lax.conv_general_dilated(h, w_dw, window_strides=(1,),
                                 padding=((CONV_KERNEL // 2, CONV_KERNEL // 2),),
                                 dimension_numbers=('NWC', 'WIO', 'NWC'),
                                 feature_group_count=CONV_WIDTH) + b_dw
    h = jax.nn.silu(layer_norm(h, ln_g, ln_b))
    return h @ w_out


def diff_attention(q, k, v, lam, table):
    B, S = q.shape[0], q.shape[1]
    nblk = S // Q_BLOCK
    qb = q.reshape(B, nblk, Q_BLOCK, N_HEADS, 2, HEAD_DIM).transpose(1, 0, 2, 3, 4, 5)
    k_pos = jnp.arange(S, dtype=jnp.int32)

    def block(args):
        qi, idx = args
        q_pos = idx * Q_BLOCK + jnp.arange(Q_BLOCK, dtype=jnp.int32)
        bucket = t5_bucket(k_pos[None, :] - q_pos[:, None])
        bias = jnp.transpose(table[bucket], (2, 0, 1)).astype(jnp.float32)
        logits = jnp.einsum('bqhmd,bkhmd->bmhqk', qi, k).astype(jnp.float32) * ATTN_SCALE
        p = jax.nn.softmax(logits + bias[None, None], axis=-1)
        a = p[:, 0] - lam * p[:, 1]
        return jnp.einsum('bhqk,bkhe->bqhe', a.astype(v.dtype), v)

    out = lax.map(block, (qb, jnp.arange(nblk, dtype=jnp.int32)))
    return out.transpose(1, 0, 2, 3, 4).reshape(B, S, N_HEADS, V_DIM)


def mixer(x, table, w_in, b_gate, conv_w_dw, conv_b_dw, conv_ln_g, conv_ln_b, w_conv_out,
          lq1, lk1, lq2, lk2, subln_g, w_attn_out, w_o, layer_idx):
    B, S = x.shape[0], x.shape[1]
    p = x @ w_in
    o = 0
    u = p[..., o:o + COL_U]; o += COL_U
    q = p[..., o:o + COL_Q].reshape(B, S, N_HEADS, 2, HEAD_DIM); o += COL_Q
    k = p[..., o:o + COL_K].reshape(B, S, N_HEADS, 2, HEAD_DIM); o += COL_K
    v = p[..., o:o + COL_V].reshape(B, S, N_HEADS, V_DIM); o += COL_V
    gates = jax.nn.sigmoid(p[..., o:o + COL_G] + b_gate)
    g_conv, g_attn = gates[..., :D_MODEL], gates[..., D_MODEL:]

    y_conv = conv_branch(u, conv_w_dw, conv_b_dw, conv_ln_g, conv_ln_b, w_conv_out)

    lambda_init = 0.8 - 0.6 * math.exp(-0.3 * layer_idx)
    f32 = jnp.float32
    lam = (jnp.exp(jnp.sum(lq1.astype(f32) * lk1.astype(f32)))
           - jnp.exp(jnp.sum(lq2.astype(f32) * lk2.astype(f32))) + lambda_init)
    att = diff_attention(q, k, v, lam, table)
    att = rms_norm(att, subln_g) * (1.0 - lambda_init)
    y_attn = att.reshape(B, S, N_HEADS * V_DIM) @ w_attn_out

    merged = g_conv * y_conv + g_attn * y_attn
    return merged @ w_o


def trunk(x, rel_bias_table, ffn1_w_gu, ffn1_w_down, ln1_g, ln1_b, w_in, b_gate,
          conv_w_dw, conv_b_dw, conv_ln_g, conv_ln_b, w_conv_out,
          lambda_q1, lambda_k1, lambda_q2, lambda_k2, subln_g, w_attn_out, w_o,
          ln2_g, ln2_b, ffn2_w_gu, ffn2_w_down, ln3_g, ln3_b):
    for l in range(DEPTH):
        x = layer_norm(ALPHA * x + 0.5 * swiglu_ffn(x, ffn1_w_gu[l], ffn1_w_down[l]), ln1_g[l], ln1_b[l])
        m = mixer(x, rel_bias_table, w_in[l], b_gate[l], conv_w_dw[l], conv_b_dw[l],
                  conv_ln_g[l], conv_ln_b[l], w_conv_out[l], lambda_q1[l], lambda_k1[l],
                  lambda_q2[l], lambda_k2[l], subln_g[l], w_attn_out[l], w_o[l], l)
        x = layer_norm(ALPHA * x + m, ln2_g[l], ln2_b[l])
        x = layer_norm(ALPHA * x + 0.5 * swiglu_ffn(x, ffn2_w_gu[l], ffn2_w_down[l]), ln3_g[l], ln3_b[l])
    return x


def setup_inputs(seed: int = 0) -> dict:
    key = jax.random.key(seed)
    ks = iter(jax.random.split(key, 40))

    def nrm(shape, scale):
        return jax.random.normal(next(ks), shape, jnp.float32) * scale

    def gain(shape):
        return 1.0 + nrm(shape, 0.02)

    L, D = DEPTH, D_MODEL
    x_prompt = nrm((BATCH, SEQ, D), 1.0)
    x_sample = nrm((DEC_BATCH, DEC_SEQ, D), 1.0)
    rel_bias_table = nrm((NUM_BUCKETS, N_HEADS), 0.1)
    ffn1_w_gu = nrm((L, D, 2 * D_FF), D ** -0.5)
    ffn1_w_down = nrm((L, D_FF, D), D_FF ** -0.5 * BETA)
    ln1_g = gain((L, D)); ln1_b = nrm((L, D), 0.02)
    w_uqk = nrm((L, D, COL_U + COL_Q + COL_K), D ** -0.5)
    w_v = nrm((L, D, COL_V), D ** -0.5 * BETA)
    w_g = nrm((L, D, COL_G), D ** -0.5)
    w_in = jnp.concatenate([w_uqk, w_v, w_g], axis=-1)
    b_gate = nrm((L, COL_G), 0.02)
    conv_w_dw = nrm((L, CONV_KERNEL, 1, CONV_WIDTH), CONV_KERNEL ** -0.5)
    conv_b_dw = nrm((L, CONV_WIDTH), 0.02)
    conv_ln_g = gain((L, CONV_WIDTH)); conv_ln_b = nrm((L, CONV_WIDTH), 0.02)
    w_conv_out = nrm((L, CONV_WIDTH, D), CONV_WIDTH ** -0.5 * BETA)
    lambda_q1 = nrm((L, HEAD_DIM), 0.1)
    lambda_k1 = nrm((L, HEAD_DIM), 0.1)
    lambda_q2 = nrm((L, HEAD_DIM), 0.1)
    lambda_k2 = nrm((L, HEAD_DIM), 0.1)
    subln_g = gain((L, V_DIM))
    w_attn_out = nrm((L, N_HEADS * V_DIM, D), (N_HEADS * V_DIM) ** -0.5 * BETA)
    w_o = nrm((L, D, D), D ** -0.5 * BETA)
    ln2_g = gain((L, D)); ln2_b = nrm((L, D), 0.02)
    ffn2_w_gu = nrm((L, D, 2 * D_FF), D ** -0.5)
    ffn2_w_down = nrm((L, D_FF, D), D_FF ** -0.5 * BETA)
    ln3_g = gain((L, D)); ln3_b = nrm((L, D), 0.02)
    return {"x_prompt": x_prompt, "x_sample": x_sample, "rel_bias_table": rel_bias_table,
            "ffn1_w_gu": ffn1_w_gu, "ffn1_w_down": ffn1_w_down, "ln1_g": ln1_g, "ln1_b": ln1_b,
            "w_in": w_in, "b_gate": b_gate, "conv_w_dw": conv_w_dw, "conv_b_dw": conv_b_dw,
            "conv_ln_g": conv_ln_g, "conv_ln_b": conv_ln_b, "w_conv_out": w_conv_out,
            "lambda_q1": lambda_q1, "lambda_k1": lambda_k1, "lambda_q2": lambda_q2,
            "lambda_k2": lambda_k2, "subln_g": subln_g, "w_attn_out": w_attn_out, "w_o": w_o,
            "ln2_g": ln2_g, "ln2_b": ln2_b, "ffn2_w_gu": ffn2_w_gu, "ffn2_w_down": ffn2_w_down,
            "ln3_g": ln3_g, "ln3_b": ln3_b}


def reference(x_prompt, x_sample, rel_bias_table, ffn1_w_gu, ffn1_w_down, ln1_g, ln1_b,
              w_in, b_gate, conv_w_dw, conv_b_dw, conv_ln_g, conv_ln_b, w_conv_out,
              lambda_q1, lambda_k1, lambda_q2, lambda_k2, subln_g, w_attn_out, w_o,
              ln2_g, ln2_b, ffn2_w_gu, ffn2_w_down, ln3_g, ln3_b):
    y_prompt = trunk(x_prompt, rel_bias_table, ffn1_w_gu, ffn1_w_down, ln1_g, ln1_b, w_in, b_gate,
                     conv_w_dw, conv_b_dw, conv_ln_g, conv_ln_b, w_conv_out,
                     lambda_q1, lambda_k1, lambda_q2, lambda_k2, subln_g, w_attn_out, w_o,
                     ln2_g, ln2_b, ffn2_w_gu, ffn2_w_down, ln3_g, ln3_b)
    y_sample = trunk(x_sample, rel_bias_table, ffn1_w_gu, ffn1_w_down, ln1_g, ln1_b, w_in, b_gate,
                     conv_w_dw, conv_b_dw, conv_ln_g, conv_ln_b, w_conv_out,
                     lambda_q1, lambda_k1, lambda_q2, lambda_k2, subln_g, w_attn_out, w_o,
                     ln2_g, ln2_b, ffn2_w_gu, ffn2_w_down, ln3_g, ln3_b)
    return (y_prompt, y_sample)
```

```python
import contextlib
import math
import numpy as np
import concourse.bass as bass
import concourse.mybir as mybir
from concourse.bass_utils import run_bass_kernel_spmd

F32 = mybir.dt.float32
BF16 = mybir.dt.bfloat16
AF = mybir.ActivationFunctionType
ALU = mybir.AluOpType
AX = mybir.AxisListType

D = 1024
DFF = 2816
NFC = DFF // 128
PROJ = 4608
COL_Q, COL_K, COL_V, COL_G = 1024, 1536, 2048, 2560
ALPHA = 2.0 ** 0.25
LAMBDA_INIT = 0.8 - 0.6 * math.exp(0.0)
LN_EPS = 1e-5
LB = 1280
ENGS = ("pe", "act", "dve", "pool", "sp")
NDMASEM = 12


class Res:
    __slots__ = ("name", "writer", "readers")

    def __init__(self, name=""):
        self.name = name
        self.writer = None
        self.readers = {}


class Op:
    __slots__ = ("eng", "fn", "deps", "needed", "is_dma", "sem", "val")

    def __init__(self, eng, fn, is_dma):
        self.eng = eng
        self.fn = fn
        self.deps = []
        self.needed = False
        self.is_dma = is_dma
        self.sem = None
        self.val = None


class Sched:
    def __init__(self, nc):
        self.nc = nc
        self.ops = {e: [] for e in ENGS}
        self.dmas = {e: [] for e in ENGS}
        self.pending = {e: [] for e in ENGS}

    def full_barrier(self):
        deps = []
        for e in ENGS:
            comp = [o for o in self.ops[e][-1:] if not o.is_dma]
            for o in reversed(self.ops[e]):
                if not o.is_dma:
                    comp = [o]
                    break
            deps.extend(comp)
            deps.extend(self.dmas[e][-NDMASEM:])
        for o in deps:
            o.needed = True
        for e in ENGS:
            self.pending[e] = list(deps)

    def _track(self, op, reads, writes):
        deps = []
        if self.pending[op.eng]:
            deps.extend(self.pending[op.eng])
            self.pending[op.eng] = []
        for r in reads:
            if r.writer is not None:
                deps.append(r.writer)
        for w in writes:
            if w.writer is not None:
                deps.append(w.writer)
            for o in w.readers.values():
                if isinstance(o, list):
                    deps.extend(o)
                else:
                    deps.append(o)
        for r in reads:
            if op.is_dma:
                r.readers.setdefault("dma_" + op.eng, []).append(op)
            else:
                r.readers[op.eng] = op
        for w in writes:
            w.writer = op
            w.readers = {}
        keep = []
        for d in deps:
            if d is op:
                continue
            if d.eng == "pe" and op.eng == "pe" and not d.is_dma and not op.is_dma:
                continue
            keep.append(d)
            d.needed = True
        op.deps = keep

    def op(self, eng, fn, reads=(), writes=()):
        o = Op(eng, fn, False)
        self._track(o, reads, writes)
        self.ops[eng].append(o)
        return o

    def dma(self, eng, out, in_, reads=(), writes=(), **kw):
        o = Op(eng, (lambda e, out=out, in_=in_, kw=kw: e.dma_start(out=out, in_=in_, **kw)), True)
        o.needed = True
        self._track(o, reads, writes)
        self.ops[eng].append(o)
        self.dmas[eng].append(o)
        return o

    def pe(self, fn, reads=(), writes=()):
        return self.op("pe", fn, reads, writes)

    def act(self, fn, reads=(), writes=()):
        return self.op("act", fn, reads, writes)

    def dve(self, fn, reads=(), writes=()):
        return self.op("dve", fn, reads, writes)

    def pool(self, fn, reads=(), writes=()):
        return self.op("pool", fn, reads, writes)

    def emit(self):
        nc = self.nc
        csem = {e: nc.alloc_semaphore(name=f"c_{e}") for e in ENGS}
        dsem = {e: [nc.alloc_semaphore(name=f"d_{e}_{j}") for j in range(NDMASEM)] for e in ("sp", "pool")}
        last_on = {}
        for e in ENGS:
            n = 0
            nd = 0
            for o in self.ops[e]:
                if o.is_dma:
                    j = nd % NDMASEM
                    o.sem = dsem[e][j]
                    o.val = 16 * (nd // NDMASEM + 1)
                    if (e, j) in last_on:
                        o.deps.append(last_on[(e, j)])
                    last_on[(e, j)] = o
                    nd += 1
                elif o.needed:
                    n += 1
                    o.sem = csem[e]
                    o.val = n
        finals = list(last_on.values())
        stats = {}
        with nc.Block() as block:
            def body(e):
                def run(eng):
                    seen = {}
                    nw = 0
                    for o in self.ops[e]:
                        for d in o.deps:
                            k = id(d.sem)
                            if seen.get(k, 0) < d.val:
                                eng.wait_ge(d.sem, d.val)
                                seen[k] = d.val
                                nw += 1
                        ins = o.fn(eng)
                        if o.is_dma:
                            ins.then_inc(o.sem, 16)
                        elif o.needed:
                            ins.then_inc(o.sem, 1)
                    if e == "sp":
                        for o in finals:
                            k = id(o.sem)
                            if seen.get(k, 0) < o.val:
                                eng.wait_ge(o.sem, o.val)
                                seen[k] = o.val
                    stats[e] = (len(self.ops[e]), nw)
                return run
            block.tensor(body("pe"))
            block.scalar(body("act"))
            block.vector(body("dve"))
            block.gpsimd(body("pool"))
            block.sync(body("sp"))
        return stats


class Ring:
    def __init__(self, views):
        self.views = views
        self.res = [Res() for _ in views]
        self.i = 0

    def next(self):
        k = self.i % len(self.views)
        self.i += 1
        return self.views[k], self.res[k]


def build(SP, SS, NQ, debug=False):
    assert SP % 512 == 0 and SS % 512 == 0 and NQ % 512 == 0 and SS >= NQ + 1024
    R = SP + SS
    NM = SP + NQ
    NBP, NBS = SP // 128, SS // 128
    HB = 32 + SP
    HC = HB + NQ + 256
    nc = bass.Bass("TRN2", target_bir_lowering=False)
    S = Sched(nc)

    def din(name, shape):
        return nc.dram_tensor(name, shape, F32, kind="ExternalInput").ap()

    def dscr(name, shape, dt):
        if debug:
            return nc.dram_tensor(name, shape, dt, kind="ExternalOutput").ap()
        return nc.dram_tensor(name, shape, dt).ap()

    xr = din("xr", [R, D])
    w_ffn = [(din("ffn1_gu", [D, 2 * DFF]), din("ffn1_d", [DFF, D])),
             (din("ffn2_gu", [D, 2 * DFF]), din("ffn2_d", [DFF, D]))]
    w_in = din("w_in", [D, PROJ])
    w_co = din("w_co", [512, D])
    w_ao = din("w_ao", [512, D])
    w_o = din("w_o", [D, D])
    lnv = din("lnv", [6, D])
    NCOL = 16 + 124 + 12 + 4
    colp = din("colp", [128, NCOL])
    wrapm = din("wrapm", [128, NBS])
    sublg = din("sublg", [1, 128])
    lamv = din("lamv", [1, 256])
    tab = din("tab", [32, 4])
    oh = din("oh", [32, LB])
    ident = din("ident", [128, 128])
    yo = nc.dram_tensor("yo", [NM, D], F32, kind="ExternalOutput").ap()

    x1 = dscr("x1", [R, D], F32)
    hT = dscr("hT", [512, HC], F32)
    QT = dscr("QT", [4, 128, NM], BF16)
    KT = dscr("KT", [4, 128, R], BF16)
    Vs = dscr("Vs", [R, 4 * 129], BF16)
    brep = dscr("brep", [4, 128, LB], F32)
    cactT = dscr("cactT", [512, NM], BF16)
    attnT = dscr("attnT", [512, NM], BF16)
    x2 = dscr("x2", [NM, D], F32)
    r_x1, r_hT, r_QT, r_KT, r_Vs, r_brep, r_cact, r_attn, r_x2 = (Res() for _ in range(9))

    def dram_ap(t, offset, pattern):
        return bass.AP(t.tensor, offset, pattern)

    def bcast_rows(src, row, n, parts=128):
        return dram_ap(src, row * src.shape[1], [[0, parts], [1, n]])

    with contextlib.ExitStack() as top:
        def SB(st, name, shape, dt):
            return st.enter_context(nc.sbuf_tensor(name, shape, dt))

        ps = top.enter_context(nc.psum_tensor("ps", [128, 7, 512], F32))
        ps_res = [Res(f"ps{b}") for b in range(7)]
        pstT = top.enter_context(nc.psum_tensor("pstT", [128, 1024], BF16))
        r_pstT = Res("pstT")

        idf = SB(top, "idf", [128, 128], F32)
        idb = SB(top, "idb", [128, 128], BF16)
        colt = SB(top, "colt", [128, NCOL], F32)
        lam = SB(top, "lam", [128, 8], F32)
        r_idf, r_idb, r_colt, r_lam = Res(), Res(), Res(), Res()
        S.dma("sp", idf[:], ident[:, :], writes=[r_idf])
        S.dma("sp", colt[:], colp[:, :], writes=[r_colt])
        S.dve(lambda e: e.tensor_copy(out=idb[:], in_=idf[:]), reads=[r_idf], writes=[r_idb])
        C_BG, C_CW, C_CB, C_CG, C_CBE, C_MK = 0, 16, 140, 144, 148, 152

        def colv(c):
            return colt[:, c:c + 1]

        with contextlib.ExitStack() as st:
            lv = SB(st, "lv", [128, 256], F32)
            lp = SB(st, "lp", [128, 128], F32)
            r_lv, r_lp = Res(), Res()
            S.dma("sp", lv[:], bcast_rows(lamv, 0, 256), writes=[r_lv])
            S.dve(lambda e: e.tensor_tensor(out=lp[:, 0:64], in0=lv[:, 0:64], in1=lv[:, 64:128], op=ALU.mult), reads=[r_lv], writes=[r_lp])
            S.dve(lambda e: e.tensor_tensor(out=lp[:, 64:128], in0=lv[:, 128:192], in1=lv[:, 192:256], op=ALU.mult), reads=[r_lv, r_lp], writes=[r_lp])
            S.dve(lambda e: e.reduce_sum(out=lam[:, 0:1], in_=lp[:, 0:64], axis=AX.X), reads=[r_lp], writes=[r_lam])
            S.dve(lambda e: e.reduce_sum(out=lam[:, 1:2], in_=lp[:, 64:128], axis=AX.X), reads=[r_lp, r_lam], writes=[r_lam])
            S.act(lambda e: e.activation(out=lam[:, 2:4], in_=lam[:, 0:2], func=AF.Exp), reads=[r_lam], writes=[r_lam])
            S.dve(lambda e: e.tensor_tensor(out=lam[:, 4:5], in0=lam[:, 3:4], in1=lam[:, 2:3], op=ALU.subtract), reads=[r_lam], writes=[r_lam])
            S.dve(lambda e: e.tensor_scalar_add(out=lam[:, 5:6], in0=lam[:, 4:5], scalar1=-LAMBDA_INIT), reads=[r_lam], writes=[r_lam])
            tb = SB(st, "tb", [32, 4, 128], F32)
            oht = SB(st, "oht", [32, LB], F32)
            gb = SB(st, "gb", [128, LB], F32)
            r_tb, r_oht, r_gb = Res(), Res(), Res()
            S.dma("sp", oht[:], oh[:, :], writes=[r_oht])
            tabt = SB(st, "tabt", [32, 4], F32)
            r_tabt = Res()
            S.dma("sp", tabt[:], tab[:, :], writes=[r_tabt])
            S.pool(lambda e: e.memset(tb[:], 1.0), writes=[r_tb])
            for h in range(4):
                S.dve(lambda e, h=h: e.tensor_scalar_mul(out=tb[:, h, :], in0=tb[:, h, :], scalar1=tabt[:, h:h + 1]), reads=[r_tb, r_tabt], writes=[r_tb])
            for h in range(4):
                for (c0, c1) in ((0, 512), (512, 1024), (1024, LB)):
                    S.pe(lambda e, h=h, c0=c0, c1=c1: e.matmul(ps[:, 0, 0:c1 - c0], tb[:, h, :], oht[:, c0:c1], start=True, stop=True),
                         reads=[r_tb, r_oht], writes=[ps_res[0]])
                    S.dve(lambda e, c0=c0, c1=c1: e.tensor_copy(out=gb[:, c0:c1], in_=ps[:, 0, 0:c1 - c0]), reads=[ps_res[0]], writes=[r_gb])
                S.dma("sp", brep[h], gb[:], reads=[r_gb], writes=[r_brep])

        S.full_barrier()

        def load_weight(st, name, src, K, N, queue="pool"):
            kc = K // 128
            t = SB(st, name, [128, kc, N], BF16)
            r = Res(name)
            for c in range(kc):
                S.dma(queue, t[:, c, :], src[c * 128:(c + 1) * 128, :], writes=[r])
            return t, r

        def load_T(rows_ap_fn, nsub, xin_ring, xbf_ring, xT, r_xT, tbank, r_src=None):
            for s in range(nsub):
                xin, r_xin = xin_ring.next()
                xbf, r_xbf = xbf_ring.next()
                S.dma("sp", xin, rows_ap_fn(s), writes=[r_xin])
                S.act(lambda e, xbf=xbf, xin=xin: e.copy(out=xbf, in_=xin), reads=[r_xin], writes=[r_xbf])
                pst = pstT[:, :]
                for c in range(8):
                    S.pe(lambda e, c=c, xbf=xbf, pst=pst: e.transpose(pst[:, c * 128:(c + 1) * 128], xbf[:, c * 128:(c + 1) * 128], idb[:]),
                         reads=[r_xbf, r_idb], writes=[r_pstT])
                S.dve(lambda e, s=s, pst=pst: e.tensor_copy(out=xT[:, :, s * 128:(s + 1) * 128],
                                                           in_=pst.rearrange("p (c t) -> p c t", c=8)),
                      reads=[r_pstT], writes=[r_xT])

        def layer_norm_store(st_tiles, z, r_z, gB, bB, r_gb, dst_ap):
            stt, r_stt, mv, r_mv = st_tiles
            S.dve(lambda e: e.bn_stats(out=stt[:, 0:6], in_=z[:, 0:512]), reads=[r_z], writes=[r_stt])
            S.dve(lambda e: e.bn_stats(out=stt[:, 6:12], in_=z[:, 512:1024]), reads=[r_z, r_stt], writes=[r_stt])
            S.dve(lambda e: e.bn_aggr(out=mv[:, 0:2], in_=stt[:, 0:12]), reads=[r_stt], writes=[r_mv])
            S.act(lambda e: e.activation(out=mv[:, 2:3], in_=mv[:, 1:2], func=AF.Sqrt, bias=epsc[:, 0:1], scale=1.0), reads=[r_mv, r_eps], writes=[r_mv])
            S.dve(lambda e: e.reciprocal(out=mv[:, 3:4], in_=mv[:, 2:3]), reads=[r_mv], writes=[r_mv])
            S.dve(lambda e: e.tensor_scalar(out=z, in0=z, scalar1=mv[:, 0:1], scalar2=mv[:, 3:4], op0=ALU.subtract, op1=ALU.mult),
                  reads=[r_z, r_mv], writes=[r_z])
            S.pool(lambda e: e.tensor_tensor(out=z, in0=z, in1=gB[:], op=ALU.mult), reads=[r_z, r_gb], writes=[r_z])
            S.pool(lambda e: e.tensor_tensor(out=z, in0=z, in1=bB[:], op=ALU.add), reads=[r_z, r_gb], writes=[r_z])

        cbt = SB(top, "cbt", [128, 12], F32)
        r_cbt = Res()
        S.dma("sp", cbt[:, 0:4], bcast_rows(tab, 15, 4), writes=[r_cbt])
        S.dma("sp", cbt[:, 4:8], bcast_rows(tab, 31, 4), writes=[r_cbt])
        S.dve(lambda e: e.tensor_tensor(out=cbt[:, 8:12], in0=cbt[:, 0:4], in1=cbt[:, 4:8], op=ALU.subtract), reads=[r_cbt], writes=[r_cbt])
        epsc = SB(top, "epsc", [128, 1], F32)
        r_eps = Res()
        S.pool(lambda e: e.memset(epsc[:], LN_EPS), writes=[r_eps])

        def ffn_phase(tag, src, r_src, nrows, wgu_d, wd_d, ln_row, dst, r_dst):
            with contextlib.ExitStack() as st:
                wgu, r_wgu = load_weight(st, "wgu" + tag, wgu_d, D, 2 * DFF)
                wd, r_wd = load_weight(st, "wd" + tag, wd_d, DFF, D)
                gB = SB(st, "gB" + tag, [128, D], F32)
                bB = SB(st, "bB" + tag, [128, D], F32)
                r_gb = Res()
                S.dma("sp", gB[:], bcast_rows(lnv, ln_row, D), writes=[r_gb])
                S.dma("sp", bB[:], bcast_rows(lnv, ln_row + 1, D), writes=[r_gb])
                xin_t = SB(st, "xin" + tag, [128, 2, D], F32)
                xbf_t = SB(st, "xbf" + tag, [128, 2, D], BF16)
                xin_ring = Ring([xin_t[:, k, :] for k in range(2)])
                xbf_ring = Ring([xbf_t[:, k, :] for k in range(2)])
                xT = SB(st, "xT" + tag, [128, 8, 512], BF16)
                r_xT = Res()
                actT = SB(st, "actT" + tag, [128, NFC, 512], BF16)
                r_act = [Res() for _ in range(NFC)]
                sa_t = SB(st, "sa" + tag, [128, 2, 512], F32)
                sa_ring = Ring([sa_t[:, k, :] for k in range(2)])
                z_t = SB(st, "z" + tag, [128, 3, D], F32)
                z_ring = Ring([z_t[:, k, :] for k in range(3)])
                stt_t = SB(st, "stt" + tag, [128, 2, 12], F32)
                mv_t = SB(st, "mv" + tag, [128, 2, 4], F32)
                st_ring = Ring([(stt_t[:, k, :], mv_t[:, k, :]) for k in range(2)])
                st_res2 = [(Res(), Res()) for _ in range(2)]
                ntile = nrows // 512
                load_T(lambda s: src[s * 128:(s + 1) * 128, :], 4, xin_ring, xbf_ring, xT, r_xT, 6)
                for t in range(ntile):
                    r0 = t * 512
                    for f in range(NFC):
                        ba, bg = (0, 1) if f % 2 == 0 else (2, 3)
                        for dd in range(8):
                            S.pe(lambda e, f=f, dd=dd, ba=ba: e.matmul(ps[:, ba, :], wgu[:, dd, f * 128:(f + 1) * 128], xT[:, dd, :], start=(dd == 0), stop=(dd == 7)),
                                 reads=[r_wgu, r_xT], writes=[ps_res[ba]])
                        for dd in range(8):
                            S.pe(lambda e, f=f, dd=dd, bg=bg: e.matmul(ps[:, bg, :], wgu[:, dd, DFF + f * 128:DFF + (f + 1) * 128], xT[:, dd, :], start=(dd == 0), stop=(dd == 7)),
                                 reads=[r_wgu, r_xT], writes=[ps_res[bg]])
                        sa, r_sa = sa_ring.next()
                        S.act(lambda e, sa=sa, ba=ba: e.activation(out=sa, in_=ps[:, ba, :], func=AF.Silu), reads=[ps_res[ba]], writes=[r_sa])
                        S.dve(lambda e, sa=sa, bg=bg, f=f: e.tensor_tensor(out=actT[:, f, :], in0=sa, in1=ps[:, bg, :], op=ALU.mult),
                              reads=[r_sa, ps_res[bg]], writes=[r_act[f]])
                    if t + 1 < ntile:
                        load_T(lambda s, r1=r0 + 512: src[r1 + s * 128:r1 + (s + 1) * 128, :], 4, xin_ring, xbf_ring, xT, r_xT, 6)
                    for s in range(4):
                        z, r_z = z_ring.next()
                        rows = slice(r0 + s * 128, r0 + (s + 1) * 128)
                        S.dma("sp", z, src[rows, :], reads=[r_src], writes=[r_z])
                        S.act(lambda e, z=z: e.mul(out=z, in_=z, mul=ALPHA), reads=[r_z], writes=[r_z])
                        for half in range(2):
                            b = 4 + half
                            for f in range(NFC):
                                S.pe(lambda e, f=f, s=s, half=half, b=b: e.matmul(ps[:, b, :], actT[:, f, s * 128:(s + 1) * 128], wd[:, f, half * 512:(half + 1) * 512], start=(f == 0), stop=(f == NFC - 1)),
                                     reads=[r_act[f], r_wd], writes=[ps_res[b]])
                        S.dve(lambda e, z=z: e.scalar_tensor_tensor(out=z, in0=ps[:, 4:6, :].rearrange("p a b -> p (a b)"), scalar=0.5, in1=z, op0=ALU.mult, op1=ALU.add),
                              reads=[ps_res[4], ps_res[5], r_z], writes=[r_z])
                        (stt, mv), _ = st_ring.next()
                        k = (st_ring.i - 1) % 2
                        layer_norm_store((stt, st_res2[k][0], mv, st_res2[k][1]), z, r_z, gB, bB, r_gb, None)
                        S.dma("pool", dst[rows, :], z, reads=[r_z], writes=[r_dst])

        r_xr = Res()
        ffn_phase("f1", xr, r_xr, R, w_ffn[0][0], w_ffn[0][1], 0, x1, r_x1)

        S.full_barrier()
        with contextlib.ExitStack() as st:
            win, r_win = load_weight(st, "win", w_in[:, 0:COL_G], D, COL_G)
            xin_t = SB(st, "xin2", [128, 2, D], F32)
            xbf_t = SB(st, "xbf2", [128, 2, D], BF16)
            xin_ring = Ring([xin_t[:, k, :] for k in range(2)])
            xbf_ring = Ring([xbf_t[:, k, :] for k in range(2)])
            xT2_t = SB(st, "xT2", [128, 2, 8, 512], BF16)
            xT2_ring = Ring([xT2_t[:, k, :, :] for k in range(2)])
            sg_t = SB(st, "sg2", [128, 2, 512], F32)
            sg_ring = Ring([sg_t[:, k, :] for k in range(2)])
            ho_t = SB(st, "ho2", [128, 3, 512], F32)
            ho_ring = Ring([ho_t[:, k, :] for k in range(3)])
            qk_t = SB(st, "qk2", [128, 3, 512], BF16)
            qk_ring = Ring([qk_t[:, k, :] for k in range(3)])
            v_t = SB(st, "v2", [128, 2, 4, 129], BF16)
            v_ring = Ring([v_t[:, k, :, :] for k in range(2)])
            zt = SB(st, "zero2", [128, 16], F32)
            r_zt = Res()
            S.pool(lambda e: e.memset(zt[:], 0.0), writes=[r_zt])
            S.pool(lambda e: e.memset(v_t[:], 1.0), writes=v_ring.res)
            for c in range(4):
                S.dma("sp", hT[c * 128:(c + 1) * 128, 0:16], zt[:], reads=[r_zt], writes=[r_hT])
                S.dma("sp", hT[c * 128:(c + 1) * 128, 16 + SP:32 + SP], zt[:], reads=[r_zt], writes=[r_hT])
            xT_next = xT2_ring.next()
            load_T(lambda s: x1[s * 128:(s + 1) * 128, :], 4, xin_ring, xbf_ring, xT_next[0], xT_next[1], 6)
            for t in range(R // 512):
                r0 = t * 512
                xT, r_xT = xT_next
                if r0 + 512 < R:
                    xT_next = xT2_ring.next()
                    load_T(lambda s, r1=r0 + 512: x1[r1 + s * 128:r1 + (s + 1) * 128, :], 4, xin_ring, xbf_ring, xT_next[0], xT_next[1], 6)
                samp = r0 >= SP
                rs = r0 - SP
                need_q = (not samp) or rs < NQ
                if not samp:
                    need_u, hcol, ucols, mask = True, 16 + r0, (0, 512), None
                elif rs < NQ:
                    need_u, hcol, ucols, mask = True, HB + 128 + rs, (0, 512), None
                elif rs == NQ:
                    need_u, hcol, ucols, mask = True, HB + 128 + NQ, (0, 128), C_MK + 1
                elif rs == SS - 512:
                    need_u, hcol, ucols, mask = True, HB - 384, (384, 512), C_MK + 0
                else:
                    need_u = False
                if False:
                    dxT = nc.dram_tensor("dbg_xT", [128, 8 * 512], BF16, kind="ExternalOutput").ap()
                    dwin = nc.dram_tensor("dbg_win", [128, 2560], BF16, kind="ExternalOutput").ap()
                    dxin = nc.dram_tensor("dbg_xin", [128, 1024], F32, kind="ExternalOutput").ap()
                    S.dma("sp", dxT[:, :], xT[:].rearrange("p c t -> p (c t)"), reads=[r_xT])
                    S.dma("sp", dwin[:, :], win[:, 0, :], reads=[r_win])
                    S.dma("sp", dxin[:, :], xin_t[:, 1, :], reads=[xin_ring.res[1]])
                nb = [0]

                def bank():
                    b = nb[0] % 4
                    nb[0] += 1
                    return b

                def proj_fm(col0):
                    b = bank()
                    for dd in range(8):
                        S.pe(lambda e, dd=dd, b=b, col0=col0, xT=xT: e.matmul(ps[:, b, :], win[:, dd, col0:col0 + 128], xT[:, dd, :], start=(dd == 0), stop=(dd == 7)),
                             reads=[r_win, r_xT], writes=[ps_res[b]])
                    return b

                if need_u:
                    for j in range(4):
                        ba = proj_fm(j * 128)
                        bg = proj_fm(512 + j * 128)
                        sg, r_sg = sg_ring.next()
                        ho, r_ho = ho_ring.next()
                        S.act(lambda e, sg=sg, bg=bg: e.activation(out=sg, in_=ps[:, bg, :], func=AF.Sigmoid), reads=[ps_res[bg]], writes=[r_sg])
                        S.dve(lambda e, sg=sg, ho=ho, ba=ba: e.tensor_tensor(out=ho, in0=sg, in1=ps[:, ba, :], op=ALU.mult), reads=[r_sg, ps_res[ba]], writes=[r_ho])
                        if mask is not None:
                            S.dve(lambda e, ho=ho, mask=mask: e.tensor_scalar_mul(out=ho, in0=ho, scalar1=colv(mask)), reads=[r_ho, r_colt], writes=[r_ho])
                        u0, u1 = ucols
                        S.dma("pool", hT[j * 128:(j + 1) * 128, hcol + u0:hcol + u1], ho[:, u0:u1], reads=[r_ho], writes=[r_hT])
                for (need, col0, dstT, r_d, cbase) in ((need_q, COL_Q, QT, r_QT, (r0 if not samp else SP + rs)), (True, COL_K, KT, r_KT, r0)):
                    if not need:
                        continue
                    for h in range(4):
                        b = proj_fm(col0 + h * 128)
                        qk, r_qk = qk_ring.next()
                        S.act(lambda e, qk=qk, b=b: e.copy(out=qk, in_=ps[:, b, :]), reads=[ps_res[b]], writes=[r_qk])
                        S.dma("pool", dstT[h, :, cbase:cbase + 512], qk, reads=[r_qk], writes=[r_d])
                for s in range(4):
                    b = bank()
                    for dd in range(8):
                        S.pe(lambda e, dd=dd, b=b, s=s, xT=xT: e.matmul(ps[:, b, :], xT[:, dd, s * 128:(s + 1) * 128], win[:, dd, COL_V:COL_V + 512], start=(dd == 0), stop=(dd == 7)),
                             reads=[r_win, r_xT], writes=[ps_res[b]])
                    vt, r_vt = v_ring.next()
                    S.dve(lambda e, vt=vt, b=b: e.tensor_copy(out=vt[:, :, 0:128], in_=ps[:, b, :].rearrange("p (h e) -> p h e", h=4)), reads=[ps_res[b]], writes=[r_vt])
                    S.dma("pool", Vs[r0 + s * 128:r0 + (s + 1) * 128, :], vt.rearrange("p h e -> p (h e)"), reads=[r_vt], writes=[r_Vs])

        S.full_barrier()
        with contextlib.ExitStack() as st:
            hin_t = SB(st, "hin", [128, 2, 4, 544], F32)
            hin_ring = Ring([hin_t[:, k, :, :] for k in range(2)])
            acc = SB(st, "cacc", [128, 4, 512], F32)
            r_acc = [Res() for _ in range(4)]
            sq = SB(st, "csq", [128, 4, 512], F32)
            r_sq = [Res() for _ in range(4)]
            onesm = SB(st, "onesm", [128, 128], F32)
            r_ones = Res()
            S.pool(lambda e: e.memset(onesm[:], 1.0 / 512.0), writes=[r_ones])
            mean = SB(st, "cmean", [128, 512], F32)
            rstd = SB(st, "crstd", [128, 512], F32)
            r_mean, r_rstd = Res(), Res()
            co_t = SB(st, "cout", [128, 3, 512], BF16)
            co_ring = Ring([co_t[:, k, :] for k in range(3)])
            tiles = [(16 + t * 512, t * 512) for t in range(SP // 512)] + [(HB + 128 + t * 512, SP + t * 512) for t in range(NQ // 512)]
            for (hc, mrow) in tiles:
                hin, r_hin = hin_ring.next()
                for c in range(4):
                    S.dma("sp", hin[:, c, 0:542], hT[c * 128:(c + 1) * 128, hc - 15:hc + 527], reads=[r_hT], writes=[r_hin])
                for j in range(31):
                    for c in range(4):
                        if j == 0:
                            S.dve(lambda e, c=c, hin=hin: e.tensor_scalar(out=acc[:, c, :], in0=hin[:, c, 0:512], scalar1=colv(C_CW + c * 31), scalar2=colv(C_CB + c), op0=ALU.mult, op1=ALU.add),
                                  reads=[r_hin, r_colt], writes=[r_acc[c]])
                        else:
                            S.dve(lambda e, c=c, j=j, hin=hin: e.scalar_tensor_tensor(out=acc[:, c, :], in0=hin[:, c, j:j + 512], scalar=colv(C_CW + c * 31 + j), in1=acc[:, c, :], op0=ALU.mult, op1=ALU.add),
                                  reads=[r_hin, r_colt, r_acc[c]], writes=[r_acc[c]])
                for c in range(4):
                    S.pool(lambda e, c=c: e.tensor_tensor(out=sq[:, c, :], in0=acc[:, c, :], in1=acc[:, c, :], op=ALU.mult), reads=[r_acc[c]], writes=[r_sq[c]])
                for c in range(4):
                    S.pe(lambda e, c=c: e.matmul(ps[:, 0, :], onesm[:], acc[:, c, :], start=(c == 0), stop=(c == 3)), reads=[r_ones, r_acc[c]], writes=[ps_res[0]])
                for c in range(4):
                    S.pe(lambda e, c=c: e.matmul(ps[:, 1, :], onesm[:], sq[:, c, :], start=(c == 0), stop=(c == 3)), reads=[r_ones, r_sq[c]], writes=[ps_res[1]])
                S.act(lambda e: e.copy(out=mean[:], in_=ps[:, 0, :]), reads=[ps_res[0]], writes=[r_mean])
                S.pool(lambda e: e.tensor_tensor(out=rstd[:], in0=mean[:], in1=mean[:], op=ALU.mult), reads=[r_mean], writes=[r_rstd])
                S.dve(lambda e: e.tensor_tensor(out=rstd[:], in0=ps[:, 1, :], in1=rstd[:], op=ALU.subtract), reads=[ps_res[1], r_rstd], writes=[r_rstd])
                S.act(lambda e: e.activation(out=rstd[:], in_=rstd[:], func=AF.Sqrt, bias=epsc[:, 0:1], scale=1.0), reads=[r_rstd, r_eps], writes=[r_rstd])
                S.dve(lambda e: e.reciprocal(out=rstd[:], in_=rstd[:]), reads=[r_rstd], writes=[r_rstd])
                for c in range(4):
                    S.pool(lambda e, c=c: e.tensor_tensor(out=acc[:, c, :], in0=acc[:, c, :], in1=mean[:], op=ALU.subtract), reads=[r_acc[c], r_mean], writes=[r_acc[c]])
                    S.dve(lambda e, c=c: e.tensor_tensor(out=acc[:, c, :], in0=acc[:, c, :], in1=rstd[:], op=ALU.mult), reads=[r_acc[c], r_rstd], writes=[r_acc[c]])
                    co, r_co = co_ring.next()
                    S.act(lambda e, c=c, co=co: e.activation(out=co, in_=acc[:, c, :], func=AF.Silu, bias=colv(C_CBE + c), scale=colv(C_CG + c)), reads=[r_acc[c], r_colt], writes=[r_co])
                    S.dma("pool", cactT[c * 128:(c + 1) * 128, mrow:mrow + 512], co, reads=[r_co], writes=[r_cact])

        S.full_barrier()
        with contextlib.ExitStack() as st:
            KMAX = max(SP, SS)
            kt_t = SB(st, "kt", [128, KMAX], BF16)
            vh_t = SB(st, "vh", [128, KMAX // 128, 129], BF16)
            r_kt, r_vh = Res(), Res()
            bt = SB(st, "bt", [128, 8, 512], F32)
            r_bt = Res()
            farb = SB(st, "farb", [128, NBS], F32)
            r_farb = Res()
            wm = SB(st, "wm", [128, NBS], F32)
            r_wm = Res()
            S.dma("sp", wm[:], wrapm[:, :], writes=[r_wm])
            gsub = SB(st, "gsub", [128, 128], F32)
            r_gsub = Res()
            S.dma("sp", gsub[:], bcast_rows(sublg, 0, 128), writes=[r_gsub])
            S.pool(lambda e: e.tensor_scalar_mul(out=gsub[:], in0=gsub[:], scalar1=1.0 - LAMBDA_INIT), reads=[r_gsub], writes=[r_gsub])
            qt_t = SB(st, "qt", [128, 2, 512], BF16)
            qt_ring = Ring([qt_t[:, k, :] for k in range(2)])
            pt_t = SB(st, "pt", [128, 3, 512], BF16)
            pt_ring = Ring([pt_t[:, k, :] for k in range(3)])
            tmp_t = SB(st, "stmp", [128, 2, 512], F32)
            tmp_ring = Ring([tmp_t[:, k, :] for k in range(2)])
            om = SB(st, "om", [128, 2, 4, 129], F32)
            r_om = [Res(), Res()]
            sm = SB(st, "sm", [128, 16], F32)
            r_sm = Res()
            av = SB(st, "av", [128, 4, 128], F32)
            r_av = Res()
            jk = SB(st, "jk", [128, 128], F32)
            r_jk = Res()
            ab_t = SB(st, "ab", [128, 2, 4, 128], BF16)
            ab_ring = Ring([ab_t[:, k, :, :] for k in range(2)])
            at_t = SB(st, "at", [128, 2, 512], BF16)
            at_ring = Ring([at_t[:, k, :] for k in range(2)])
            SB_BANKS = (0, 1, 2)
            units = [(0, SP, SP, 0, False, h) for h in range(4)] + [(SP, SS, NQ, SP, True, h) for h in range(4)]
            for (krow0, S_k, nq, qcol0, samp, h) in units:
                NB = S_k // 128
                nqt = nq // 512
                S.dma("sp", kt_t[:, 0:S_k], KT[h, :, krow0:krow0 + S_k], reads=[r_KT], writes=[r_kt])
                nvc = max(1, NB // 16)
                for vc in range(nvc):
                    b0, b1 = vc * NB // nvc, (vc + 1) * NB // nvc
                    S.dma("sp", vh_t[:, b0:b1, :],
                          dram_ap(Vs, (krow0 + b0 * 128) * 516 + h * 129, [[516, 128], [128 * 516, b1 - b0], [1, 129]]),
                          reads=[r_Vs], writes=[r_vh])
                for kk in range(6):
                    S.dma("sp", bt[:, kk, :], dram_ap(brep, h * 128 * LB + 639 - 128 * (kk - 1), [[LB - 1, 128], [1, 512]]), reads=[r_brep], writes=[r_bt])
                cneg, cpos, cdif = cbt[:, h:h + 1], cbt[:, 4 + h:5 + h], cbt[:, 8 + h:9 + h]
                if samp:
                    S.dve(lambda e, cdif=cdif, cpos=cpos: e.tensor_scalar(out=farb[:], in0=wm[:], scalar1=cdif, scalar2=cpos, op0=ALU.mult, op1=ALU.add), reads=[r_wm, r_cbt], writes=[r_farb])
                    for (dst, srck, cc, mcol) in ((6, 0, cpos, C_MK + 2), (7, 5, cneg, C_MK + 3)):
                        S.dve(lambda e, dst=dst, srck=srck, cc=cc, mcol=mcol: e.tensor_scalar(out=bt[:, dst, :], in0=bt[:, srck, :], scalar1=cc, scalar2=colv(mcol), op0=ALU.subtract, op1=ALU.mult),
                              reads=[r_bt, r_cbt, r_colt], writes=[r_bt])
                        S.dve(lambda e, dst=dst, cc=cc: e.tensor_scalar_add(out=bt[:, dst, :], in0=bt[:, dst, :], scalar1=cc), reads=[r_bt, r_cbt], writes=[r_bt])
                for i in range(nqt):
                    qt, r_qt = qt_ring.next()
                    S.dma("sp", qt, QT[h, :, qcol0 + i * 512:qcol0 + (i + 1) * 512], reads=[r_QT], writes=[r_qt])
                    def kind(j):
                        rel = j - 4 * i
                        if samp and i == 0 and j == NB - 1:
                            return ("near", 6)
                        if samp and i == nqt - 1 and rel == 4:
                            return ("near", 7)
                        if -1 <= rel <= 4:
                            return ("near", rel + 1)
                        if rel < -1:
                            return ("far", cneg, r_cbt)
                        if samp:
                            return ("far", farb[:, j:j + 1], r_farb)
                        return ("far", cpos, r_cbt)
                    for m in range(2):
                        rows = slice(m * 64, (m + 1) * 64)
                        pend = []

                        def qk(j):
                            b = SB_BANKS[j % 3]
                            S.pe(lambda e, j=j, b=b, rows=rows, qt=qt: e.matmul(ps[:, b, :], kt_t[rows, j * 128:(j + 1) * 128], qt[rows, :], start=True, stop=True),
                                 reads=[r_kt, r_qt], writes=[ps_res[b]])
                            kd = kind(j)
                            pt, r_pt = pt_ring.next()
                            if kd[0] == "near":
                                tmp, r_tmp = tmp_ring.next()
                                S.dve(lambda e, b=b, tmp=tmp, kk=kd[1]: e.scalar_tensor_tensor(out=tmp, in0=ps[:, b, :], scalar=0.125, in1=bt[:, kk, :], op0=ALU.mult, op1=ALU.add),
                                      reads=[ps_res[b], r_bt], writes=[r_tmp])
                                S.act(lambda e, pt=pt, tmp=tmp: e.activation(out=pt, in_=tmp, func=AF.Exp), reads=[r_tmp], writes=[r_pt])
                            else:
                                S.act(lambda e, pt=pt, b=b, bc=kd[1]: e.activation(out=pt, in_=ps[:, b, :], func=AF.Exp, bias=bc, scale=0.125), reads=[ps_res[b], kd[2]], writes=[r_pt])
                            pend.append((j, pt, r_pt))

                        def avmm():
                            j, pt, r_pt = pend.pop(0)
                            for s in range(4):
                                b = 3 + s // 2
                                o = (s % 2) * 129
                                S.pe(lambda e, j=j, s=s, b=b, o=o, pt=pt, st_=(j == 0 and s % 2 == 0), sp_=(j == NB - 1): e.matmul(ps[:, b, o:o + 129], pt[:, s * 128:(s + 1) * 128], vh_t[:, j, :], start=st_, stop=sp_),
                                     reads=[r_pt, r_vh], writes=[ps_res[b]])

                        for j in range(NB):
                            qk(j)
                            if j >= 2:
                                avmm()
                        while pend:
                            avmm()
                        S.dve(lambda e, m=m: e.tensor_copy(out=om[:, m, 0:2, :], in_=ps[:, 3, 0:258].rearrange("p (a b) -> p a b", a=2)), reads=[ps_res[3]], writes=[r_om[m]])
                        S.dve(lambda e, m=m: e.tensor_copy(out=om[:, m, 2:4, :], in_=ps[:, 4, 0:258].rearrange("p (a b) -> p a b", a=2)), reads=[ps_res[4], r_om[m]], writes=[r_om[m]])
                    S.dve(lambda e: e.reciprocal(out=sm[:, 0:4], in_=om[:, 0, :, 128]), reads=[r_om[0]], writes=[r_sm])
                    S.dve(lambda e: e.reciprocal(out=sm[:, 4:8], in_=om[:, 1, :, 128]), reads=[r_om[1], r_sm], writes=[r_sm])
                    S.dve(lambda e: e.tensor_scalar_mul(out=sm[:, 4:8], in0=sm[:, 4:8], scalar1=lam[:, 5:6]), reads=[r_sm, r_lam], writes=[r_sm])
                    ab, r_ab = ab_ring.next()
                    for s in range(4):
                        S.dve(lambda e, s=s: e.tensor_scalar_mul(out=av[:, s, :], in0=om[:, 0, s, 0:128], scalar1=sm[:, s:s + 1]), reads=[r_om[0], r_sm], writes=[r_av])
                        S.dve(lambda e, s=s: e.scalar_tensor_tensor(out=av[:, s, :], in0=om[:, 1, s, 0:128], scalar=sm[:, 4 + s:5 + s], in1=av[:, s, :], op0=ALU.mult, op1=ALU.add),
                              reads=[r_om[1], r_sm, r_av], writes=[r_av])
                        S.pool(lambda e, s=s: e.tensor_tensor(out=jk[:], in0=av[:, s, :], in1=av[:, s, :], op=ALU.mult), reads=[r_av], writes=[r_jk])
                        S.dve(lambda e, s=s: e.reduce_sum(out=sm[:, 8 + s:9 + s], in_=jk[:], axis=AX.X), reads=[r_jk, r_sm], writes=[r_sm])
                    S.act(lambda e: e.activation(out=sm[:, 12:16], in_=sm[:, 8:12], func=AF.Sqrt, bias=epsc[:, 0:1], scale=1.0 / 128.0), reads=[r_sm, r_eps], writes=[r_sm])
                    S.dve(lambda e: e.reciprocal(out=sm[:, 12:16], in_=sm[:, 12:16]), reads=[r_sm], writes=[r_sm])
                    pst = pstT[:, :]
                    for s in range(4):
                        S.dve(lambda e, s=s, ab=ab: e.scalar_tensor_tensor(out=ab[:, s, :], in0=av[:, s, :], scalar=sm[:, 12 + s:13 + s], in1=gsub[:], op0=ALU.mult, op1=ALU.mult),
                              reads=[r_av, r_sm, r_gsub], writes=[r_ab])
                    for s in range(4):
                        S.pe(lambda e, s=s, ab=ab, pst=pst: e.transpose(pst[:, s * 128:(s + 1) * 128], ab[:, s, :], idb[:]), reads=[r_ab, r_idb], writes=[r_pstT])
                    at, r_at = at_ring.next()
                    S.dve(lambda e, at=at, pst=pst: e.tensor_copy(out=at, in_=pst[:, 0:512]), reads=[r_pstT], writes=[r_at])
                    S.dma("pool", attnT[h * 128:(h + 1) * 128, qcol0 + i * 512:qcol0 + (i + 1) * 512], at, reads=[r_at], writes=[r_attn])

        S.full_barrier()
        with contextlib.ExitStack() as st:
            wg, r_wg = load_weight(st, "wg", w_in[:, COL_G:PROJ], D, 2 * D)
            wco, r_wco = load_weight(st, "wco", w_co, 512, D)
            wao, r_wao = load_weight(st, "wao", w_ao, 512, D)
            wo, r_wo = load_weight(st, "wo", w_o, D, D)
            gB = SB(st, "gB5", [128, D], F32)
            bB = SB(st, "bB5", [128, D], F32)
            r_gb = Res()
            S.dma("sp", gB[:], bcast_rows(lnv, 2, D), writes=[r_gb])
            S.dma("sp", bB[:], bcast_rows(lnv, 3, D), writes=[r_gb])
            xin_t = SB(st, "xin5", [128, 2, D], F32)
            xbf_t = SB(st, "xbf5", [128, 2, D], BF16)
            xin_ring = Ring([xin_t[:, k, :] for k in range(2)])
            xbf_ring = Ring([xbf_t[:, k, :] for k in range(2)])
            xT5v = SB(st, "xT5", [128, 8, 512], BF16)
            r_xT = Res()
            ca_t = SB(st, "ca5", [128, 2, 4, 512], BF16)
            ca_ring = Ring([ca_t[:, k, :, :] for k in range(2)])
            aa_t = SB(st, "aa5", [128, 2, 4, 512], BF16)
            aa_ring = Ring([aa_t[:, k, :, :] for k in range(2)])
            g_t = SB(st, "g5", [128, 4, 512], F32)
            g_ring = Ring([g_t[:, k, :] for k in range(4)])
            mT = SB(st, "mT5", [128, 8, 512], BF16)
            r_mT = [Res() for _ in range(8)]
            z_t = SB(st, "z5", [128, 3, D], F32)
            z_ring = Ring([z_t[:, k, :] for k in range(3)])
            stt_t = SB(st, "stt5", [128, 2, 12], F32)
            mv_t = SB(st, "mv5", [128, 2, 4], F32)
            st_res2 = [(Res(), Res()) for _ in range(2)]
            nst = 0
            load_T(lambda s: x1[s * 128:(s + 1) * 128, :], 4, xin_ring, xbf_ring, xT5v, r_xT, 6)
            for t in range(NM // 512):
                r0 = t * 512
                xrow = r0
                ca, r_ca = ca_ring.next()
                aa, r_aa = aa_ring.next()
                for c in range(4):
                    S.dma("sp", ca[:, c, :], cactT[c * 128:(c + 1) * 128, r0:r0 + 512], reads=[r_cact], writes=[r_ca])
                    S.dma("sp", aa[:, c, :], attnT[c * 128:(c + 1) * 128, r0:r0 + 512], reads=[r_attn], writes=[r_aa])
                for n in range(8):
                    for (b, col0) in ((0, n * 128), (1, D + n * 128)):
                        for dd in range(8):
                            S.pe(lambda e, b=b, dd=dd, col0=col0: e.matmul(ps[:, b, :], wg[:, dd, col0:col0 + 128], xT5v[:, dd, :], start=(dd == 0), stop=(dd == 7)),
                                 reads=[r_wg, r_xT], writes=[ps_res[b]])
                    for c in range(4):
                        S.pe(lambda e, c=c, n=n, ca=ca: e.matmul(ps[:, 2, :], wco[:, c, n * 128:(n + 1) * 128], ca[:, c, :], start=(c == 0), stop=(c == 3)),
                             reads=[r_wco, r_ca], writes=[ps_res[2]])
                    for c in range(4):
                        S.pe(lambda e, c=c, n=n, aa=aa: e.matmul(ps[:, 3, :], wao[:, c, n * 128:(n + 1) * 128], aa[:, c, :], start=(c == 0), stop=(c == 3)),
                             reads=[r_wao, r_aa], writes=[ps_res[3]])
                    gc, r_gc = g_ring.next()
                    ga, r_ga = g_ring.next()
                    S.act(lambda e, gc=gc, n=n: e.activation(out=gc, in_=ps[:, 0, :], func=AF.Sigmoid, bias=colv(C_BG + n), scale=1.0), reads=[ps_res[0], r_colt], writes=[r_gc])
                    S.act(lambda e, ga=ga, n=n: e.activation(out=ga, in_=ps[:, 1, :], func=AF.Sigmoid, bias=colv(C_BG + 8 + n), scale=1.0), reads=[ps_res[1], r_colt], writes=[r_ga])
                    S.dve(lambda e, gc=gc: e.tensor_tensor(out=gc, in0=gc, in1=ps[:, 2, :], op=ALU.mult), reads=[r_gc, ps_res[2]], writes=[r_gc])
                    S.dve(lambda e, ga=ga: e.tensor_tensor(out=ga, in0=ga, in1=ps[:, 3, :], op=ALU.mult), reads=[r_ga, ps_res[3]], writes=[r_ga])
                    S.pool(lambda e, gc=gc, ga=ga, n=n: e.tensor_tensor(out=mT[:, n, :], in0=gc, in1=ga, op=ALU.add), reads=[r_gc, r_ga], writes=[r_mT[n]])
                if r0 + 512 < NM:
                    load_T(lambda s, r1=r0 + 512: x1[r1 + s * 128:r1 + (s + 1) * 128, :], 4, xin_ring, xbf_ring, xT5v, r_xT, 6)
                for s in range(4):
                    z, r_z = z_ring.next()
                    S.dma("sp", z, x1[xrow + s * 128:xrow + (s + 1) * 128, :], reads=[r_x1], writes=[r_z])
                    S.act(lambda e, z=z: e.mul(out=z, in_=z, mul=ALPHA), reads=[r_z], writes=[r_z])
                    for half in range(2):
                        b = 4 + half
                        for n in range(8):
                            S.pe(lambda e, n=n, s=s, half=half, b=b: e.matmul(ps[:, b, :], mT[:, n, s * 128:(s + 1) * 128], wo[:, n, half * 512:(half + 1) * 512], start=(n == 0), stop=(n == 7)),
                                 reads=[r_mT[n], r_wo], writes=[ps_res[b]])
                    S.dve(lambda e, z=z: e.tensor_tensor(out=z, in0=z, in1=ps[:, 4:6, :].rearrange("p a b -> p (a b)"), op=ALU.add),
                          reads=[ps_res[4], ps_res[5], r_z], writes=[r_z])
                    k = nst % 2
                    nst += 1
                    layer_norm_store((stt_t[:, k, :], st_res2[k][0], mv_t[:, k, :], st_res2[k][1]), z, r_z, gB, bB, r_gb, None)
                    S.dma("pool", x2[r0 + s * 128:r0 + (s + 1) * 128, :], z, reads=[r_z], writes=[r_x2])

        S.full_barrier()
        r_yo = Res()
        ffn_phase("f2", x2, r_x2, NM, w_ffn[1][0], w_ffn[1][1], 4, yo, r_yo)
        stats = S.emit()
    return nc, stats


def _t5_bucket_np(rel):
    nb = 16
    ret = np.where(rel > 0, nb, 0)
    n = np.abs(rel)
    max_exact = 8
    nf = np.maximum(n, 1).astype(np.float32)
    large = max_exact + (np.log(nf / np.float32(max_exact)) / np.float32(math.log(128 / max_exact)) * np.float32(nb - max_exact)).astype(np.int32)
    large = np.minimum(large, nb - 1)
    return ret + np.where(n < max_exact, n, large)


def _onehot():
    i = np.arange(LB)
    b = _t5_bucket_np(639 - i)
    oh = np.zeros((32, LB), np.float32)
    oh[b, i] = 1.0
    return oh


def make_in_maps(inputs, SP, SS, NQ, ncores):
    f = lambda a: np.ascontiguousarray(np.asarray(a, dtype=np.float32))
    xp = f(inputs["x_prompt"])
    xs = f(inputs["x_sample"])[0]
    common = {
        "ffn1_gu": f(inputs["ffn1_w_gu"][0]), "ffn1_d": f(inputs["ffn1_w_down"][0]),
        "ffn2_gu": f(inputs["ffn2_w_gu"][0]), "ffn2_d": f(inputs["ffn2_w_down"][0]),
        "w_in": f(inputs["w_in"][0]), "w_co": f(inputs["w_conv_out"][0]),
        "w_ao": f(inputs["w_attn_out"][0]), "w_o": f(inputs["w_o"][0]),
        "lnv": f(np.stack([inputs["ln1_g"][0], inputs["ln1_b"][0], inputs["ln2_g"][0], inputs["ln2_b"][0], inputs["ln3_g"][0], inputs["ln3_b"][0]])),
        "sublg": f(inputs["subln_g"]).reshape(1, 128),
        "lamv": f(np.concatenate([inputs["lambda_q1"][0], inputs["lambda_k1"][0], inputs["lambda_q2"][0], inputs["lambda_k2"][0]])).reshape(1, 256),
        "tab": f(inputs["rel_bias_table"]),
        "oh": _onehot(),
        "ident": np.eye(128, dtype=np.float32),
    }
    NCOL = 16 + 124 + 12 + 4
    colbase = np.zeros((128, NCOL), np.float32)
    colbase[:, 0:16] = f(inputs["b_gate"][0]).reshape(16, 128).T
    cw = f(inputs["conv_w_dw"][0])[:, 0, :]
    for c in range(4):
        colbase[:, 16 + c * 31:16 + (c + 1) * 31] = cw[:, c * 128:(c + 1) * 128].T
    colbase[:, 140:144] = f(inputs["conv_b_dw"][0]).reshape(4, 128).T
    colbase[:, 144:148] = f(inputs["conv_ln_g"][0]).reshape(4, 128).T
    colbase[:, 148:152] = f(inputs["conv_ln_b"][0]).reshape(4, 128).T
    maps = []
    for r in range(ncores):
        col = colbase.copy()
        col[:, 152] = 1.0 if r > 0 else 0.0
        col[:, 153] = 1.0 if r < ncores - 1 else 0.0
        col[:, 154] = 1.0 if r > 0 else 0.0
        col[:, 155] = 1.0 if r < ncores - 1 else 0.0
        rot = np.roll(xs, -NQ * r, axis=0)
        wrap = ((np.arange(SS // 128) * 128 + NQ * r) >= SS).astype(np.float32)
        m = dict(common)
        m["xr"] = np.ascontiguousarray(np.concatenate([xp[r], rot], axis=0))
        m["colp"] = col
        m["wrapm"] = np.ascontiguousarray(np.broadcast_to(wrap[None, :], (128, SS // 128)))
        maps.append(m)
    return maps


_CACHE = {}


def kernel(**inputs):
    SP, SS, NQ, ncores = 8192, 16384, 2048, 8
    if "nc" not in _CACHE:
        _CACHE["nc"] = build(SP, SS, NQ)[0]
    nc = _CACHE["nc"]
    in_maps = make_in_maps(inputs, SP, SS, NQ, ncores)
    res = run_bass_kernel_spmd(nc, in_maps, core_ids=list(range(ncores)))
    y_prompt = np.stack([np.asarray(res.results[r]["yo"][:SP], dtype=np.float32) for r in range(ncores)], axis=0)
    y_sample = np.concatenate([np.asarray(res.results[r]["yo"][SP:], dtype=np.float32) for r in range(ncores)], axis=0)[None]
    return (y_prompt, y_sample)
```

```python
import contextlib
import math
import numpy as np
import concourse.bass as bass
import concourse.mybir as mybir
from concourse.bass_utils import run_bass_kernel_spmd

F32 = mybir.dt.float32
BF16 = mybir.dt.bfloat16
AF = mybir.ActivationFunctionType
ALU = mybir.AluOpType
AX = mybir.AxisListType

D = 1024
DFF = 2816
NFC = DFF // 128
PROJ = 4608
COL_Q, COL_K, COL_V, COL_G = 1024, 1536, 2048, 2560
ALPHA = 2.0 ** 0.25
LAMBDA_INIT = 0.8 - 0.6 * math.exp(0.0)
LN_EPS = 1e-5
LB = 1280
ENGS = ("pe", "act", "dve", "pool", "sp")
NDMASEM = 12


class Res:
    __slots__ = ("name", "writer", "readers")

    def __init__(self, name=""):
        self.name = name
        self.writer = None
        self.readers = {}


class Op:
    __slots__ = ("eng", "fn", "deps", "needed", "is_dma", "sem", "val")

    def __init__(self, eng, fn, is_dma):
        self.eng = eng
        self.fn = fn
        self.deps = []
        self.needed = False
        self.is_dma = is_dma
        self.sem = None
        self.val = None


class Sched:
    def __init__(self, nc):
        self.nc = nc
        self.ops = {e: [] for e in ENGS}
        self.dmas = {e: [] for e in ENGS}
        self.pending = {e: [] for e in ENGS}

    def full_barrier(self):
        deps = []
        for e in ENGS:
            comp = [o for o in self.ops[e][-1:] if not o.is_dma]
            for o in reversed(self.ops[e]):
                if not o.is_dma:
                    comp = [o]
                    break
            deps.extend(comp)
            deps.extend(self.dmas[e][-NDMASEM:])
        for o in deps:
            o.needed = True
        for e in ENGS:
            self.pending[e] = list(deps)

    def _track(self, op, reads, writes):
        deps = []
        if self.pending[op.eng]:
            deps.extend(self.pending[op.eng])
            self.pending[op.eng] = []
        for r in reads:
            if r.writer is not None:
                deps.append(r.writer)
        for w in writes:
            if w.writer is not None:
                deps.append(w.writer)
            for o in w.readers.values():
                if isinstance(o, list):
                    deps.extend(o)
                else:
                    deps.append(o)
        for r in reads:
            if op.is_dma:
                r.readers.setdefault("dma_" + op.eng, []).append(op)
            else:
                r.readers[op.eng] = op
        for w in writes:
            w.writer = op
            w.readers = {}
        keep = []
        for d in deps:
            if d is op:
                continue
            if d.eng == "pe" and op.eng == "pe" and not d.is_dma and not op.is_dma:
                continue
            keep.append(d)
            d.needed = True
        op.deps = keep

    def op(self, eng, fn, reads=(), writes=()):
        o = Op(eng, fn, False)
        self._track(o, reads, writes)
        self.ops[eng].append(o)
        return o

    def dma(self, eng, out, in_, reads=(), writes=(), **kw):
        o = Op(eng, (lambda e, out=out, in_=in_, kw=kw: e.dma_start(out=out, in_=in_, **kw)), True)
        o.needed = True
        self._track(o, reads, writes)
        self.ops[eng].append(o)
        self.dmas[eng].append(o)
        return o

    def pe(self, fn, reads=(), writes=()):
        return self.op("pe", fn, reads, writes)

    def act(self, fn, reads=(), writes=()):
        return self.op("act", fn, reads, writes)

    def dve(self, fn, reads=(), writes=()):
        return self.op("dve", fn, reads, writes)

    def pool(self, fn, reads=(), writes=()):
        return self.op("pool", fn, reads, writes)

    def emit(self):
        nc = self.nc
        csem = {e: nc.alloc_semaphore(name=f"c_{e}") for e in ENGS}
        dsem = {e: [nc.alloc_semaphore(name=f"d_{e}_{j}") for j in range(NDMASEM)] for e in ("sp", "pool")}
        last_on = {}
        for e in ENGS:
            n = 0
            nd = 0
            for o in self.ops[e]:
                if o.is_dma:
                    j = nd % NDMASEM
                    o.sem = dsem[e][j]
                    o.val = 16 * (nd // NDMASEM + 1)
                    if (e, j) in last_on:
                        o.deps.append(last_on[(e, j)])
                    last_on[(e, j)] = o
                    nd += 1
                elif o.needed:
                    n += 1
                    o.sem = csem[e]
                    o.val = n
        finals = list(last_on.values())
        stats = {}
        with nc.Block() as block:
            def body(e):
                def run(eng):
                    seen = {}
                    nw = 0
                    for o in self.ops[e]:
                        for d in o.deps:
                            k = id(d.sem)
                            if seen.get(k, 0) < d.val:
                                eng.wait_ge(d.sem, d.val)
                                seen[k] = d.val
                                nw += 1
                        ins = o.fn(eng)
                        if o.is_dma:
                            ins.then_inc(o.sem, 16)
                        elif o.needed:
                            ins.then_inc(o.sem, 1)
                    if e == "sp":
                        for o in finals:
                            k = id(o.sem)
                            if seen.get(k, 0) < o.val:
                                eng.wait_ge(o.sem, o.val)
                                seen[k] = o.val
                    stats[e] = (len(self.ops[e]), nw)
                return run
            block.tensor(body("pe"))
            block.scalar(body("act"))
            block.vector(body("dve"))
            block.gpsimd(body("pool"))
            block.sync(body("sp"))
        return stats


class Ring:
    def __init__(self, views):
        self.views = views
        self.res = [Res() for _ in views]
        self.i = 0

    def next(self):
        k = self.i % len(self.views)
        self.i += 1
        return self.views[k], self.res[k]


def build(SP, SS, NQ, debug=False):
    assert SP % 512 == 0 and SS % 512 == 0 and NQ % 512 == 0 and SS >= NQ + 1024
    R = SP + SS
    NM = SP + NQ
    NBP, NBS = SP // 128, SS // 128
    HB = 32 + SP
    HC = HB + NQ + 256
    nc = bass.Bass("TRN2", target_bir_lowering=False)
    S = Sched(nc)

    def din(name, shape):
        return nc.dram_tensor(name, shape, F32, kind="ExternalInput").ap()

    def dscr(name, shape, dt):
        if debug:
            return nc.dram_tensor(name, shape, dt, kind="ExternalOutput").ap()
        return nc.dram_tensor(name, shape, dt).ap()

    xr = din("xr", [R, D])
    w_ffn = [(din("ffn1_gu", [D, 2 * DFF]), din("ffn1_d", [DFF, D])),
             (din("ffn2_gu", [D, 2 * DFF]), din("ffn2_d", [DFF, D]))]
    w_in = din("w_in", [D, PROJ])
    w_co = din("w_co", [512, D])
    w_ao = din("w_ao", [512, D])
    w_o = din("w_o", [D, D])
    lnv = din("lnv", [6, D])
    NCOL = 16 + 124 + 12 + 4 + 1
    colp = din("colp", [128, NCOL])
    wrapm = din("wrapm", [128, NBS])
    sublg = din("sublg", [1, 128])
    lamv = din("lamv", [1, 256])
    tab = din("tab", [32, 4])
    oh = din("oh", [32, LB])
    ident = din("ident", [128, 128])
    yo = nc.dram_tensor("yo", [NM, D], F32, kind="ExternalOutput").ap()

    x1 = dscr("x1", [R, D], F32)
    hT = dscr("hT", [512, HC], F32)
    QT = dscr("QT", [4, 128, NM], BF16)
    KT = dscr("KT", [4, 128, R], BF16)
    Vs = dscr("Vs", [R, 4 * 129], BF16)
    brep = dscr("brep", [4, 128, LB], F32)
    cactT = dscr("cactT", [512, NM], BF16)
    attnT = dscr("attnT", [512, NM], BF16)
    x2 = dscr("x2", [NM, D], F32)
    r_x1, r_hT, r_QT, r_KT, r_Vs, r_brep, r_cact, r_attn, r_x2 = (Res() for _ in range(9))

    def dram_ap(t, offset, pattern):
        return bass.AP(t.tensor, offset, pattern)

    def bcast_rows(src, row, n, parts=128):
        return dram_ap(src, row * src.shape[1], [[0, parts], [1, n]])

    with contextlib.ExitStack() as top:
        def SB(st, name, shape, dt):
            return st.enter_context(nc.sbuf_tensor(name, shape, dt))

        ps = top.enter_context(nc.psum_tensor("ps", [128, 7, 512], F32))
        ps_res = [Res(f"ps{b}") for b in range(7)]
        pstT = top.enter_context(nc.psum_tensor("pstT", [128, 1024], BF16))
        r_pstT = Res("pstT")

        idf = SB(top, "idf", [128, 128], F32)
        idb = SB(top, "idb", [128, 128], BF16)
        colt = SB(top, "colt", [128, NCOL], F32)
        lam = SB(top, "lam", [128, 8], F32)
        r_idf, r_idb, r_colt, r_lam = Res(), Res(), Res(), Res()
        S.dma("sp", idf[:], ident[:, :], writes=[r_idf])
        S.dma("sp", colt[:], colp[:, :], writes=[r_colt])
        S.dve(lambda e: e.tensor_copy(out=idb[:], in_=idf[:]), reads=[r_idf], writes=[r_idb])
        C_BG, C_CW, C_CB, C_CG, C_CBE, C_MK, C_SG = 0, 16, 140, 144, 148, 152, 156

        def colv(c):
            return colt[:, c:c + 1]

        with contextlib.ExitStack() as st:
            lv = SB(st, "lv", [128, 256], F32)
            lp = SB(st, "lp", [128, 128], F32)
            r_lv, r_lp = Res(), Res()
            S.dma("sp", lv[:], bcast_rows(lamv, 0, 256), writes=[r_lv])
            S.dve(lambda e: e.tensor_tensor(out=lp[:, 0:64], in0=lv[:, 0:64], in1=lv[:, 64:128], op=ALU.mult), reads=[r_lv], writes=[r_lp])
            S.dve(lambda e: e.tensor_tensor(out=lp[:, 64:128], in0=lv[:, 128:192], in1=lv[:, 192:256], op=ALU.mult), reads=[r_lv, r_lp], writes=[r_lp])
            S.dve(lambda e: e.reduce_sum(out=lam[:, 0:1], in_=lp[:, 0:64], axis=AX.X), reads=[r_lp], writes=[r_lam])
            S.dve(lambda e: e.reduce_sum(out=lam[:, 1:2], in_=lp[:, 64:128], axis=AX.X), reads=[r_lp, r_lam], writes=[r_lam])
            S.act(lambda e: e.activation(out=lam[:, 2:4], in_=lam[:, 0:2], func=AF.Exp), reads=[r_lam], writes=[r_lam])
            S.dve(lambda e: e.tensor_tensor(out=lam[:, 4:5], in0=lam[:, 3:4], in1=lam[:, 2:3], op=ALU.subtract), reads=[r_lam], writes=[r_lam])
            S.dve(lambda e: e.tensor_scalar_add(out=lam[:, 5:6], in0=lam[:, 4:5], scalar1=-LAMBDA_INIT), reads=[r_lam], writes=[r_lam])
            tb = SB(st, "tb", [32, 4, 128], F32)
            oht = SB(st, "oht", [32, LB], F32)
            gb = SB(st, "gb", [128, LB], F32)
            r_tb, r_oht, r_gb = Res(), Res(), Res()
            S.dma("sp", oht[:], oh[:, :], writes=[r_oht])
            tabt = SB(st, "tabt", [32, 4], F32)
            r_tabt = Res()
            S.dma("sp", tabt[:], tab[:, :], writes=[r_tabt])
            S.pool(lambda e: e.memset(tb[:], 1.0), writes=[r_tb])
            for h in range(4):
                S.dve(lambda e, h=h: e.tensor_scalar_mul(out=tb[:, h, :], in0=tb[:, h, :], scalar1=tabt[:, h:h + 1]), reads=[r_tb, r_tabt], writes=[r_tb])
            for h in range(4):
                for (c0, c1) in ((0, 512), (512, 1024), (1024, LB)):
                    S.pe(lambda e, h=h, c0=c0, c1=c1: e.matmul(ps[:, 0, 0:c1 - c0], tb[:, h, :], oht[:, c0:c1], start=True, stop=True),
                         reads=[r_tb, r_oht], writes=[ps_res[0]])
                    S.dve(lambda e, c0=c0, c1=c1: e.tensor_copy(out=gb[:, c0:c1], in_=ps[:, 0, 0:c1 - c0]), reads=[ps_res[0]], writes=[r_gb])
                S.dma("sp", brep[h], gb[:], reads=[r_gb], writes=[r_brep])

        S.full_barrier()

        def load_weight(st, name, src, K, N, queue="pool"):
            kc = K // 128
            t = SB(st, name, [128, kc, N], BF16)
            r = Res(name)
            for c in range(kc):
                S.dma(queue, t[:, c, :], src[c * 128:(c + 1) * 128, :], writes=[r])
            return t, r

        def load_T(rows_ap_fn, nsub, xin_ring, xbf_ring, xT, r_xT, tbank, r_src=None):
            for s in range(nsub):
                xin, r_xin = xin_ring.next()
                xbf, r_xbf = xbf_ring.next()
                S.dma("sp", xin, rows_ap_fn(s), writes=[r_xin])
                S.act(lambda e, xbf=xbf, xin=xin: e.copy(out=xbf, in_=xin), reads=[r_xin], writes=[r_xbf])
                pst = pstT[:, :]
                for c in range(8):
                    S.pe(lambda e, c=c, xbf=xbf, pst=pst: e.transpose(pst[:, c * 128:(c + 1) * 128], xbf[:, c * 128:(c + 1) * 128], idb[:]),
                         reads=[r_xbf, r_idb], writes=[r_pstT])
                S.dve(lambda e, s=s, pst=pst: e.tensor_copy(out=xT[:, :, s * 128:(s + 1) * 128],
                                                           in_=pst.rearrange("p (c t) -> p c t", c=8)),
                      reads=[r_pstT], writes=[r_xT])

        def layer_norm_store(st_tiles, z, r_z, gB, bB, r_gb, dst_ap):
            stt, r_stt, mv, r_mv = st_tiles
            S.dve(lambda e: e.bn_stats(out=stt[:, 0:6], in_=z[:, 0:512]), reads=[r_z], writes=[r_stt])
            S.dve(lambda e: e.bn_stats(out=stt[:, 6:12], in_=z[:, 512:1024]), reads=[r_z, r_stt], writes=[r_stt])
            S.dve(lambda e: e.bn_aggr(out=mv[:, 0:2], in_=stt[:, 0:12]), reads=[r_stt], writes=[r_mv])
            S.act(lambda e: e.activation(out=mv[:, 2:3], in_=mv[:, 1:2], func=AF.Sqrt, bias=epsc[:, 0:1], scale=1.0), reads=[r_mv, r_eps], writes=[r_mv])
            S.dve(lambda e: e.reciprocal(out=mv[:, 3:4], in_=mv[:, 2:3]), reads=[r_mv], writes=[r_mv])
            S.dve(lambda e: e.tensor_scalar(out=z, in0=z, scalar1=mv[:, 0:1], scalar2=mv[:, 3:4], op0=ALU.subtract, op1=ALU.mult),
                  reads=[r_z, r_mv], writes=[r_z])
            S.pool(lambda e: e.tensor_tensor(out=z, in0=z, in1=gB[:], op=ALU.mult), reads=[r_z, r_gb], writes=[r_z])
            S.pool(lambda e: e.tensor_tensor(out=z, in0=z, in1=bB[:], op=ALU.add), reads=[r_z, r_gb], writes=[r_z])

        cbt = SB(top, "cbt", [128, 12], F32)
        r_cbt = Res()
        S.dma("sp", cbt[:, 0:4], bcast_rows(tab, 15, 4), writes=[r_cbt])
        S.dma("sp", cbt[:, 4:8], bcast_rows(tab, 31, 4), writes=[r_cbt])
        S.dve(lambda e: e.tensor_tensor(out=cbt[:, 8:12], in0=cbt[:, 0:4], in1=cbt[:, 4:8], op=ALU.subtract), reads=[r_cbt], writes=[r_cbt])
        epsc = SB(top, "epsc", [128, 1], F32)
        r_eps = Res()
        S.pool(lambda e: e.memset(epsc[:], LN_EPS), writes=[r_eps])

        def ffn_phase(tag, src, r_src, nrows, wgu_d, wd_d, ln_row, dst, r_dst):
            with contextlib.ExitStack() as st:
                wgu, r_wgu = load_weight(st, "wgu" + tag, wgu_d, D, 2 * DFF)
                wd, r_wd = load_weight(st, "wd" + tag, wd_d, DFF, D)
                gB = SB(st, "gB" + tag, [128, D], F32)
                bB = SB(st, "bB" + tag, [128, D], F32)
                r_gb = Res()
                S.dma("sp", gB[:], bcast_rows(lnv, ln_row, D), writes=[r_gb])
                S.dma("sp", bB[:], bcast_rows(lnv, ln_row + 1, D), writes=[r_gb])
                xin_t = SB(st, "xin" + tag, [128, 2, D], F32)
                xbf_t = SB(st, "xbf" + tag, [128, 2, D], BF16)
                xin_ring = Ring([xin_t[:, k, :] for k in range(2)])
                xbf_ring = Ring([xbf_t[:, k, :] for k in range(2)])
                xT = SB(st, "xT" + tag, [128, 8, 512], BF16)
                r_xT = Res()
                actT = SB(st, "actT" + tag, [128, NFC, 512], BF16)
                r_act = [Res() for _ in range(NFC)]
                sa_t = SB(st, "sa" + tag, [128, 2, 512], F32)
                sa_ring = Ring([sa_t[:, k, :] for k in range(2)])
                z_t = SB(st, "z" + tag, [128, 3, D], F32)
                z_ring = Ring([z_t[:, k, :] for k in range(3)])
                stt_t = SB(st, "stt" + tag, [128, 2, 12], F32)
                mv_t = SB(st, "mv" + tag, [128, 2, 4], F32)
                st_ring = Ring([(stt_t[:, k, :], mv_t[:, k, :]) for k in range(2)])
                st_res2 = [(Res(), Res()) for _ in range(2)]
                ntile = nrows // 512
                load_T(lambda s: src[s * 128:(s + 1) * 128, :], 4, xin_ring, xbf_ring, xT, r_xT, 6)
                for t in range(ntile):
                    r0 = t * 512
                    for f in range(NFC):
                        ba, bg = (0, 1) if f % 2 == 0 else (2, 3)
                        for dd in range(8):
                            S.pe(lambda e, f=f, dd=dd, ba=ba: e.matmul(ps[:, ba, :], wgu[:, dd, f * 128:(f + 1) * 128], xT[:, dd, :], start=(dd == 0), stop=(dd == 7)),
                                 reads=[r_wgu, r_xT], writes=[ps_res[ba]])
                        for dd in range(8):
                            S.pe(lambda e, f=f, dd=dd, bg=bg: e.matmul(ps[:, bg, :], wgu[:, dd, DFF + f * 128:DFF + (f + 1) * 128], xT[:, dd, :], start=(dd == 0), stop=(dd == 7)),
                                 reads=[r_wgu, r_xT], writes=[ps_res[bg]])
                        sa, r_sa = sa_ring.next()
                        S.act(lambda e, sa=sa, ba=ba: e.activation(out=sa, in_=ps[:, ba, :], func=AF.Silu), reads=[ps_res[ba]], writes=[r_sa])
                        S.dve(lambda e, sa=sa, bg=bg, f=f: e.tensor_tensor(out=actT[:, f, :], in0=sa, in1=ps[:, bg, :], op=ALU.mult),
                              reads=[r_sa, ps_res[bg]], writes=[r_act[f]])
                    if t + 1 < ntile:
                        load_T(lambda s, r1=r0 + 512: src[r1 + s * 128:r1 + (s + 1) * 128, :], 4, xin_ring, xbf_ring, xT, r_xT, 6)
                    for s in range(4):
                        z, r_z = z_ring.next()
                        rows = slice(r0 + s * 128, r0 + (s + 1) * 128)
                        S.dma("sp", z, src[rows, :], reads=[r_src], writes=[r_z])
                        S.act(lambda e, z=z: e.mul(out=z, in_=z, mul=ALPHA), reads=[r_z], writes=[r_z])
                        for half in range(2):
                            b = 4 + half
                            for f in range(NFC):
                                S.pe(lambda e, f=f, s=s, half=half, b=b: e.matmul(ps[:, b, :], actT[:, f, s * 128:(s + 1) * 128], wd[:, f, half * 512:(half + 1) * 512], start=(f == 0), stop=(f == NFC - 1)),
                                     reads=[r_act[f], r_wd], writes=[ps_res[b]])
                        S.dve(lambda e, z=z: e.scalar_tensor_tensor(out=z, in0=ps[:, 4:6, :].rearrange("p a b -> p (a b)"), scalar=0.5, in1=z, op0=ALU.mult, op1=ALU.add),
                              reads=[ps_res[4], ps_res[5], r_z], writes=[r_z])
                        (stt, mv), _ = st_ring.next()
                        k = (st_ring.i - 1) % 2
                        layer_norm_store((stt, st_res2[k][0], mv, st_res2[k][1]), z, r_z, gB, bB, r_gb, None)
                        S.dma("pool", dst[rows, :], z, reads=[r_z], writes=[r_dst])

        r_xr = Res()
        ffn_phase("f1", xr, r_xr, R, w_ffn[0][0], w_ffn[0][1], 0, x1, r_x1)

        S.full_barrier()
        with contextlib.ExitStack() as st:
            win, r_win = load_weight(st, "win", w_in[:, 0:COL_G], D, COL_G)
            xin_t = SB(st, "xin2", [128, 2, D], F32)
            xbf_t = SB(st, "xbf2", [128, 2, D], BF16)
            xin_ring = Ring([xin_t[:, k, :] for k in range(2)])
            xbf_ring = Ring([xbf_t[:, k, :] for k in range(2)])
            xT2_t = SB(st, "xT2", [128, 2, 8, 512], BF16)
            xT2_ring = Ring([xT2_t[:, k, :, :] for k in range(2)])
            sg_t = SB(st, "sg2", [128, 2, 512], F32)
            sg_ring = Ring([sg_t[:, k, :] for k in range(2)])
            ho_t = SB(st, "ho2", [128, 3, 512], F32)
            ho_ring = Ring([ho_t[:, k, :] for k in range(3)])
            qk_t = SB(st, "qk2", [128, 3, 512], BF16)
            qk_ring = Ring([qk_t[:, k, :] for k in range(3)])
            v_t = SB(st, "v2", [128, 2, 4, 129], BF16)
            v_ring = Ring([v_t[:, k, :, :] for k in range(2)])
            zt = SB(st, "zero2", [128, 16], F32)
            r_zt = Res()
            S.pool(lambda e: e.memset(zt[:], 0.0), writes=[r_zt])
            S.pool(lambda e: e.memset(v_t[:], 1.0), writes=v_ring.res)
            for c in range(4):
                S.dma("sp", hT[c * 128:(c + 1) * 128, 0:16], zt[:], reads=[r_zt], writes=[r_hT])
                S.dma("sp", hT[c * 128:(c + 1) * 128, 16 + SP:32 + SP], zt[:], reads=[r_zt], writes=[r_hT])
            xT_next = xT2_ring.next()
            load_T(lambda s: x1[s * 128:(s + 1) * 128, :], 4, xin_ring, xbf_ring, xT_next[0], xT_next[1], 6)
            for t in range(R // 512):
                r0 = t * 512
                xT, r_xT = xT_next
                if r0 + 512 < R:
                    xT_next = xT2_ring.next()
                    load_T(lambda s, r1=r0 + 512: x1[r1 + s * 128:r1 + (s + 1) * 128, :], 4, xin_ring, xbf_ring, xT_next[0], xT_next[1], 6)
                samp = r0 >= SP
                rs = r0 - SP
                need_q = (not samp) or rs < NQ
                if not samp:
                    need_u, hcol, ucols, mask = True, 16 + r0, (0, 512), None
                elif rs < NQ:
                    need_u, hcol, ucols, mask = True, HB + 128 + rs, (0, 512), None
                elif rs == NQ:
                    need_u, hcol, ucols, mask = True, HB + 128 + NQ, (0, 128), C_MK + 1
                elif rs == SS - 512:
                    need_u, hcol, ucols, mask = True, HB - 384, (384, 512), C_MK + 0
                else:
                    need_u = False
                if False:
                    dxT = nc.dram_tensor("dbg_xT", [128, 8 * 512], BF16, kind="ExternalOutput").ap()
                    dwin = nc.dram_tensor("dbg_win", [128, 2560], BF16, kind="ExternalOutput").ap()
                    dxin = nc.dram_tensor("dbg_xin", [128, 1024], F32, kind="ExternalOutput").ap()
                    S.dma("sp", dxT[:, :], xT[:].rearrange("p c t -> p (c t)"), reads=[r_xT])
                    S.dma("sp", dwin[:, :], win[:, 0, :], reads=[r_win])
                    S.dma("sp", dxin[:, :], xin_t[:, 1, :], reads=[xin_ring.res[1]])
                nb = [0]

                def bank():
                    b = nb[0] % 4
                    nb[0] += 1
                    return b

                def proj_fm(col0):
                    b = bank()
                    for dd in range(8):
                        S.pe(lambda e, dd=dd, b=b, col0=col0, xT=xT: e.matmul(ps[:, b, :], win[:, dd, col0:col0 + 128], xT[:, dd, :], start=(dd == 0), stop=(dd == 7)),
                             reads=[r_win, r_xT], writes=[ps_res[b]])
                    return b

                if need_u:
                    for j in range(4):
                        ba = proj_fm(j * 128)
                        bg = proj_fm(512 + j * 128)
                        sg, r_sg = sg_ring.next()
                        ho, r_ho = ho_ring.next()
                        S.act(lambda e, sg=sg, bg=bg: e.activation(out=sg, in_=ps[:, bg, :], func=AF.Sigmoid), reads=[ps_res[bg]], writes=[r_sg])
                        S.dve(lambda e, sg=sg, ho=ho, ba=ba: e.tensor_tensor(out=ho, in0=sg, in1=ps[:, ba, :], op=ALU.mult), reads=[r_sg, ps_res[ba]], writes=[r_ho])
                        if mask is not None:
                            S.dve(lambda e, ho=ho, mask=mask: e.tensor_scalar_mul(out=ho, in0=ho, scalar1=colv(mask)), reads=[r_ho, r_colt], writes=[r_ho])
                        u0, u1 = ucols
                        S.dma("pool", hT[j * 128:(j + 1) * 128, hcol + u0:hcol + u1], ho[:, u0:u1], reads=[r_ho], writes=[r_hT])
                for (need, col0, dstT, r_d, cbase) in ((need_q, COL_Q, QT, r_QT, (r0 if not samp else SP + rs)), (True, COL_K, KT, r_KT, r0)):
                    if not need:
                        continue
                    for h in range(4):
                        b = proj_fm(col0 + h * 128)
                        qk, r_qk = qk_ring.next()
                        S.act(lambda e, qk=qk, b=b: e.copy(out=qk, in_=ps[:, b, :]), reads=[ps_res[b]], writes=[r_qk])
                        S.dma("pool", dstT[h, :, cbase:cbase + 512], qk, reads=[r_qk], writes=[r_d])
                for s in range(4):
                    b = bank()
                    for dd in range(8):
                        S.pe(lambda e, dd=dd, b=b, s=s, xT=xT: e.matmul(ps[:, b, :], xT[:, dd, s * 128:(s + 1) * 128], win[:, dd, COL_V:COL_V + 512], start=(dd == 0), stop=(dd == 7)),
                             reads=[r_win, r_xT], writes=[ps_res[b]])
                    vt, r_vt = v_ring.next()
                    S.dve(lambda e, vt=vt, b=b: e.tensor_copy(out=vt[:, :, 0:128], in_=ps[:, b, :].rearrange("p (h e) -> p h e", h=4)), reads=[ps_res[b]], writes=[r_vt])
                    S.dma("pool", Vs[r0 + s * 128:r0 + (s + 1) * 128, :], vt.rearrange("p h e -> p (h e)"), reads=[r_vt], writes=[r_Vs])

        S.full_barrier()
        with contextlib.ExitStack() as st:
            hin_t = SB(st, "hin", [128, 2, 4, 544], F32)
            hin_ring = Ring([hin_t[:, k, :, :] for k in range(2)])
            acc = SB(st, "cacc", [128, 4, 512], F32)
            r_acc = [Res() for _ in range(4)]
            sq = SB(st, "csq", [128, 4, 512], F32)
            r_sq = [Res() for _ in range(4)]
            onesm = SB(st, "onesm", [128, 128], F32)
            r_ones = Res()
            S.pool(lambda e: e.memset(onesm[:], 1.0 / 512.0), writes=[r_ones])
            mean = SB(st, "cmean", [128, 512], F32)
            rstd = SB(st, "crstd", [128, 512], F32)
            r_mean, r_rstd = Res(), Res()
            co_t = SB(st, "cout", [128, 3, 512], BF16)
            co_ring = Ring([co_t[:, k, :] for k in range(3)])
            tiles = [(16 + t * 512, t * 512) for t in range(SP // 512)] + [(HB + 128 + t * 512, SP + t * 512) for t in range(NQ // 512)]
            for (hc, mrow) in tiles:
                hin, r_hin = hin_ring.next()
                for c in range(4):
                    S.dma("sp", hin[:, c, 0:542], hT[c * 128:(c + 1) * 128, hc - 15:hc + 527], reads=[r_hT], writes=[r_hin])
                for j in range(31):
                    for c in range(4):
                        if j == 0:
                            S.dve(lambda e, c=c, hin=hin: e.tensor_scalar(out=acc[:, c, :], in0=hin[:, c, 0:512], scalar1=colv(C_CW + c * 31), scalar2=colv(C_CB + c), op0=ALU.mult, op1=ALU.add),
                                  reads=[r_hin, r_colt], writes=[r_acc[c]])
                        else:
                            S.dve(lambda e, c=c, j=j, hin=hin: e.scalar_tensor_tensor(out=acc[:, c, :], in0=hin[:, c, j:j + 512], scalar=colv(C_CW + c * 31 + j), in1=acc[:, c, :], op0=ALU.mult, op1=ALU.add),
                                  reads=[r_hin, r_colt, r_acc[c]], writes=[r_acc[c]])
                for c in range(4):
                    S.pool(lambda e, c=c: e.tensor_tensor(out=sq[:, c, :], in0=acc[:, c, :], in1=acc[:, c, :], op=ALU.mult), reads=[r_acc[c]], writes=[r_sq[c]])
                for c in range(4):
                    S.pe(lambda e, c=c: e.matmul(ps[:, 0, :], onesm[:], acc[:, c, :], start=(c == 0), stop=(c == 3)), reads=[r_ones, r_acc[c]], writes=[ps_res[0]])
                for c in range(4):
                    S.pe(lambda e, c=c: e.matmul(ps[:, 1, :], onesm[:], sq[:, c, :], start=(c == 0), stop=(c == 3)), reads=[r_ones, r_sq[c]], writes=[ps_res[1]])
                S.act(lambda e: e.copy(out=mean[:], in_=ps[:, 0, :]), reads=[ps_res[0]], writes=[r_mean])
                S.pool(lambda e: e.tensor_tensor(out=rstd[:], in0=mean[:], in1=mean[:], op=ALU.mult), reads=[r_mean], writes=[r_rstd])
                S.dve(lambda e: e.tensor_tensor(out=rstd[:], in0=ps[:, 1, :], in1=rstd[:], op=ALU.subtract), reads=[ps_res[1], r_rstd], writes=[r_rstd])
                S.act(lambda e: e.activation(out=rstd[:], in_=rstd[:], func=AF.Sqrt, bias=epsc[:, 0:1], scale=1.0), reads=[r_rstd, r_eps], writes=[r_rstd])
                S.dve(lambda e: e.reciprocal(out=rstd[:], in_=rstd[:]), reads=[r_rstd], writes=[r_rstd])
                for c in range(4):
                    S.pool(lambda e, c=c: e.tensor_tensor(out=acc[:, c, :], in0=acc[:, c, :], in1=mean[:], op=ALU.subtract), reads=[r_acc[c], r_mean], writes=[r_acc[c]])
                    S.dve(lambda e, c=c: e.tensor_tensor(out=acc[:, c, :], in0=acc[:, c, :], in1=rstd[:], op=ALU.mult), reads=[r_acc[c], r_rstd], writes=[r_acc[c]])
                    co, r_co = co_ring.next()
                    S.act(lambda e, c=c, co=co: e.activation(out=co, in_=acc[:, c, :], func=AF.Silu, bias=colv(C_CBE + c), scale=colv(C_CG + c)), reads=[r_acc[c], r_colt], writes=[r_co])
                    S.dma("pool", cactT[c * 128:(c + 1) * 128, mrow:mrow + 512], co, reads=[r_co], writes=[r_cact])

        S.full_barrier()
        with contextlib.ExitStack() as st:
            KMAX = max(SP, SS)
            kt_t = SB(st, "kt", [128, KMAX], BF16)
            vh_t = SB(st, "vh", [128, KMAX // 128, 128], BF16)
            r_kt, r_vh = Res(), Res()
            bt = SB(st, "bt", [128, 8, 512], F32)
            r_bt = Res()
            farb = SB(st, "farb", [128, NBS], F32)
            r_farb = Res()
            wm = SB(st, "wm", [128, NBS], F32)
            r_wm = Res()
            S.dma("sp", wm[:], wrapm[:, :], writes=[r_wm])
            gcol = SB(st, "gcol", [128, 1], F32)
            r_gcol = Res()
            S.dve(lambda e: e.tensor_scalar_mul(out=gcol[:], in0=colv(C_SG), scalar1=1.0 - LAMBDA_INIT), reads=[r_colt], writes=[r_gcol])
            onesF = SB(st, "onesF", [128, 128], F32)
            r_onesF = Res()
            S.pool(lambda e: e.memset(onesF[:], 1.0), writes=[r_onesF])
            qt_t = SB(st, "qt", [128, 2, 512], BF16)
            qt_ring = Ring([qt_t[:, k, :] for k in range(2)])
            pt_t = SB(st, "pt", [128, 3, 2, 512], BF16)
            pt_ring = Ring([pt_t[:, k, :, :] for k in range(3)])
            tmp_t = SB(st, "stmp", [128, 2, 2, 512], F32)
            tmp_ring = Ring([tmp_t[:, k, :, :] for k in range(2)])
            dacc = SB(st, "dacc", [128, 2, 2, 512], F32)
            r_dacc = [Res(), Res()]
            rec = SB(st, "rec", [128, 2, 512], F32)
            r_rec = [Res(), Res()]
            aa_ = SB(st, "a4", [128, 512], F32)
            bb_ = SB(st, "b4", [128, 512], F32)
            sq_ = SB(st, "sq4", [128, 512], F32)
            rs_ = SB(st, "rs4", [128, 512], F32)
            r_aa, r_bb, r_sq4, r_rs = Res(), Res(), Res(), Res()
            at_t = SB(st, "at", [128, 2, 512], BF16)
            at_ring = Ring([at_t[:, k, :] for k in range(2)])
            units = [(0, SP, SP, 0, False, h) for h in range(4)] + [(SP, SS, NQ, SP, True, h) for h in range(4)]
            for (krow0, S_k, nq, qcol0, samp, h) in units:
                NB = S_k // 128
                nqt = nq // 512
                S.dma("sp", kt_t[:, 0:S_k], KT[h, :, krow0:krow0 + S_k], reads=[r_KT], writes=[r_kt])
                nvc = max(1, NB // 16)
                for vc in range(nvc):
                    b0, b1 = vc * NB // nvc, (vc + 1) * NB // nvc
                    S.dma("sp", vh_t[:, b0:b1, :],
                          dram_ap(Vs, (krow0 + b0 * 128) * 516 + h * 129, [[516, 128], [128 * 516, b1 - b0], [1, 128]]),
                          reads=[r_Vs], writes=[r_vh])
                for kk in range(6):
                    S.dma("sp", bt[:, kk, :], dram_ap(brep, h * 128 * LB + 639 - 128 * (kk - 1), [[LB - 1, 128], [1, 512]]), reads=[r_brep], writes=[r_bt])
                cneg, cpos, cdif = cbt[:, h:h + 1], cbt[:, 4 + h:5 + h], cbt[:, 8 + h:9 + h]
                if samp:
                    S.dve(lambda e, cdif=cdif, cpos=cpos: e.tensor_scalar(out=farb[:], in0=wm[:], scalar1=cdif, scalar2=cpos, op0=ALU.mult, op1=ALU.add), reads=[r_wm, r_cbt], writes=[r_farb])
                    for (dst, srck, cc, mcol) in ((6, 0, cpos, C_MK + 2), (7, 5, cneg, C_MK + 3)):
                        S.dve(lambda e, dst=dst, srck=srck, cc=cc, mcol=mcol: e.tensor_scalar(out=bt[:, dst, :], in0=bt[:, srck, :], scalar1=cc, scalar2=colv(mcol), op0=ALU.subtract, op1=ALU.mult),
                              reads=[r_bt, r_cbt, r_colt], writes=[r_bt])
                        S.dve(lambda e, dst=dst, cc=cc: e.tensor_scalar_add(out=bt[:, dst, :], in0=bt[:, dst, :], scalar1=cc), reads=[r_bt, r_cbt], writes=[r_bt])
                for i in range(nqt):
                    qt, r_qt = qt_ring.next()
                    S.dma("sp", qt, QT[h, :, qcol0 + i * 512:qcol0 + (i + 1) * 512], reads=[r_QT], writes=[r_qt])

                    def kind(j):
                        rel = j - 4 * i
                        if samp and i == 0 and j == NB - 1:
                            return ("near", 6)
                        if samp and i == nqt - 1 and rel == 4:
                            return ("near", 7)
                        if -1 <= rel <= 4:
                            return ("near", rel + 1)
                        if rel < -1:
                            return ("far", cneg, r_cbt)
                        if samp:
                            return ("far", farb[:, j:j + 1], r_farb)
                        return ("far", cpos, r_cbt)

                    pend = {}

                    def qk(j):
                        p = 2 * (j % 2)
                        for m in range(2):
                            S.pe(lambda e, j=j, m=m, p=p, qt=qt: e.matmul(ps[:, p + m, :], kt_t[m * 64:(m + 1) * 64, j * 128:(j + 1) * 128], qt[m * 64:(m + 1) * 64, :], start=True, stop=True),
                                 reads=[r_kt, r_qt], writes=[ps_res[p + m]])
                        kd = kind(j)
                        pt, r_pt = pt_ring.next()
                        if kd[0] == "near":
                            tmp, r_tmp = tmp_ring.next()
                            for m in range(2):
                                S.dve(lambda e, m=m, p=p, tmp=tmp, kk=kd[1]: e.scalar_tensor_tensor(out=tmp[:, m, :], in0=ps[:, p + m, :], scalar=0.125, in1=bt[:, kk, :], op0=ALU.mult, op1=ALU.add),
                                      reads=[ps_res[p + m], r_bt, r_tmp], writes=[r_tmp])
                            S.act(lambda e, pt=pt, tmp=tmp: e.activation(out=pt, in_=tmp, func=AF.Exp), reads=[r_tmp], writes=[r_pt])
                        else:
                            S.act(lambda e, pt=pt, p=p, bc=kd[1]: e.activation(out=pt, in_=ps[:, p:p + 2, :], func=AF.Exp, bias=bc, scale=0.125),
                                  reads=[ps_res[p], ps_res[p + 1], kd[2]], writes=[r_pt])
                        pend[j] = (pt, r_pt)

                    def avmm(j):
                        pt, r_pt = pend.pop(j)
                        for m in range(2):
                            S.pe(lambda e, j=j, m=m, pt=pt, st_=(j == 0), sp_=(j == NB - 1): e.matmul(ps[:, 4 + m, :], vh_t[:, j, :], pt[:, m, :], start=st_, stop=sp_),
                                 reads=[r_pt, r_vh], writes=[ps_res[4 + m]])
                        w = 1 if (j % 4 == 3) else 0
                        first = (j == 0) or (j == 3)
                        fn = S.pool if w else S.dve
                        if first:
                            fn(lambda e, pt=pt, w=w: e.tensor_copy(out=dacc[:, w, :, :], in_=pt), reads=[r_pt], writes=[r_dacc[w]])
                        else:
                            fn(lambda e, pt=pt, w=w: e.tensor_tensor(out=dacc[:, w, :, :], in0=dacc[:, w, :, :], in1=pt, op=ALU.add), reads=[r_pt, r_dacc[w]], writes=[r_dacc[w]])

                    qk(0)
                    for j in range(NB):
                        if j + 1 < NB:
                            qk(j + 1)
                        avmm(j)
                    for m in range(2):
                        S.pe(lambda e, m=m: e.matmul(ps[:, 6, :], onesF[:], dacc[:, 0, m, :], start=True, stop=False), reads=[r_onesF, r_dacc[0]], writes=[ps_res[6]])
                        S.pe(lambda e, m=m: e.matmul(ps[:, 6, :], onesF[:], dacc[:, 1, m, :], start=False, stop=True), reads=[r_onesF, r_dacc[1]], writes=[ps_res[6]])
                        S.dve(lambda e, m=m: e.reciprocal(out=rec[:, m, :], in_=ps[:, 6, :]), reads=[ps_res[6]], writes=[r_rec[m]])
                    S.dve(lambda e: e.tensor_tensor(out=aa_[:], in0=ps[:, 4, :], in1=rec[:, 0, :], op=ALU.mult), reads=[ps_res[4], r_rec[0]], writes=[r_aa])
                    S.dve(lambda e: e.tensor_tensor(out=bb_[:], in0=ps[:, 5, :], in1=rec[:, 1, :], op=ALU.mult), reads=[ps_res[5], r_rec[1]], writes=[r_bb])
                    S.dve(lambda e: e.scalar_tensor_tensor(out=aa_[:], in0=bb_[:], scalar=lam[:, 5:6], in1=aa_[:], op0=ALU.mult, op1=ALU.add), reads=[r_bb, r_aa, r_lam], writes=[r_aa])
                    S.pool(lambda e: e.tensor_tensor(out=sq_[:], in0=aa_[:], in1=aa_[:], op=ALU.mult), reads=[r_aa], writes=[r_sq4])
                    S.pe(lambda e: e.matmul(ps[:, 6, :], onesF[:], sq_[:], start=True, stop=True), reads=[r_onesF, r_sq4], writes=[ps_res[6]])
                    S.act(lambda e: e.activation(out=rs_[:], in_=ps[:, 6, :], func=AF.Sqrt, bias=epsc[:, 0:1], scale=1.0 / 128.0), reads=[ps_res[6], r_eps], writes=[r_rs])
                    S.dve(lambda e: e.reciprocal(out=rs_[:], in_=rs_[:]), reads=[r_rs], writes=[r_rs])
                    at, r_at = at_ring.next()
                    S.dve(lambda e, at=at: e.scalar_tensor_tensor(out=at, in0=aa_[:], scalar=gcol[:, 0:1], in1=rs_[:], op0=ALU.mult, op1=ALU.mult), reads=[r_aa, r_gcol, r_rs], writes=[r_at])
                    S.dma("pool", attnT[h * 128:(h + 1) * 128, qcol0 + i * 512:qcol0 + (i + 1) * 512], at, reads=[r_at], writes=[r_attn])

        S.full_barrier()
        with contextlib.ExitStack() as st:
            wg, r_wg = load_weight(st, "wg", w_in[:, COL_G:PROJ], D, 2 * D)
            wco, r_wco = load_weight(st, "wco", w_co, 512, D)
            wao, r_wao = load_weight(st, "wao", w_ao, 512, D)
            wo, r_wo = load_weight(st, "wo", w_o, D, D)
            gB = SB(st, "gB5", [128, D], F32)
            bB = SB(st, "bB5", [128, D], F32)
            r_gb = Res()
            S.dma("sp", gB[:], bcast_rows(lnv, 2, D), writes=[r_gb])
            S.dma("sp", bB[:], bcast_rows(lnv, 3, D), writes=[r_gb])
            xin_t = SB(st, "xin5", [128, 2, D], F32)
            xbf_t = SB(st, "xbf5", [128, 2, D], BF16)
            xin_ring = Ring([xin_t[:, k, :] for k in range(2)])
            xbf_ring = Ring([xbf_t[:, k, :] for k in range(2)])
            xT5v = SB(st, "xT5", [128, 8, 512], BF16)
            r_xT = Res()
            ca_t = SB(st, "ca5", [128, 2, 4, 512], BF16)
            ca_ring = Ring([ca_t[:, k, :, :] for k in range(2)])
            aa_t = SB(st, "aa5", [128, 2, 4, 512], BF16)
            aa_ring = Ring([aa_t[:, k, :, :] for k in range(2)])
            g_t = SB(st, "g5", [128, 4, 512], F32)
            g_ring = Ring([g_t[:, k, :] for k in range(4)])
            mT = SB(st, "mT5", [128, 8, 512], BF16)
            r_mT = [Res() for _ in range(8)]
            z_t = SB(st, "z5", [128, 3, D], F32)
            z_ring = Ring([z_t[:, k, :] for k in range(3)])
            stt_t = SB(st, "stt5", [128, 2, 12], F32)
            mv_t = SB(st, "mv5", [128, 2, 4], F32)
            st_res2 = [(Res(), Res()) for _ in range(2)]
            nst = 0
            load_T(lambda s: x1[s * 128:(s + 1) * 128, :], 4, xin_ring, xbf_ring, xT5v, r_xT, 6)
            for t in range(NM // 512):
                r0 = t * 512
                xrow = r0
                ca, r_ca = ca_ring.next()
                aa, r_aa = aa_ring.next()
                for c in range(4):
                    S.dma("sp", ca[:, c, :], cactT[c * 128:(c + 1) * 128, r0:r0 + 512], reads=[r_cact], writes=[r_ca])
                    S.dma("sp", aa[:, c, :], attnT[c * 128:(c + 1) * 128, r0:r0 + 512], reads=[r_attn], writes=[r_aa])
                for n in range(8):
                    for (b, col0) in ((0, n * 128), (1, D + n * 128)):
                        for dd in range(8):
                            S.pe(lambda e, b=b, dd=dd, col0=col0: e.matmul(ps[:, b, :], wg[:, dd, col0:col0 + 128], xT5v[:, dd, :], start=(dd == 0), stop=(dd == 7)),
                                 reads=[r_wg, r_xT], writes=[ps_res[b]])
                    for c in range(4):
                        S.pe(lambda e, c=c, n=n, ca=ca: e.matmul(ps[:, 2, :], wco[:, c, n * 128:(n + 1) * 128], ca[:, c, :], start=(c == 0), stop=(c == 3)),
                             reads=[r_wco, r_ca], writes=[ps_res[2]])
                    for c in range(4):
                        S.pe(lambda e, c=c, n=n, aa=aa: e.matmul(ps[:, 3, :], wao[:, c, n * 128:(n + 1) * 128], aa[:, c, :], start=(c == 0), stop=(c == 3)),
                             reads=[r_wao, r_aa], writes=[ps_res[3]])
                    gc, r_gc = g_ring.next()
                    ga, r_ga = g_ring.next()
                    S.act(lambda e, gc=gc, n=n: e.activation(out=gc, in_=ps[:, 0, :], func=AF.Sigmoid, bias=colv(C_BG + n), scale=1.0), reads=[ps_res[0], r_colt], writes=[r_gc])
                    S.act(lambda e, ga=ga, n=n: e.activation(out=ga, in_=ps[:, 1, :], func=AF.Sigmoid, bias=colv(C_BG + 8 + n), scale=1.0), reads=[ps_res[1], r_colt], writes=[r_ga])
                    S.dve(lambda e, gc=gc: e.tensor_tensor(out=gc, in0=gc, in1=ps[:, 2, :], op=ALU.mult), reads=[r_gc, ps_res[2]], writes=[r_gc])
                    S.dve(lambda e, ga=ga: e.tensor_tensor(out=ga, in0=ga, in1=ps[:, 3, :], op=ALU.mult), reads=[r_ga, ps_res[3]], writes=[r_ga])
                    S.pool(lambda e, gc=gc, ga=ga, n=n: e.tensor_tensor(out=mT[:, n, :], in0=gc, in1=ga, op=ALU.add), reads=[r_gc, r_ga], writes=[r_mT[n]])
                if r0 + 512 < NM:
                    load_T(lambda s, r1=r0 + 512: x1[r1 + s * 128:r1 + (s + 1) * 128, :], 4, xin_ring, xbf_ring, xT5v, r_xT, 6)
                for s in range(4):
                    z, r_z = z_ring.next()
                    S.dma("sp", z, x1[xrow + s * 128:xrow + (s + 1) * 128, :], reads=[r_x1], writes=[r_z])
                    S.act(lambda e, z=z: e.mul(out=z, in_=z, mul=ALPHA), reads=[r_z], writes=[r_z])
                    for half in range(2):
                        b = 4 + half
                        for n in range(8):
                            S.pe(lambda e, n=n, s=s, half=half, b=b: e.matmul(ps[:, b, :], mT[:, n, s * 128:(s + 1) * 128], wo[:, n, half * 512:(half + 1) * 512], start=(n == 0), stop=(n == 7)),
                                 reads=[r_mT[n], r_wo], writes=[ps_res[b]])
                    S.dve(lambda e, z=z: e.tensor_tensor(out=z, in0=z, in1=ps[:, 4:6, :].rearrange("p a b -> p (a b)"), op=ALU.add),
                          reads=[ps_res[4], ps_res[5], r_z], writes=[r_z])
                    k = nst % 2
                    nst += 1
                    layer_norm_store((stt_t[:, k, :], st_res2[k][0], mv_t[:, k, :], st_res2[k][1]), z, r_z, gB, bB, r_gb, None)
                    S.dma("pool", x2[r0 + s * 128:r0 + (s + 1) * 128, :], z, reads=[r_z], writes=[r_x2])

        S.full_barrier()
        r_yo = Res()
        ffn_phase("f2", x2, r_x2, NM, w_ffn[1][0], w_ffn[1][1], 4, yo, r_yo)
        stats = S.emit()
    return nc, stats


def _t5_bucket_np(rel):
    nb = 16
    ret = np.where(rel > 0, nb, 0)
    n = np.abs(rel)
    max_exact = 8
    nf = np.maximum(n, 1).astype(np.float32)
    large = max_exact + (np.log(nf / np.float32(max_exact)) / np.float32(math.log(128 / max_exact)) * np.float32(nb - max_exact)).astype(np.int32)
    large = np.minimum(large, nb - 1)
    return ret + np.where(n < max_exact, n, large)


def _onehot():
    i = np.arange(LB)
    b = _t5_bucket_np(639 - i)
    oh = np.zeros((32, LB), np.float32)
    oh[b, i] = 1.0
    return oh


def make_in_maps(inputs, SP, SS, NQ, ncores):
    f = lambda a: np.ascontiguousarray(np.asarray(a, dtype=np.float32))
    xp = f(inputs["x_prompt"])
    xs = f(inputs["x_sample"])[0]
    common = {
        "ffn1_gu": f(inputs["ffn1_w_gu"][0]), "ffn1_d": f(inputs["ffn1_w_down"][0]),
        "ffn2_gu": f(inputs["ffn2_w_gu"][0]), "ffn2_d": f(inputs["ffn2_w_down"][0]),
        "w_in": f(inputs["w_in"][0]), "w_co": f(inputs["w_conv_out"][0]),
        "w_ao": f(inputs["w_attn_out"][0]), "w_o": f(inputs["w_o"][0]),
        "lnv": f(np.stack([inputs["ln1_g"][0], inputs["ln1_b"][0], inputs["ln2_g"][0], inputs["ln2_b"][0], inputs["ln3_g"][0], inputs["ln3_b"][0]])),
        "sublg": f(inputs["subln_g"]).reshape(1, 128),
        "lamv": f(np.concatenate([inputs["lambda_q1"][0], inputs["lambda_k1"][0], inputs["lambda_q2"][0], inputs["lambda_k2"][0]])).reshape(1, 256),
        "tab": f(inputs["rel_bias_table"]),
        "oh": _onehot(),
        "ident": np.eye(128, dtype=np.float32),
    }
    NCOL = 16 + 124 + 12 + 4 + 1
    colbase = np.zeros((128, NCOL), np.float32)
    colbase[:, 0:16] = f(inputs["b_gate"][0]).reshape(16, 128).T
    cw = f(inputs["conv_w_dw"][0])[:, 0, :]
    for c in range(4):
        colbase[:, 16 + c * 31:16 + (c + 1) * 31] = cw[:, c * 128:(c + 1) * 128].T
    colbase[:, 140:144] = f(inputs["conv_b_dw"][0]).reshape(4, 128).T
    colbase[:, 144:148] = f(inputs["conv_ln_g"][0]).reshape(4, 128).T
    colbase[:, 148:152] = f(inputs["conv_ln_b"][0]).reshape(4, 128).T
    colbase[:, 156] = f(inputs["subln_g"]).reshape(128)
    maps = []
    for r in range(ncores):
        col = colbase.copy()
        col[:, 152] = 1.0 if r > 0 else 0.0
        col[:, 153] = 1.0 if r < ncores - 1 else 0.0
        col[:, 154] = 1.0 if r > 0 else 0.0
        col[:, 155] = 1.0 if r < ncores - 1 else 0.0
        rot = np.roll(xs, -NQ * r, axis=0)
        wrap = ((np.arange(SS // 128) * 128 + NQ * r) >= SS).astype(np.float32)
        m = dict(common)
        m["xr"] = np.ascontiguousarray(np.concatenate([xp[r], rot], axis=0))
        m["colp"] = col
        m["wrapm"] = np.ascontiguousarray(np.broadcast_to(wrap[None, :], (128, SS // 128)))
        maps.append(m)
    return maps


_CACHE = {}


def kernel(**inputs):
    SP, SS, NQ, ncores = 8192, 16384, 2048, 8
    if "nc" not in _CACHE:
        _CACHE["nc"] = build(SP, SS, NQ)[0]
    nc = _CACHE["nc"]
    in_maps = make_in_maps(inputs, SP, SS, NQ, ncores)
    res = run_bass_kernel_spmd(nc, in_maps, core_ids=list(range(ncores)))
    y_prompt = np.stack([np.asarray(res.results[r]["yo"][:SP], dtype=np.float32) for r in range(ncores)], axis=0)
    y_sample = np.concatenate([np.asarray(res.results[r]["yo"][SP:], dtype=np.float32) for r in range(ncores)], axis=0)[None]
    return (y_prompt, y_sample)
```

```python
import contextlib
import math
import numpy as np
import concourse.bass as bass
import concourse.mybir as mybir
from concourse.bass_utils import run_bass_kernel_spmd

F32 = mybir.dt.float32
BF16 = mybir.dt.bfloat16
AF = mybir.ActivationFunctionType
ALU = mybir.AluOpType
AX = mybir.AxisListType

D = 1024
DFF = 2816
NFC = DFF // 128
PROJ = 4608
COL_Q, COL_K, COL_V, COL_G = 1024, 1536, 2048, 2560
ALPHA = 2.0 ** 0.25
LAMBDA_INIT = 0.8 - 0.6 * math.exp(0.0)
LN_EPS = 1e-5
LB = 1280
ENGS = ("pe", "act", "dve", "pool", "sp")
NDMASEM = 12


class Res:
    __slots__ = ("name", "writer", "readers")

    def __init__(self, name=""):
        self.name = name
        self.writer = None
        self.readers = {}


class Op:
    __slots__ = ("eng", "fn", "deps", "needed", "is_dma", "sem", "val")

    def __init__(self, eng, fn, is_dma):
        self.eng = eng
        self.fn = fn
        self.deps = []
        self.needed = False
        self.is_dma = is_dma
        self.sem = None
        self.val = None


class Sched:
    def __init__(self, nc):
        self.nc = nc
        self.ops = {e: [] for e in ENGS}
        self.dmas = {e: [] for e in ENGS}
        self.pending = {e: [] for e in ENGS}

    def full_barrier(self):
        deps = []
        for e in ENGS:
            comp = [o for o in self.ops[e][-1:] if not o.is_dma]
            for o in reversed(self.ops[e]):
                if not o.is_dma:
                    comp = [o]
                    break
            deps.extend(comp)
            deps.extend(self.dmas[e][-NDMASEM:])
        for o in deps:
            o.needed = True
        for e in ENGS:
            self.pending[e] = list(deps)

    def _track(self, op, reads, writes):
        deps = []
        if self.pending[op.eng]:
            deps.extend(self.pending[op.eng])
            self.pending[op.eng] = []
        for r in reads:
            if r.writer is not None:
                deps.append(r.writer)
        for w in writes:
            if w.writer is not None:
                deps.append(w.writer)
            for o in w.readers.values():
                if isinstance(o, list):
                    deps.extend(o)
                else:
                    deps.append(o)
        for r in reads:
            if op.is_dma:
                r.readers.setdefault("dma_" + op.eng, []).append(op)
            else:
                r.readers[op.eng] = op
        for w in writes:
            w.writer = op
            w.readers = {}
        keep = []
        for d in deps:
            if d is op:
                continue
            if d.eng == "pe" and op.eng == "pe" and not d.is_dma and not op.is_dma:
                continue
            keep.append(d)
            d.needed = True
        op.deps = keep

    def op(self, eng, fn, reads=(), writes=()):
        o = Op(eng, fn, False)
        self._track(o, reads, writes)
        self.ops[eng].append(o)
        return o

    def dma(self, eng, out, in_, reads=(), writes=(), **kw):
        o = Op(eng, (lambda e, out=out, in_=in_, kw=kw: e.dma_start(out=out, in_=in_, **kw)), True)
        o.needed = True
        self._track(o, reads, writes)
        self.ops[eng].append(o)
        self.dmas[eng].append(o)
        return o

    def pe(self, fn, reads=(), writes=()):
        return self.op("pe", fn, reads, writes)

    def act(self, fn, reads=(), writes=()):
        return self.op("act", fn, reads, writes)

    def dve(self, fn, reads=(), writes=()):
        return self.op("dve", fn, reads, writes)

    def pool(self, fn, reads=(), writes=()):
        return self.op("pool", fn, reads, writes)

    def emit(self):
        nc = self.nc
        csem = {e: nc.alloc_semaphore(name=f"c_{e}") for e in ENGS}
        dsem = {e: [nc.alloc_semaphore(name=f"d_{e}_{j}") for j in range(NDMASEM)] for e in ("sp", "pool")}
        last_on = {}
        for e in ENGS:
            n = 0
            nd = 0
            for o in self.ops[e]:
                if o.is_dma:
                    j = nd % NDMASEM
                    o.sem = dsem[e][j]
                    o.val = 16 * (nd // NDMASEM + 1)
                    if (e, j) in last_on:
                        o.deps.append(last_on[(e, j)])
                    last_on[(e, j)] = o
                    nd += 1
                elif o.needed:
                    n += 1
                    o.sem = csem[e]
                    o.val = n
        finals = list(last_on.values())
        stats = {}
        with nc.Block() as block:
            def body(e):
                def run(eng):
                    seen = {}
                    nw = 0
                    for o in self.ops[e]:
                        for d in o.deps:
                            k = id(d.sem)
                            if seen.get(k, 0) < d.val:
                                eng.wait_ge(d.sem, d.val)
                                seen[k] = d.val
                                nw += 1
                        ins = o.fn(eng)
                        if o.is_dma:
                            ins.then_inc(o.sem, 16)
                        elif o.needed:
                            ins.then_inc(o.sem, 1)
                    if e == "sp":
                        for o in finals:
                            k = id(o.sem)
                            if seen.get(k, 0) < o.val:
                                eng.wait_ge(o.sem, o.val)
                                seen[k] = o.val
                    stats[e] = (len(self.ops[e]), nw)
                return run
            block.tensor(body("pe"))
            block.scalar(body("act"))
            block.vector(body("dve"))
            block.gpsimd(body("pool"))
            block.sync(body("sp"))
        return stats


class Ring:
    def __init__(self, views):
        self.views = views
        self.res = [Res() for _ in views]
        self.i = 0

    def next(self):
        k = self.i % len(self.views)
        self.i += 1
        return self.views[k], self.res[k]


def build(SP, SS, NQ, debug=False):
    assert SP % 512 == 0 and SS % 512 == 0 and NQ % 512 == 0 and SS >= NQ + 1024
    R = SP + SS
    NM = SP + NQ
    NBP, NBS = SP // 128, SS // 128
    HB = 32 + SP
    HC = HB + NQ + 256
    nc = bass.Bass("TRN2", target_bir_lowering=False)
    S = Sched(nc)

    def din(name, shape):
        return nc.dram_tensor(name, shape, F32, kind="ExternalInput").ap()

    def dscr(name, shape, dt):
        if debug:
            return nc.dram_tensor(name, shape, dt, kind="ExternalOutput").ap()
        return nc.dram_tensor(name, shape, dt).ap()

    xr = din("xr", [R, D])
    w_ffn = [(din("ffn1_gu", [D, 2 * DFF]), din("ffn1_d", [DFF, D])),
             (din("ffn2_gu", [D, 2 * DFF]), din("ffn2_d", [DFF, D]))]
    w_in = din("w_in", [D, PROJ])
    w_co = din("w_co", [512, D])
    w_ao = din("w_ao", [512, D])
    w_o = din("w_o", [D, D])
    lnv = din("lnv", [6, D])
    NCOL = 16 + 124 + 12 + 4 + 1
    colp = din("colp", [128, NCOL])
    wrapm = din("wrapm", [128, NBS])
    sublg = din("sublg", [1, 128])
    lamv = din("lamv", [1, 256])
    tab = din("tab", [32, 4])
    oh = din("oh", [32, LB])
    ident = din("ident", [128, 128])
    yo = nc.dram_tensor("yo", [NM, D], F32, kind="ExternalOutput").ap()

    x1 = dscr("x1", [R, D], F32)
    hT = dscr("hT", [512, HC], F32)
    QT = dscr("QT", [4, 128, NM], BF16)
    KT = dscr("KT", [4, 128, R], BF16)
    Vs = dscr("Vs", [R, 4 * 129], BF16)
    brep = dscr("brep", [4, 128, LB], F32)
    cactT = dscr("cactT", [512, NM], BF16)
    attnT = dscr("attnT", [512, NM], BF16)
    x2 = dscr("x2", [NM, D], F32)
    r_x1, r_hT, r_QT, r_KT, r_Vs, r_brep, r_cact, r_attn, r_x2 = (Res() for _ in range(9))

    def dram_ap(t, offset, pattern):
        return bass.AP(t.tensor, offset, pattern)

    def bcast_rows(src, row, n, parts=128):
        return dram_ap(src, row * src.shape[1], [[0, parts], [1, n]])

    with contextlib.ExitStack() as top:
        def SB(st, name, shape, dt):
            return st.enter_context(nc.sbuf_tensor(name, shape, dt))

        ps = top.enter_context(nc.psum_tensor("ps", [128, 7, 512], F32))
        ps_res = [Res(f"ps{b}") for b in range(7)]
        r_pstT = Res("pstT")
        PSX = {}

        def alloc_pst(st, tag):
            PSX["t"] = st.enter_context(nc.psum_tensor("pstT" + tag, [128, 1024], BF16))

        idf = SB(top, "idf", [128, 128], F32)
        idb = SB(top, "idb", [128, 128], BF16)
        colt = SB(top, "colt", [128, NCOL], F32)
        lam = SB(top, "lam", [128, 8], F32)
        r_idf, r_idb, r_colt, r_lam = Res(), Res(), Res(), Res()
        S.dma("sp", idf[:], ident[:, :], writes=[r_idf])
        S.dma("sp", colt[:], colp[:, :], writes=[r_colt])
        S.dve(lambda e: e.tensor_copy(out=idb[:], in_=idf[:]), reads=[r_idf], writes=[r_idb])
        C_BG, C_CW, C_CB, C_CG, C_CBE, C_MK, C_SG = 0, 16, 140, 144, 148, 152, 156

        def colv(c):
            return colt[:, c:c + 1]

        with contextlib.ExitStack() as st:
            lv = SB(st, "lv", [128, 256], F32)
            lp = SB(st, "lp", [128, 128], F32)
            r_lv, r_lp = Res(), Res()
            S.dma("sp", lv[:], bcast_rows(lamv, 0, 256), writes=[r_lv])
            S.dve(lambda e: e.tensor_tensor(out=lp[:, 0:64], in0=lv[:, 0:64], in1=lv[:, 64:128], op=ALU.mult), reads=[r_lv], writes=[r_lp])
            S.dve(lambda e: e.tensor_tensor(out=lp[:, 64:128], in0=lv[:, 128:192], in1=lv[:, 192:256], op=ALU.mult), reads=[r_lv, r_lp], writes=[r_lp])
            S.dve(lambda e: e.reduce_sum(out=lam[:, 0:1], in_=lp[:, 0:64], axis=AX.X), reads=[r_lp], writes=[r_lam])
            S.dve(lambda e: e.reduce_sum(out=lam[:, 1:2], in_=lp[:, 64:128], axis=AX.X), reads=[r_lp, r_lam], writes=[r_lam])
            S.act(lambda e: e.activation(out=lam[:, 2:4], in_=lam[:, 0:2], func=AF.Exp), reads=[r_lam], writes=[r_lam])
            S.dve(lambda e: e.tensor_tensor(out=lam[:, 4:5], in0=lam[:, 3:4], in1=lam[:, 2:3], op=ALU.subtract), reads=[r_lam], writes=[r_lam])
            S.dve(lambda e: e.tensor_scalar_add(out=lam[:, 5:6], in0=lam[:, 4:5], scalar1=-LAMBDA_INIT), reads=[r_lam], writes=[r_lam])
            tb = SB(st, "tb", [32, 4, 128], F32)
            oht = SB(st, "oht", [32, LB], F32)
            gb = SB(st, "gb", [128, LB], F32)
            r_tb, r_oht, r_gb = Res(), Res(), Res()
            S.dma("sp", oht[:], oh[:, :], writes=[r_oht])
            tabt = SB(st, "tabt", [32, 4], F32)
            r_tabt = Res()
            S.dma("sp", tabt[:], tab[:, :], writes=[r_tabt])
            S.pool(lambda e: e.memset(tb[:], 1.0), writes=[r_tb])
            for h in range(4):
                S.dve(lambda e, h=h: e.tensor_scalar_mul(out=tb[:, h, :], in0=tb[:, h, :], scalar1=tabt[:, h:h + 1]), reads=[r_tb, r_tabt], writes=[r_tb])
            for h in range(4):
                for (c0, c1) in ((0, 512), (512, 1024), (1024, LB)):
                    S.pe(lambda e, h=h, c0=c0, c1=c1: e.matmul(ps[:, 0, 0:c1 - c0], tb[:, h, :], oht[:, c0:c1], start=True, stop=True),
                         reads=[r_tb, r_oht], writes=[ps_res[0]])
                    S.dve(lambda e, c0=c0, c1=c1: e.tensor_copy(out=gb[:, c0:c1], in_=ps[:, 0, 0:c1 - c0]), reads=[ps_res[0]], writes=[r_gb])
                S.dma("sp", brep[h], gb[:], reads=[r_gb], writes=[r_brep])

        S.full_barrier()

        def load_weight(st, name, src, K, N, queue="pool"):
            kc = K // 128
            t = SB(st, name, [128, kc, N], BF16)
            r = Res(name)
            for c in range(kc):
                S.dma(queue, t[:, c, :], src[c * 128:(c + 1) * 128, :], writes=[r])
            return t, r

        def load_T(rows_ap_fn, nsub, xin_ring, xbf_ring, xT, r_xT, tbank, r_src=None):
            for s in range(nsub):
                xin, r_xin = xin_ring.next()
                xbf, r_xbf = xbf_ring.next()
                S.dma("sp", xin, rows_ap_fn(s), writes=[r_xin])
                S.act(lambda e, xbf=xbf, xin=xin: e.copy(out=xbf, in_=xin), reads=[r_xin], writes=[r_xbf])
                pst = PSX["t"][:, :]
                for c in range(8):
                    S.pe(lambda e, c=c, xbf=xbf, pst=pst: e.transpose(pst[:, c * 128:(c + 1) * 128], xbf[:, c * 128:(c + 1) * 128], idb[:]),
                         reads=[r_xbf, r_idb], writes=[r_pstT])
                S.dve(lambda e, s=s, pst=pst: e.tensor_copy(out=xT[:, :, s * 128:(s + 1) * 128],
                                                           in_=pst.rearrange("p (c t) -> p c t", c=8)),
                      reads=[r_pstT], writes=[r_xT])

        def layer_norm_store(st_tiles, z, r_z, gB, bB, r_gb, dst_ap):
            stt, r_stt, mv, r_mv = st_tiles
            S.dve(lambda e: e.bn_stats(out=stt[:, 0:6], in_=z[:, 0:512]), reads=[r_z], writes=[r_stt])
            S.dve(lambda e: e.bn_stats(out=stt[:, 6:12], in_=z[:, 512:1024]), reads=[r_z, r_stt], writes=[r_stt])
            S.dve(lambda e: e.bn_aggr(out=mv[:, 0:2], in_=stt[:, 0:12]), reads=[r_stt], writes=[r_mv])
            S.act(lambda e: e.activation(out=mv[:, 2:3], in_=mv[:, 1:2], func=AF.Sqrt, bias=epsc[:, 0:1], scale=1.0), reads=[r_mv, r_eps], writes=[r_mv])
            S.dve(lambda e: e.reciprocal(out=mv[:, 3:4], in_=mv[:, 2:3]), reads=[r_mv], writes=[r_mv])
            S.dve(lambda e: e.tensor_scalar(out=z, in0=z, scalar1=mv[:, 0:1], scalar2=mv[:, 3:4], op0=ALU.subtract, op1=ALU.mult),
                  reads=[r_z, r_mv], writes=[r_z])
            S.pool(lambda e: e.tensor_tensor(out=z, in0=z, in1=gB[:], op=ALU.mult), reads=[r_z, r_gb], writes=[r_z])
            S.pool(lambda e: e.tensor_tensor(out=z, in0=z, in1=bB[:], op=ALU.add), reads=[r_z, r_gb], writes=[r_z])

        cbt = SB(top, "cbt", [128, 12], F32)
        r_cbt = Res()
        S.dma("sp", cbt[:, 0:4], bcast_rows(tab, 15, 4), writes=[r_cbt])
        S.dma("sp", cbt[:, 4:8], bcast_rows(tab, 31, 4), writes=[r_cbt])
        S.dve(lambda e: e.tensor_tensor(out=cbt[:, 8:12], in0=cbt[:, 0:4], in1=cbt[:, 4:8], op=ALU.subtract), reads=[r_cbt], writes=[r_cbt])
        epsc = SB(top, "epsc", [128, 1], F32)
        r_eps = Res()
        S.pool(lambda e: e.memset(epsc[:], LN_EPS), writes=[r_eps])

        def ffn_phase(tag, src, r_src, nrows, wgu_d, wd_d, ln_row, dst, r_dst):
            with contextlib.ExitStack() as st:
                alloc_pst(st, tag)
                wgu, r_wgu = load_weight(st, "wgu" + tag, wgu_d, D, 2 * DFF)
                wd, r_wd = load_weight(st, "wd" + tag, wd_d, DFF, D)
                gB = SB(st, "gB" + tag, [128, D], F32)
                bB = SB(st, "bB" + tag, [128, D], F32)
                r_gb = Res()
                S.dma("sp", gB[:], bcast_rows(lnv, ln_row, D), writes=[r_gb])
                S.dma("sp", bB[:], bcast_rows(lnv, ln_row + 1, D), writes=[r_gb])
                xin_t = SB(st, "xin" + tag, [128, 2, D], F32)
                xbf_t = SB(st, "xbf" + tag, [128, 2, D], BF16)
                xin_ring = Ring([xin_t[:, k, :] for k in range(2)])
                xbf_ring = Ring([xbf_t[:, k, :] for k in range(2)])
                xT = SB(st, "xT" + tag, [128, 8, 512], BF16)
                r_xT = Res()
                actT = SB(st, "actT" + tag, [128, NFC, 512], BF16)
                r_act = [Res() for _ in range(NFC)]
                sa_t = SB(st, "sa" + tag, [128, 2, 512], F32)
                sa_ring = Ring([sa_t[:, k, :] for k in range(2)])
                z_t = SB(st, "z" + tag, [128, 3, D], F32)
                z_ring = Ring([z_t[:, k, :] for k in range(3)])
                stt_t = SB(st, "stt" + tag, [128, 2, 12], F32)
                mv_t = SB(st, "mv" + tag, [128, 2, 4], F32)
                st_ring = Ring([(stt_t[:, k, :], mv_t[:, k, :]) for k in range(2)])
                st_res2 = [(Res(), Res()) for _ in range(2)]
                ntile = nrows // 512
                load_T(lambda s: src[s * 128:(s + 1) * 128, :], 4, xin_ring, xbf_ring, xT, r_xT, 6)
                for t in range(ntile):
                    r0 = t * 512
                    for f in range(NFC):
                        ba, bg = (0, 1) if f % 2 == 0 else (2, 3)
                        for dd in range(8):
                            S.pe(lambda e, f=f, dd=dd, ba=ba: e.matmul(ps[:, ba, :], wgu[:, dd, f * 128:(f + 1) * 128], xT[:, dd, :], start=(dd == 0), stop=(dd == 7)),
                                 reads=[r_wgu, r_xT], writes=[ps_res[ba]])
                        for dd in range(8):
                            S.pe(lambda e, f=f, dd=dd, bg=bg: e.matmul(ps[:, bg, :], wgu[:, dd, DFF + f * 128:DFF + (f + 1) * 128], xT[:, dd, :], start=(dd == 0), stop=(dd == 7)),
                                 reads=[r_wgu, r_xT], writes=[ps_res[bg]])
                        sa, r_sa = sa_ring.next()
                        S.act(lambda e, sa=sa, ba=ba: e.activation(out=sa, in_=ps[:, ba, :], func=AF.Silu), reads=[ps_res[ba]], writes=[r_sa])
                        S.dve(lambda e, sa=sa, bg=bg, f=f: e.tensor_tensor(out=actT[:, f, :], in0=sa, in1=ps[:, bg, :], op=ALU.mult),
                              reads=[r_sa, ps_res[bg]], writes=[r_act[f]])
                    if t + 1 < ntile:
                        load_T(lambda s, r1=r0 + 512: src[r1 + s * 128:r1 + (s + 1) * 128, :], 4, xin_ring, xbf_ring, xT, r_xT, 6)
                    for s in range(4):
                        z, r_z = z_ring.next()
                        rows = slice(r0 + s * 128, r0 + (s + 1) * 128)
                        S.dma("sp", z, src[rows, :], reads=[r_src], writes=[r_z])
                        S.act(lambda e, z=z: e.mul(out=z, in_=z, mul=ALPHA), reads=[r_z], writes=[r_z])
                        for half in range(2):
                            b = 4 + half
                            for f in range(NFC):
                                S.pe(lambda e, f=f, s=s, half=half, b=b: e.matmul(ps[:, b, :], actT[:, f, s * 128:(s + 1) * 128], wd[:, f, half * 512:(half + 1) * 512], start=(f == 0), stop=(f == NFC - 1)),
                                     reads=[r_act[f], r_wd], writes=[ps_res[b]])
                        S.dve(lambda e, z=z: e.scalar_tensor_tensor(out=z, in0=ps[:, 4:6, :].rearrange("p a b -> p (a b)"), scalar=0.5, in1=z, op0=ALU.mult, op1=ALU.add),
                              reads=[ps_res[4], ps_res[5], r_z], writes=[r_z])
                        (stt, mv), _ = st_ring.next()
                        k = (st_ring.i - 1) % 2
                        layer_norm_store((stt, st_res2[k][0], mv, st_res2[k][1]), z, r_z, gB, bB, r_gb, None)
                        S.dma("pool", dst[rows, :], z, reads=[r_z], writes=[r_dst])

        r_xr = Res()
        ffn_phase("f1", xr, r_xr, R, w_ffn[0][0], w_ffn[0][1], 0, x1, r_x1)

        S.full_barrier()
        with contextlib.ExitStack() as st:
            alloc_pst(st, "p2")
            win, r_win = load_weight(st, "win", w_in[:, 0:COL_G], D, COL_G)
            xin_t = SB(st, "xin2", [128, 2, D], F32)
            xbf_t = SB(st, "xbf2", [128, 2, D], BF16)
            xin_ring = Ring([xin_t[:, k, :] for k in range(2)])
            xbf_ring = Ring([xbf_t[:, k, :] for k in range(2)])
            xT2_t = SB(st, "xT2", [128, 2, 8, 512], BF16)
            xT2_ring = Ring([xT2_t[:, k, :, :] for k in range(2)])
            sg_t = SB(st, "sg2", [128, 2, 512], F32)
            sg_ring = Ring([sg_t[:, k, :] for k in range(2)])
            ho_t = SB(st, "ho2", [128, 3, 512], F32)
            ho_ring = Ring([ho_t[:, k, :] for k in range(3)])
            qk_t = SB(st, "qk2", [128, 3, 512], BF16)
            qk_ring = Ring([qk_t[:, k, :] for k in range(3)])
            v_t = SB(st, "v2", [128, 2, 4, 129], BF16)
            v_ring = Ring([v_t[:, k, :, :] for k in range(2)])
            zt = SB(st, "zero2", [128, 16], F32)
            r_zt = Res()
            S.pool(lambda e: e.memset(zt[:], 0.0), writes=[r_zt])
            S.pool(lambda e: e.memset(v_t[:], 1.0), writes=v_ring.res)
            for c in range(4):
                S.dma("sp", hT[c * 128:(c + 1) * 128, 0:16], zt[:], reads=[r_zt], writes=[r_hT])
                S.dma("sp", hT[c * 128:(c + 1) * 128, 16 + SP:32 + SP], zt[:], reads=[r_zt], writes=[r_hT])
            xT_next = xT2_ring.next()
            load_T(lambda s: x1[s * 128:(s + 1) * 128, :], 4, xin_ring, xbf_ring, xT_next[0], xT_next[1], 6)
            for t in range(R // 512):
                r0 = t * 512
                xT, r_xT = xT_next
                if r0 + 512 < R:
                    xT_next = xT2_ring.next()
                    load_T(lambda s, r1=r0 + 512: x1[r1 + s * 128:r1 + (s + 1) * 128, :], 4, xin_ring, xbf_ring, xT_next[0], xT_next[1], 6)
                samp = r0 >= SP
                rs = r0 - SP
                need_q = (not samp) or rs < NQ
                if not samp:
                    need_u, hcol, ucols, mask = True, 16 + r0, (0, 512), None
                elif rs < NQ:
                    need_u, hcol, ucols, mask = True, HB + 128 + rs, (0, 512), None
                elif rs == NQ:
                    need_u, hcol, ucols, mask = True, HB + 128 + NQ, (0, 128), C_MK + 1
                elif rs == SS - 512:
                    need_u, hcol, ucols, mask = True, HB - 384, (384, 512), C_MK + 0
                else:
                    need_u = False
                if False:
                    dxT = nc.dram_tensor("dbg_xT", [128, 8 * 512], BF16, kind="ExternalOutput").ap()
                    dwin = nc.dram_tensor("dbg_win", [128, 2560], BF16, kind="ExternalOutput").ap()
                    dxin = nc.dram_tensor("dbg_xin", [128, 1024], F32, kind="ExternalOutput").ap()
                    S.dma("sp", dxT[:, :], xT[:].rearrange("p c t -> p (c t)"), reads=[r_xT])
                    S.dma("sp", dwin[:, :], win[:, 0, :], reads=[r_win])
                    S.dma("sp", dxin[:, :], xin_t[:, 1, :], reads=[xin_ring.res[1]])
                nb = [0]

                def bank():
                    b = nb[0] % 4
                    nb[0] += 1
                    return b

                def proj_fm(col0):
                    b = bank()
                    for dd in range(8):
                        S.pe(lambda e, dd=dd, b=b, col0=col0, xT=xT: e.matmul(ps[:, b, :], win[:, dd, col0:col0 + 128], xT[:, dd, :], start=(dd == 0), stop=(dd == 7)),
                             reads=[r_win, r_xT], writes=[ps_res[b]])
                    return b

                if need_u:
                    for j in range(4):
                        ba = proj_fm(j * 128)
                        bg = proj_fm(512 + j * 128)
                        sg, r_sg = sg_ring.next()
                        ho, r_ho = ho_ring.next()
                        S.act(lambda e, sg=sg, bg=bg: e.activation(out=sg, in_=ps[:, bg, :], func=AF.Sigmoid), reads=[ps_res[bg]], writes=[r_sg])
                        S.dve(lambda e, sg=sg, ho=ho, ba=ba: e.tensor_tensor(out=ho, in0=sg, in1=ps[:, ba, :], op=ALU.mult), reads=[r_sg, ps_res[ba]], writes=[r_ho])
                        if mask is not None:
                            S.dve(lambda e, ho=ho, mask=mask: e.tensor_scalar_mul(out=ho, in0=ho, scalar1=colv(mask)), reads=[r_ho, r_colt], writes=[r_ho])
                        u0, u1 = ucols
                        S.dma("pool", hT[j * 128:(j + 1) * 128, hcol + u0:hcol + u1], ho[:, u0:u1], reads=[r_ho], writes=[r_hT])
                for (need, col0, dstT, r_d, cbase) in ((need_q, COL_Q, QT, r_QT, (r0 if not samp else SP + rs)), (True, COL_K, KT, r_KT, r0)):
                    if not need:
                        continue
                    for h in range(4):
                        b = proj_fm(col0 + h * 128)
                        qk, r_qk = qk_ring.next()
                        S.act(lambda e, qk=qk, b=b: e.copy(out=qk, in_=ps[:, b, :]), reads=[ps_res[b]], writes=[r_qk])
                        S.dma("pool", dstT[h, :, cbase:cbase + 512], qk, reads=[r_qk], writes=[r_d])
                for s in range(4):
                    b = bank()
                    for dd in range(8):
                        S.pe(lambda e, dd=dd, b=b, s=s, xT=xT: e.matmul(ps[:, b, :], xT[:, dd, s * 128:(s + 1) * 128], win[:, dd, COL_V:COL_V + 512], start=(dd == 0), stop=(dd == 7)),
                             reads=[r_win, r_xT], writes=[ps_res[b]])
                    vt, r_vt = v_ring.next()
                    S.dve(lambda e, vt=vt, b=b: e.tensor_copy(out=vt[:, :, 0:128], in_=ps[:, b, :].rearrange("p (h e) -> p h e", h=4)), reads=[ps_res[b]], writes=[r_vt])
                    S.dma("pool", Vs[r0 + s * 128:r0 + (s + 1) * 128, :], vt.rearrange("p h e -> p (h e)"), reads=[r_vt], writes=[r_Vs])

        S.full_barrier()
        with contextlib.ExitStack() as st:
            hin_t = SB(st, "hin", [128, 2, 4, 544], F32)
            hin_ring = Ring([hin_t[:, k, :, :] for k in range(2)])
            acc = SB(st, "cacc", [128, 4, 512], F32)
            r_acc = [Res() for _ in range(4)]
            sq = SB(st, "csq", [128, 4, 512], F32)
            r_sq = [Res() for _ in range(4)]
            onesm = SB(st, "onesm", [128, 128], F32)
            r_ones = Res()
            S.pool(lambda e: e.memset(onesm[:], 1.0 / 512.0), writes=[r_ones])
            mean = SB(st, "cmean", [128, 512], F32)
            rstd = SB(st, "crstd", [128, 512], F32)
            r_mean, r_rstd = Res(), Res()
            co_t = SB(st, "cout", [128, 3, 512], BF16)
            co_ring = Ring([co_t[:, k, :] for k in range(3)])
            tiles = [(16 + t * 512, t * 512) for t in range(SP // 512)] + [(HB + 128 + t * 512, SP + t * 512) for t in range(NQ // 512)]
            for (hc, mrow) in tiles:
                hin, r_hin = hin_ring.next()
                for c in range(4):
                    S.dma("sp", hin[:, c, 0:542], hT[c * 128:(c + 1) * 128, hc - 15:hc + 527], reads=[r_hT], writes=[r_hin])
                for j in range(31):
                    for c in range(4):
                        if j == 0:
                            S.dve(lambda e, c=c, hin=hin: e.tensor_scalar(out=acc[:, c, :], in0=hin[:, c, 0:512], scalar1=colv(C_CW + c * 31), scalar2=colv(C_CB + c), op0=ALU.mult, op1=ALU.add),
                                  reads=[r_hin, r_colt], writes=[r_acc[c]])
                        else:
                            S.dve(lambda e, c=c, j=j, hin=hin: e.scalar_tensor_tensor(out=acc[:, c, :], in0=hin[:, c, j:j + 512], scalar=colv(C_CW + c * 31 + j), in1=acc[:, c, :], op0=ALU.mult, op1=ALU.add),
                                  reads=[r_hin, r_colt, r_acc[c]], writes=[r_acc[c]])
                for c in range(4):
                    S.pool(lambda e, c=c: e.tensor_tensor(out=sq[:, c, :], in0=acc[:, c, :], in1=acc[:, c, :], op=ALU.mult), reads=[r_acc[c]], writes=[r_sq[c]])
                for c in range(4):
                    S.pe(lambda e, c=c: e.matmul(ps[:, 0, :], onesm[:], acc[:, c, :], start=(c == 0), stop=(c == 3)), reads=[r_ones, r_acc[c]], writes=[ps_res[0]])
                for c in range(4):
                    S.pe(lambda e, c=c: e.matmul(ps[:, 1, :], onesm[:], sq[:, c, :], start=(c == 0), stop=(c == 3)), reads=[r_ones, r_sq[c]], writes=[ps_res[1]])
                S.act(lambda e: e.copy(out=mean[:], in_=ps[:, 0, :]), reads=[ps_res[0]], writes=[r_mean])
                S.pool(lambda e: e.tensor_tensor(out=rstd[:], in0=mean[:], in1=mean[:], op=ALU.mult), reads=[r_mean], writes=[r_rstd])
                S.dve(lambda e: e.tensor_tensor(out=rstd[:], in0=ps[:, 1, :], in1=rstd[:], op=ALU.subtract), reads=[ps_res[1], r_rstd], writes=[r_rstd])
                S.act(lambda e: e.activation(out=rstd[:], in_=rstd[:], func=AF.Sqrt, bias=epsc[:, 0:1], scale=1.0), reads=[r_rstd, r_eps], writes=[r_rstd])
                S.dve(lambda e: e.reciprocal(out=rstd[:], in_=rstd[:]), reads=[r_rstd], writes=[r_rstd])
                for c in range(4):
                    S.pool(lambda e, c=c: e.tensor_tensor(out=acc[:, c, :], in0=acc[:, c, :], in1=mean[:], op=ALU.subtract), reads=[r_acc[c], r_mean], writes=[r_acc[c]])
                    S.dve(lambda e, c=c: e.tensor_tensor(out=acc[:, c, :], in0=acc[:, c, :], in1=rstd[:], op=ALU.mult), reads=[r_acc[c], r_rstd], writes=[r_acc[c]])
                    co, r_co = co_ring.next()
                    S.act(lambda e, c=c, co=co: e.activation(out=co, in_=acc[:, c, :], func=AF.Silu, bias=colv(C_CBE + c), scale=colv(C_CG + c)), reads=[r_acc[c], r_colt], writes=[r_co])
                    S.dma("pool", cactT[c * 128:(c + 1) * 128, mrow:mrow + 512], co, reads=[r_co], writes=[r_cact])

        S.full_barrier()
        with contextlib.ExitStack() as st:
            KMAX = max(SP, SS)
            kt_t = SB(st, "kt", [128, KMAX], BF16)
            vh_t = SB(st, "vh", [128, KMAX // 128, 128], BF16)
            r_kt, r_vh = Res(), Res()
            bt = SB(st, "bt", [128, 8, 512], F32)
            r_bt = Res()
            farb = SB(st, "farb", [128, NBS], F32)
            r_farb = Res()
            wm = SB(st, "wm", [128, NBS], F32)
            r_wm = Res()
            S.dma("sp", wm[:], wrapm[:, :], writes=[r_wm])
            gcol = SB(st, "gcol", [128, 1], F32)
            r_gcol = Res()
            S.dve(lambda e: e.tensor_scalar_mul(out=gcol[:], in0=colv(C_SG), scalar1=1.0 - LAMBDA_INIT), reads=[r_colt], writes=[r_gcol])
            ps7 = st.enter_context(nc.psum_tensor("ps7", [128, 512], F32))
            r_ps7 = Res()
            onesF = SB(st, "onesF", [128, 128], F32)
            onesB = SB(st, "onesB", [128, 128], BF16)
            r_onesF = Res()
            S.pool(lambda e: e.memset(onesF[:], 1.0), writes=[r_onesF])
            S.pool(lambda e: e.memset(onesB[:], 1.0), writes=[r_onesF])
            qt_t = SB(st, "qt", [128, 2, 512], BF16)
            qt_ring = Ring([qt_t[:, k, :] for k in range(2)])
            pt_t = SB(st, "pt", [128, 4, 2, 512], BF16)
            pt_ring = Ring([pt_t[:, k, :, :] for k in range(4)])
            tmp_t = SB(st, "stmp", [128, 2, 2, 512], F32)
            tmp_ring = Ring([tmp_t[:, k, :, :] for k in range(2)])
            dacc = SB(st, "dacc", [128, 2, 2, 512], F32)
            r_dacc = [Res(), Res()]
            rec = SB(st, "rec", [128, 2, 512], F32)
            r_rec = [Res(), Res()]
            osb = SB(st, "osb", [128, 2, 512], F32)
            r_osb = [Res(), Res()]
            aa_ = SB(st, "a4", [128, 512], F32)
            bb_ = SB(st, "b4", [128, 512], F32)
            sq_ = SB(st, "sq4", [128, 512], F32)
            rs_ = SB(st, "rs4", [128, 512], F32)
            r_aa, r_bb, r_sq4, r_rs = Res(), Res(), Res(), Res()
            at_t = SB(st, "at", [128, 2, 512], BF16)
            at_ring = Ring([at_t[:, k, :] for k in range(2)])
            units = [(0, SP, SP, 0, False, h) for h in range(4)] + [(SP, SS, NQ, SP, True, h) for h in range(4)]
            for (krow0, S_k, nq, qcol0, samp, h) in units:
                NB = S_k // 128
                nqt = nq // 512
                S.dma("sp", kt_t[:, 0:S_k], KT[h, :, krow0:krow0 + S_k], reads=[r_KT], writes=[r_kt])
                nvc = max(1, NB // 16)
                for vc in range(nvc):
                    b0, b1 = vc * NB // nvc, (vc + 1) * NB // nvc
                    S.dma("sp", vh_t[:, b0:b1, :],
                          dram_ap(Vs, (krow0 + b0 * 128) * 516 + h * 129, [[516, 128], [128 * 516, b1 - b0], [1, 128]]),
                          reads=[r_Vs], writes=[r_vh])
                for kk in range(6):
                    S.dma("sp", bt[:, kk, :], dram_ap(brep, h * 128 * LB + 639 - 128 * (kk - 1), [[LB - 1, 128], [1, 512]]), reads=[r_brep], writes=[r_bt])
                cneg, cpos, cdif = cbt[:, h:h + 1], cbt[:, 4 + h:5 + h], cbt[:, 8 + h:9 + h]
                if samp:
                    S.dve(lambda e, cdif=cdif, cpos=cpos: e.tensor_scalar(out=farb[:], in0=wm[:], scalar1=cdif, scalar2=cpos, op0=ALU.mult, op1=ALU.add), reads=[r_wm, r_cbt], writes=[r_farb])
                    for (dst, srck, cc, mcol) in ((6, 0, cpos, C_MK + 2), (7, 5, cneg, C_MK + 3)):
                        S.dve(lambda e, dst=dst, srck=srck, cc=cc, mcol=mcol: e.tensor_scalar(out=bt[:, dst, :], in0=bt[:, srck, :], scalar1=cc, scalar2=colv(mcol), op0=ALU.subtract, op1=ALU.mult),
                              reads=[r_bt, r_cbt, r_colt], writes=[r_bt])
                        S.dve(lambda e, dst=dst, cc=cc: e.tensor_scalar_add(out=bt[:, dst, :], in0=bt[:, dst, :], scalar1=cc), reads=[r_bt, r_cbt], writes=[r_bt])
                for i in range(nqt):
                    qt, r_qt = qt_ring.next()
                    S.dma("sp", qt, QT[h, :, qcol0 + i * 512:qcol0 + (i + 1) * 512], reads=[r_QT], writes=[r_qt])

                    def kind(j):
                        rel = j - 4 * i
                        if samp and i == 0 and j == NB - 1:
                            return ("near", 6)
                        if samp and i == nqt - 1 and rel == 4:
                            return ("near", 7)
                        if -1 <= rel <= 4:
                            return ("near", rel + 1)
                        if rel < -1:
                            return ("far", cneg, r_cbt)
                        if samp:
                            return ("far", farb[:, j:j + 1], r_farb)
                        return ("far", cpos, r_cbt)

                    pend = {}

                    def qk(j):
                        p = 2 * (j % 2)
                        for m in range(2):
                            S.pe(lambda e, j=j, m=m, p=p, qt=qt: e.matmul(ps[:, p + m, :], kt_t[m * 64:(m + 1) * 64, j * 128:(j + 1) * 128], qt[m * 64:(m + 1) * 64, :], start=True, stop=True),
                                 reads=[r_kt, r_qt], writes=[ps_res[p + m]])
                        kd = kind(j)
                        pt, r_pt = pt_ring.next()
                        if kd[0] == "near":
                            tmp, r_tmp = tmp_ring.next()
                            for m in range(2):
                                S.dve(lambda e, m=m, p=p, tmp=tmp, kk=kd[1]: e.scalar_tensor_tensor(out=tmp[:, m, :], in0=ps[:, p + m, :], scalar=0.125, in1=bt[:, kk, :], op0=ALU.mult, op1=ALU.add),
                                      reads=[ps_res[p + m], r_bt, r_tmp], writes=[r_tmp])
                            S.act(lambda e, pt=pt, tmp=tmp: e.activation(out=pt, in_=tmp, func=AF.Exp), reads=[r_tmp], writes=[r_pt])
                        else:
                            S.act(lambda e, pt=pt, p=p, bc=kd[1]: e.activation(out=pt, in_=ps[:, p:p + 2, :], func=AF.Exp, bias=bc, scale=0.125),
                                  reads=[ps_res[p], ps_res[p + 1], kd[2]], writes=[r_pt])
                        pend[j] = (pt, r_pt)

                    def avmm(j):
                        pt, r_pt = pend.pop(j)
                        for m in range(2):
                            S.pe(lambda e, j=j, m=m, pt=pt, st_=(j == 0), sp_=(j == NB - 1): e.matmul(ps[:, 4 + m, :], vh_t[:, j, :], pt[:, m, :], start=st_, stop=sp_),
                                 reads=[r_pt, r_vh], writes=[ps_res[4 + m]])
                        if j % 2 == 0:
                            if j == 0:
                                S.dve(lambda e, pt=pt: e.tensor_copy(out=dacc[:, 0, :, :], in_=pt), reads=[r_pt], writes=[r_dacc[0]])
                            else:
                                S.dve(lambda e, pt=pt: e.tensor_tensor(out=dacc[:, 0, :, :], in0=dacc[:, 0, :, :], in1=pt, op=ALU.add), reads=[r_pt, r_dacc[0]], writes=[r_dacc[0]])
                        else:
                            for m in range(2):
                                dst = ps[:, 6, :] if m == 0 else ps7[:, :]
                                S.pe(lambda e, m=m, pt=pt, dst=dst, st_=(j == 1): e.matmul(dst, onesB[:], pt[:, m, :], start=st_, stop=False),
                                     reads=[r_pt, r_onesF], writes=[ps_res[6] if m == 0 else r_ps7])

                    qk(0)
                    qk(1)
                    for j in range(NB):
                        if j + 2 < NB:
                            qk(j + 2)
                        avmm(j)
                    for m in range(2):
                        S.act(lambda e, m=m: e.copy(out=osb[:, m, :], in_=ps[:, 4 + m, :]), reads=[ps_res[4 + m]], writes=[r_osb[m]])
                    for m in range(2):
                        dst = ps[:, 6, :] if m == 0 else ps7[:, :]
                        rd = ps_res[6] if m == 0 else r_ps7
                        S.pe(lambda e, m=m, dst=dst: e.matmul(dst, onesF[:], dacc[:, 0, m, :], start=False, stop=True), reads=[r_onesF, r_dacc[0]], writes=[rd])
                        S.dve(lambda e, m=m, dst=dst: e.reciprocal(out=rec[:, m, :], in_=dst), reads=[rd], writes=[r_rec[m]])
                    S.dve(lambda e: e.tensor_tensor(out=aa_[:], in0=osb[:, 0, :], in1=rec[:, 0, :], op=ALU.mult), reads=[r_osb[0], r_rec[0]], writes=[r_aa])
                    S.dve(lambda e: e.tensor_tensor(out=bb_[:], in0=osb[:, 1, :], in1=rec[:, 1, :], op=ALU.mult), reads=[r_osb[1], r_rec[1]], writes=[r_bb])
                    S.dve(lambda e: e.scalar_tensor_tensor(out=aa_[:], in0=bb_[:], scalar=lam[:, 5:6], in1=aa_[:], op0=ALU.mult, op1=ALU.add), reads=[r_bb, r_aa, r_lam], writes=[r_aa])
                    S.pool(lambda e: e.tensor_tensor(out=sq_[:], in0=aa_[:], in1=aa_[:], op=ALU.mult), reads=[r_aa], writes=[r_sq4])
                    S.pe(lambda e: e.matmul(ps[:, 6, :], onesF[:], sq_[:], start=True, stop=True), reads=[r_onesF, r_sq4], writes=[ps_res[6]])
                    S.act(lambda e: e.activation(out=rs_[:], in_=ps[:, 6, :], func=AF.Sqrt, bias=epsc[:, 0:1], scale=1.0 / 128.0), reads=[ps_res[6], r_eps], writes=[r_rs])
                    S.dve(lambda e: e.reciprocal(out=rs_[:], in_=rs_[:]), reads=[r_rs], writes=[r_rs])
                    at, r_at = at_ring.next()
                    S.dve(lambda e, at=at: e.scalar_tensor_tensor(out=at, in0=aa_[:], scalar=gcol[:, 0:1], in1=rs_[:], op0=ALU.mult, op1=ALU.mult), reads=[r_aa, r_gcol, r_rs], writes=[r_at])
                    S.dma("pool", attnT[h * 128:(h + 1) * 128, qcol0 + i * 512:qcol0 + (i + 1) * 512], at, reads=[r_at], writes=[r_attn])

        S.full_barrier()
        with contextlib.ExitStack() as st:
            alloc_pst(st, "p5")
            wg, r_wg = load_weight(st, "wg", w_in[:, COL_G:PROJ], D, 2 * D)
            wco, r_wco = load_weight(st, "wco", w_co, 512, D)
            wao, r_wao = load_weight(st, "wao", w_ao, 512, D)
            wo, r_wo = load_weight(st, "wo", w_o, D, D)
            gB = SB(st, "gB5", [128, D], F32)
            bB = SB(st, "bB5", [128, D], F32)
            r_gb = Res()
            S.dma("sp", gB[:], bcast_rows(lnv, 2, D), writes=[r_gb])
            S.dma("sp", bB[:], bcast_rows(lnv, 3, D), writes=[r_gb])
            xin_t = SB(st, "xin5", [128, 2, D], F32)
            xbf_t = SB(st, "xbf5", [128, 2, D], BF16)
            xin_ring = Ring([xin_t[:, k, :] for k in range(2)])
            xbf_ring = Ring([xbf_t[:, k, :] for k in range(2)])
            xT5v = SB(st, "xT5", [128, 8, 512], BF16)
            r_xT = Res()
            ca_t = SB(st, "ca5", [128, 2, 4, 512], BF16)
            ca_ring = Ring([ca_t[:, k, :, :] for k in range(2)])
            aa_t = SB(st, "aa5", [128, 2, 4, 512], BF16)
            aa_ring = Ring([aa_t[:, k, :, :] for k in range(2)])
            g_t = SB(st, "g5", [128, 4, 512], F32)
            g_ring = Ring([g_t[:, k, :] for k in range(4)])
            mT = SB(st, "mT5", [128, 8, 512], BF16)
            r_mT = [Res() for _ in range(8)]
            z_t = SB(st, "z5", [128, 3, D], F32)
            z_ring = Ring([z_t[:, k, :] for k in range(3)])
            stt_t = SB(st, "stt5", [128, 2, 12], F32)
            mv_t = SB(st, "mv5", [128, 2, 4], F32)
            st_res2 = [(Res(), Res()) for _ in range(2)]
            nst = 0
            load_T(lambda s: x1[s * 128:(s + 1) * 128, :], 4, xin_ring, xbf_ring, xT5v, r_xT, 6)
            for t in range(NM // 512):
                r0 = t * 512
                xrow = r0
                ca, r_ca = ca_ring.next()
                aa, r_aa = aa_ring.next()
                for c in range(4):
                    S.dma("sp", ca[:, c, :], cactT[c * 128:(c + 1) * 128, r0:r0 + 512], reads=[r_cact], writes=[r_ca])
                    S.dma("sp", aa[:, c, :], attnT[c * 128:(c + 1) * 128, r0:r0 + 512], reads=[r_attn], writes=[r_aa])
                for n in range(8):
                    for (b, col0) in ((0, n * 128), (1, D + n * 128)):
                        for dd in range(8):
                            S.pe(lambda e, b=b, dd=dd, col0=col0: e.matmul(ps[:, b, :], wg[:, dd, col0:col0 + 128], xT5v[:, dd, :], start=(dd == 0), stop=(dd == 7)),
                                 reads=[r_wg, r_xT], writes=[ps_res[b]])
                    for c in range(4):
                        S.pe(lambda e, c=c, n=n, ca=ca: e.matmul(ps[:, 2, :], wco[:, c, n * 128:(n + 1) * 128], ca[:, c, :], start=(c == 0), stop=(c == 3)),
                             reads=[r_wco, r_ca], writes=[ps_res[2]])
                    for c in range(4):
                        S.pe(lambda e, c=c, n=n, aa=aa: e.matmul(ps[:, 3, :], wao[:, c, n * 128:(n + 1) * 128], aa[:, c, :], start=(c == 0), stop=(c == 3)),
                             reads=[r_wao, r_aa], writes=[ps_res[3]])
                    gc, r_gc = g_ring.next()
                    ga, r_ga = g_ring.next()
                    S.act(lambda e, gc=gc, n=n: e.activation(out=gc, in_=ps[:, 0, :], func=AF.Sigmoid, bias=colv(C_BG + n), scale=1.0), reads=[ps_res[0], r_colt], writes=[r_gc])
                    S.act(lambda e, ga=ga, n=n: e.activation(out=ga, in_=ps[:, 1, :], func=AF.Sigmoid, bias=colv(C_BG + 8 + n), scale=1.0), reads=[ps_res[1], r_colt], writes=[r_ga])
                    S.dve(lambda e, gc=gc: e.tensor_tensor(out=gc, in0=gc, in1=ps[:, 2, :], op=ALU.mult), reads=[r_gc, ps_res[2]], writes=[r_gc])
                    S.dve(lambda e, ga=ga: e.tensor_tensor(out=ga, in0=ga, in1=ps[:, 3, :], op=ALU.mult), reads=[r_ga, ps_res[3]], writes=[r_ga])
                    S.pool(lambda e, gc=gc, ga=ga, n=n: e.tensor_tensor(out=mT[:, n, :], in0=gc, in1=ga, op=ALU.add), reads=[r_gc, r_ga], writes=[r_mT[n]])
                if r0 + 512 < NM:
                    load_T(lambda s, r1=r0 + 512: x1[r1 + s * 128:r1 + (s + 1) * 128, :], 4, xin_ring, xbf_ring, xT5v, r_xT, 6)
                for s in range(4):
                    z, r_z = z_ring.next()
                    S.dma("sp", z, x1[xrow + s * 128:xrow + (s + 1) * 128, :], reads=[r_x1], writes=[r_z])
                    S.act(lambda e, z=z: e.mul(out=z, in_=z, mul=ALPHA), reads=[r_z], writes=[r_z])
                    for half in range(2):
                        b = 4 + half
                        for n in range(8):
                            S.pe(lambda e, n=n, s=s, half=half, b=b: e.matmul(ps[:, b, :], mT[:, n, s * 128:(s + 1) * 128], wo[:, n, half * 512:(half + 1) * 512], start=(n == 0), stop=(n == 7)),
                                 reads=[r_mT[n], r_wo], writes=[ps_res[b]])
                    S.dve(lambda e, z=z: e.tensor_tensor(out=z, in0=z, in1=ps[:, 4:6, :].rearrange("p a b -> p (a b)"), op=ALU.add),
                          reads=[ps_res[4], ps_res[5], r_z], writes=[r_z])
                    k = nst % 2
                    nst += 1
                    layer_norm_store((stt_t[:, k, :], st_res2[k][0], mv_t[:, k, :], st_res2[k][1]), z, r_z, gB, bB, r_gb, None)
                    S.dma("pool", x2[r0 + s * 128:r0 + (s + 1) * 128, :], z, reads=[r_z], writes=[r_x2])

        S.full_barrier()
        r_yo = Res()
        ffn_phase("f2", x2, r_x2, NM, w_ffn[1][0], w_ffn[1][1], 4, yo, r_yo)
        stats = S.emit()
    return nc, stats


def _t5_bucket_np(rel):
    nb = 16
    ret = np.where(rel > 0, nb, 0)
    n = np.abs(rel)
    max_exact = 8
    nf = np.maximum(n, 1).astype(np.float32)
    large = max_exact + (np.log(nf / np.float32(max_exact)) / np.float32(math.log(128 / max_exact)) * np.float32(nb - max_exact)).astype(np.int32)
    large = np.minimum(large, nb - 1)
    return ret + np.where(n < max_exact, n, large)


def _onehot():
    i = np.arange(LB)
    b = _t5_bucket_np(639 - i)
    oh = np.zeros((32, LB), np.float32)
    oh[b, i] = 1.0
    return oh


def make_in_maps(inputs, SP, SS, NQ, ncores):
    f = lambda a: np.ascontiguousarray(np.asarray(a, dtype=np.float32))
    xp = f(inputs["x_prompt"])
    xs = f(inputs["x_sample"])[0]
    common = {
        "ffn1_gu": f(inputs["ffn1_w_gu"][0]), "ffn1_d": f(inputs["ffn1_w_down"][0]),
        "ffn2_gu": f(inputs["ffn2_w_gu"][0]), "ffn2_d": f(inputs["ffn2_w_down"][0]),
        "w_in": f(inputs["w_in"][0]), "w_co": f(inputs["w_conv_out"][0]),
        "w_ao": f(inputs["w_attn_out"][0]), "w_o": f(inputs["w_o"][0]),
        "lnv": f(np.stack([inputs["ln1_g"][0], inputs["ln1_b"][0], inputs["ln2_g"][0], inputs["ln2_b"][0], inputs["ln3_g"][0], inputs["ln3_b"][0]])),
        "sublg": f(inputs["subln_g"]).reshape(1, 128),
        "lamv": f(np.concatenate([inputs["lambda_q1"][0], inputs["lambda_k1"][0], inputs["lambda_q2"][0], inputs["lambda_k2"][0]])).reshape(1, 256),
        "tab": f(inputs["rel_bias_table"]),
        "oh": _onehot(),
        "ident": np.eye(128, dtype=np.float32),
    }
    NCOL = 16 + 124 + 12 + 4 + 1
    colbase = np.zeros((128, NCOL), np.float32)
    colbase[:, 0:16] = f(inputs["b_gate"][0]).reshape(16, 128).T
    cw = f(inputs["conv_w_dw"][0])[:, 0, :]
    for c in range(4):
        colbase[:, 16 + c * 31:16 + (c + 1) * 31] = cw[:, c * 128:(c + 1) * 128].T
    colbase[:, 140:144] = f(inputs["conv_b_dw"][0]).reshape(4, 128).T
    colbase[:, 144:148] = f(inputs["conv_ln_g"][0]).reshape(4, 128).T
    colbase[:, 148:152] = f(inputs["conv_ln_b"][0]).reshape(4, 128).T
    colbase[:, 156] = f(inputs["subln_g"]).reshape(128)
    maps = []
    for r in range(ncores):
        col = colbase.copy()
        col[:, 152] = 1.0 if r > 0 else 0.0
        col[:, 153] = 1.0 if r < ncores - 1 else 0.0
        col[:, 154] = 1.0 if r > 0 else 0.0
        col[:, 155] = 1.0 if r < ncores - 1 else 0.0
        rot = np.roll(xs, -NQ * r, axis=0)
        wrap = ((np.arange(SS // 128) * 128 + NQ * r) >= SS).astype(np.float32)
        m = dict(common)
        m["xr"] = np.ascontiguousarray(np.concatenate([xp[r], rot], axis=0))
        m["colp"] = col
        m["wrapm"] = np.ascontiguousarray(np.broadcast_to(wrap[None, :], (128, SS // 128)))
        maps.append(m)
    return maps


_CACHE = {}


def kernel(**inputs):
    SP, SS, NQ, ncores = 8192, 16384, 2048, 8
    if "nc" not in _CACHE:
        _CACHE["nc"] = build(SP, SS, NQ)[0]
    nc = _CACHE["nc"]
    in_maps = make_in_maps(inputs, SP, SS, NQ, ncores)
    res = run_bass_kernel_spmd(nc, in_maps, core_ids=list(range(ncores)))
    y_prompt = np.stack([np.asarray(res.results[r]["yo"][:SP], dtype=np.float32) for r in range(ncores)], axis=0)
    y_sample = np.concatenate([np.asarray(res.results[r]["yo"][SP:], dtype=np.float32) for r in range(ncores)], axis=0)[None]
    return (y_prompt, y_sample)
```

```python
import contextlib
import math
import numpy as np
import concourse.bass as bass
import concourse.mybir as mybir
from concourse.bass_utils import run_bass_kernel_spmd

F32 = mybir.dt.float32
BF16 = mybir.dt.bfloat16
AF = mybir.ActivationFunctionType
ALU = mybir.AluOpType
AX = mybir.AxisListType

D = 1024
DFF = 2816
NFC = DFF // 128
PROJ = 4608
COL_Q, COL_K, COL_V, COL_G = 1024, 1536, 2048, 2560
ALPHA = 2.0 ** 0.25
LAMBDA_INIT = 0.8 - 0.6 * math.exp(0.0)
LN_EPS = 1e-5
LB = 1280
ENGS = ("pe", "act", "dve", "pool", "sp")
NDMASEM = 12


class Res:
    __slots__ = ("name", "writer", "readers")

    def __init__(self, name=""):
        self.name = name
        self.writer = None
        self.readers = {}


class Op:
    __slots__ = ("eng", "fn", "deps", "needed", "is_dma", "sem", "val")

    def __init__(self, eng, fn, is_dma):
        self.eng = eng
        self.fn = fn
        self.deps = []
        self.needed = False
        self.is_dma = is_dma
        self.sem = None
        self.val = None


class Sched:
    def __init__(self, nc):
        self.nc = nc
        self.ops = {e: [] for e in ENGS}
        self.dmas = {e: [] for e in ENGS}
        self.pending = {e: [] for e in ENGS}

    def full_barrier(self):
        deps = []
        for e in ENGS:
            comp = [o for o in self.ops[e][-1:] if not o.is_dma]
            for o in reversed(self.ops[e]):
                if not o.is_dma:
                    comp = [o]
                    break
            deps.extend(comp)
            deps.extend(self.dmas[e][-NDMASEM:])
        for o in deps:
            o.needed = True
        for e in ENGS:
            self.pending[e] = list(deps)

    def _track(self, op, reads, writes):
        deps = []
        if self.pending[op.eng]:
            deps.extend(self.pending[op.eng])
            self.pending[op.eng] = []
        for r in reads:
            if r.writer is not None:
                deps.append(r.writer)
        for w in writes:
            if w.writer is not None:
                deps.append(w.writer)
            for o in w.readers.values():
                if isinstance(o, list):
                    deps.extend(o)
                else:
                    deps.append(o)
        for r in reads:
            if op.is_dma:
                r.readers.setdefault("dma_" + op.eng, []).append(op)
            else:
                r.readers[op.eng] = op
        for w in writes:
            w.writer = op
            w.readers = {}
        keep = []
        for d in deps:
            if d is op:
                continue
            if d.eng == "pe" and op.eng == "pe" and not d.is_dma and not op.is_dma:
                continue
            keep.append(d)
            d.needed = True
        op.deps = keep

    def op(self, eng, fn, reads=(), writes=()):
        o = Op(eng, fn, False)
        self._track(o, reads, writes)
        self.ops[eng].append(o)
        return o

    def dma(self, eng, out, in_, reads=(), writes=(), **kw):
        o = Op(eng, (lambda e, out=out, in_=in_, kw=kw: e.dma_start(out=out, in_=in_, **kw)), True)
        o.needed = True
        self._track(o, reads, writes)
        self.ops[eng].append(o)
        self.dmas[eng].append(o)
        return o

    def pe(self, fn, reads=(), writes=()):
        return self.op("pe", fn, reads, writes)

    def act(self, fn, reads=(), writes=()):
        return self.op("act", fn, reads, writes)

    def dve(self, fn, reads=(), writes=()):
        return self.op("dve", fn, reads, writes)

    def pool(self, fn, reads=(), writes=()):
        return self.op("pool", fn, reads, writes)

    def emit(self):
        nc = self.nc
        csem = {e: nc.alloc_semaphore(name=f"c_{e}") for e in ENGS}
        dsem = {e: [nc.alloc_semaphore(name=f"d_{e}_{j}") for j in range(NDMASEM)] for e in ("sp", "pool")}
        last_on = {}
        for e in ENGS:
            n = 0
            nd = 0
            for o in self.ops[e]:
                if o.is_dma:
                    j = nd % NDMASEM
                    o.sem = dsem[e][j]
                    o.val = 16 * (nd // NDMASEM + 1)
                    if (e, j) in last_on:
                        o.deps.append(last_on[(e, j)])
                    last_on[(e, j)] = o
                    nd += 1
                elif o.needed:
                    n += 1
                    o.sem = csem[e]
                    o.val = n
        finals = list(last_on.values())
        stats = {}
        with nc.Block() as block:
            def body(e):
                def run(eng):
                    seen = {}
                    nw = 0
                    for o in self.ops[e]:
                        for d in o.deps:
                            k = id(d.sem)
                            if seen.get(k, 0) < d.val:
                                eng.wait_ge(d.sem, d.val)
                                seen[k] = d.val
                                nw += 1
                        ins = o.fn(eng)
                        if o.is_dma:
                            ins.then_inc(o.sem, 16)
                        elif o.needed:
                            ins.then_inc(o.sem, 1)
                    if e == "sp":
                        for o in finals:
                            k = id(o.sem)
                            if seen.get(k, 0) < o.val:
                                eng.wait_ge(o.sem, o.val)
                                seen[k] = o.val
                    stats[e] = (len(self.ops[e]), nw)
                return run
            block.tensor(body("pe"))
            block.scalar(body("act"))
            block.vector(body("dve"))
            block.gpsimd(body("pool"))
            block.sync(body("sp"))
        return stats


class Ring:
    def __init__(self, views):
        self.views = views
        self.res = [Res() for _ in views]
        self.i = 0

    def next(self):
        k = self.i % len(self.views)
        self.i += 1
        return self.views[k], self.res[k]


def build(SP, SS, NQ, debug=False):
    assert SP % 512 == 0 and SS % 512 == 0 and NQ % 512 == 0 and SS >= NQ + 1024
    R = SP + SS
    NM = SP + NQ
    NBP, NBS = SP // 128, SS // 128
    HB = 32 + SP
    HC = HB + NQ + 256
    nc = bass.Bass("TRN2", target_bir_lowering=False)
    S = Sched(nc)

    def din(name, shape):
        return nc.dram_tensor(name, shape, F32, kind="ExternalInput").ap()

    def dscr(name, shape, dt):
        if debug:
            return nc.dram_tensor(name, shape, dt, kind="ExternalOutput").ap()
        return nc.dram_tensor(name, shape, dt).ap()

    xr = din("xr", [R, D])
    w_ffn = [(din("ffn1_gu", [D, 2 * DFF]), din("ffn1_d", [DFF, D])),
             (din("ffn2_gu", [D, 2 * DFF]), din("ffn2_d", [DFF, D]))]
    w_in = din("w_in", [D, PROJ])
    w_co = din("w_co", [512, D])
    w_ao = din("w_ao", [512, D])
    w_o = din("w_o", [D, D])
    lnv = din("lnv", [6, D])
    NCOL = 16 + 124 + 12 + 4 + 1
    colp = din("colp", [128, NCOL])
    wrapm = din("wrapm", [128, NBS])
    sublg = din("sublg", [1, 128])
    lamv = din("lamv", [1, 256])
    tab = din("tab", [32, 4])
    oh = din("oh", [32, LB])
    ident = din("ident", [128, 128])
    yo = nc.dram_tensor("yo", [NM, D], F32, kind="ExternalOutput").ap()

    x1 = dscr("x1", [R, D], F32)
    hT = dscr("hT", [512, HC], F32)
    QT = dscr("QT", [4, 128, NM], BF16)
    KT = dscr("KT", [4, 128, R], BF16)
    Vs = dscr("Vs", [R, 4 * 129], BF16)
    brep = dscr("brep", [4, 128, LB], F32)
    cactT = dscr("cactT", [512, NM], BF16)
    attnT = dscr("attnT", [512, NM], BF16)
    x2 = dscr("x2", [NM, D], F32)
    r_x1, r_hT, r_QT, r_KT, r_Vs, r_brep, r_cact, r_attn, r_x2 = (Res() for _ in range(9))

    def dram_ap(t, offset, pattern):
        return bass.AP(t.tensor, offset, pattern)

    def bcast_rows(src, row, n, parts=128):
        return dram_ap(src, row * src.shape[1], [[0, parts], [1, n]])

    with contextlib.ExitStack() as top:
        def SB(st, name, shape, dt):
            return st.enter_context(nc.sbuf_tensor(name, shape, dt))

        ps = top.enter_context(nc.psum_tensor("ps", [128, 7, 512], F32))
        ps_res = [Res(f"ps{b}") for b in range(7)]
        r_pstT = Res("pstT")
        PSX = {}

        def alloc_pst(st, tag):
            PSX["t"] = st.enter_context(nc.psum_tensor("pstT" + tag, [128, 1024], BF16))

        idf = SB(top, "idf", [128, 128], F32)
        idb = SB(top, "idb", [128, 128], BF16)
        colt = SB(top, "colt", [128, NCOL], F32)
        lam = SB(top, "lam", [128, 8], F32)
        r_idf, r_idb, r_colt, r_lam = Res(), Res(), Res(), Res()
        S.dma("sp", idf[:], ident[:, :], writes=[r_idf])
        S.dma("sp", colt[:], colp[:, :], writes=[r_colt])
        S.dve(lambda e: e.tensor_copy(out=idb[:], in_=idf[:]), reads=[r_idf], writes=[r_idb])
        C_BG, C_CW, C_CB, C_CG, C_CBE, C_MK, C_SG = 0, 16, 140, 144, 148, 152, 156

        def colv(c):
            return colt[:, c:c + 1]

        with contextlib.ExitStack() as st:
            lv = SB(st, "lv", [128, 256], F32)
            lp = SB(st, "lp", [128, 128], F32)
            r_lv, r_lp = Res(), Res()
            S.dma("sp", lv[:], bcast_rows(lamv, 0, 256), writes=[r_lv])
            S.dve(lambda e: e.tensor_tensor(out=lp[:, 0:64], in0=lv[:, 0:64], in1=lv[:, 64:128], op=ALU.mult), reads=[r_lv], writes=[r_lp])
            S.dve(lambda e: e.tensor_tensor(out=lp[:, 64:128], in0=lv[:, 128:192], in1=lv[:, 192:256], op=ALU.mult), reads=[r_lv, r_lp], writes=[r_lp])
            S.dve(lambda e: e.reduce_sum(out=lam[:, 0:1], in_=lp[:, 0:64], axis=AX.X), reads=[r_lp], writes=[r_lam])
            S.dve(lambda e: e.reduce_sum(out=lam[:, 1:2], in_=lp[:, 64:128], axis=AX.X), reads=[r_lp, r_lam], writes=[r_lam])
            S.act(lambda e: e.activation(out=lam[:, 2:4], in_=lam[:, 0:2], func=AF.Exp), reads=[r_lam], writes=[r_lam])
            S.dve(lambda e: e.tensor_tensor(out=lam[:, 4:5], in0=lam[:, 3:4], in1=lam[:, 2:3], op=ALU.subtract), reads=[r_lam], writes=[r_lam])
            S.dve(lambda e: e.tensor_scalar_add(out=lam[:, 5:6], in0=lam[:, 4:5], scalar1=-LAMBDA_INIT), reads=[r_lam], writes=[r_lam])
            tb = SB(st, "tb", [32, 4, 128], F32)
            oht = SB(st, "oht", [32, LB], F32)
            gb = SB(st, "gb", [128, LB], F32)
            r_tb, r_oht, r_gb = Res(), Res(), Res()
            S.dma("sp", oht[:], oh[:, :], writes=[r_oht])
            tabt = SB(st, "tabt", [32, 4], F32)
            r_tabt = Res()
            S.dma("sp", tabt[:], tab[:, :], writes=[r_tabt])
            S.pool(lambda e: e.memset(tb[:], 1.0), writes=[r_tb])
            for h in range(4):
                S.dve(lambda e, h=h: e.tensor_scalar_mul(out=tb[:, h, :], in0=tb[:, h, :], scalar1=tabt[:, h:h + 1]), reads=[r_tb, r_tabt], writes=[r_tb])
            for h in range(4):
                for (c0, c1) in ((0, 512), (512, 1024), (1024, LB)):
                    S.pe(lambda e, h=h, c0=c0, c1=c1: e.matmul(ps[:, 0, 0:c1 - c0], tb[:, h, :], oht[:, c0:c1], start=True, stop=True),
                         reads=[r_tb, r_oht], writes=[ps_res[0]])
                    S.dve(lambda e, c0=c0, c1=c1: e.tensor_copy(out=gb[:, c0:c1], in_=ps[:, 0, 0:c1 - c0]), reads=[ps_res[0]], writes=[r_gb])
                S.dma("sp", brep[h], gb[:], reads=[r_gb], writes=[r_brep])

        S.full_barrier()

        def load_weight(st, name, src, K, N, queue="pool"):
            kc = K // 128
            t = SB(st, name, [128, kc, N], BF16)
            r = Res(name)
            for c in range(kc):
                S.dma(queue, t[:, c, :], src[c * 128:(c + 1) * 128, :], writes=[r])
            return t, r

        def load_T(rows_ap_fn, nsub, xin_ring, xbf_ring, xT, r_xT, tbank, r_src=None):
            for s in range(nsub):
                xin, r_xin = xin_ring.next()
                xbf, r_xbf = xbf_ring.next()
                S.dma("sp", xin, rows_ap_fn(s), writes=[r_xin])
                S.act(lambda e, xbf=xbf, xin=xin: e.copy(out=xbf, in_=xin), reads=[r_xin], writes=[r_xbf])
                pst = PSX["t"][:, :]
                for c in range(8):
                    S.pe(lambda e, c=c, xbf=xbf, pst=pst: e.transpose(pst[:, c * 128:(c + 1) * 128], xbf[:, c * 128:(c + 1) * 128], idb[:]),
                         reads=[r_xbf, r_idb], writes=[r_pstT])
                S.dve(lambda e, s=s, pst=pst: e.tensor_copy(out=xT[:, :, s * 128:(s + 1) * 128],
                                                           in_=pst.rearrange("p (c t) -> p c t", c=8)),
                      reads=[r_pstT], writes=[r_xT])

        def layer_norm_store(st_tiles, z, r_z, gB, bB, r_gb, dst_ap):
            stt, r_stt, mv, r_mv = st_tiles
            S.dve(lambda e: e.bn_stats(out=stt[:, 0:6], in_=z[:, 0:512]), reads=[r_z], writes=[r_stt])
            S.dve(lambda e: e.bn_stats(out=stt[:, 6:12], in_=z[:, 512:1024]), reads=[r_z, r_stt], writes=[r_stt])
            S.dve(lambda e: e.bn_aggr(out=mv[:, 0:2], in_=stt[:, 0:12]), reads=[r_stt], writes=[r_mv])
            S.act(lambda e: e.activation(out=mv[:, 2:3], in_=mv[:, 1:2], func=AF.Sqrt, bias=epsc[:, 0:1], scale=1.0), reads=[r_mv, r_eps], writes=[r_mv])
            S.dve(lambda e: e.reciprocal(out=mv[:, 3:4], in_=mv[:, 2:3]), reads=[r_mv], writes=[r_mv])
            S.dve(lambda e: e.tensor_scalar(out=z, in0=z, scalar1=mv[:, 0:1], scalar2=mv[:, 3:4], op0=ALU.subtract, op1=ALU.mult),
                  reads=[r_z, r_mv], writes=[r_z])
            S.pool(lambda e: e.tensor_tensor(out=z, in0=z, in1=gB[:], op=ALU.mult), reads=[r_z, r_gb], writes=[r_z])
            S.pool(lambda e: e.tensor_tensor(out=z, in0=z, in1=bB[:], op=ALU.add), reads=[r_z, r_gb], writes=[r_z])

        cbt = SB(top, "cbt", [128, 12], F32)
        r_cbt = Res()
        S.dma("sp", cbt[:, 0:4], bcast_rows(tab, 15, 4), writes=[r_cbt])
        S.dma("sp", cbt[:, 4:8], bcast_rows(tab, 31, 4), writes=[r_cbt])
        S.dve(lambda e: e.tensor_tensor(out=cbt[:, 8:12], in0=cbt[:, 0:4], in1=cbt[:, 4:8], op=ALU.subtract), reads=[r_cbt], writes=[r_cbt])
        epsc = SB(top, "epsc", [128, 1], F32)
        r_eps = Res()
        S.pool(lambda e: e.memset(epsc[:], LN_EPS), writes=[r_eps])

        def ffn_phase(tag, src, r_src, nrows, wgu_d, wd_d, ln_row, dst, r_dst):
            with contextlib.ExitStack() as st:
                alloc_pst(st, tag)
                wgu, r_wgu = load_weight(st, "wgu" + tag, wgu_d, D, 2 * DFF)
                wd, r_wd = load_weight(st, "wd" + tag, wd_d, DFF, D)
                gB = SB(st, "gB" + tag, [128, D], F32)
                bB = SB(st, "bB" + tag, [128, D], F32)
                r_gb = Res()
                S.dma("sp", gB[:], bcast_rows(lnv, ln_row, D), writes=[r_gb])
                S.dma("sp", bB[:], bcast_rows(lnv, ln_row + 1, D), writes=[r_gb])
                xin_t = SB(st, "xin" + tag, [128, 2, D], F32)
                xbf_t = SB(st, "xbf" + tag, [128, 2, D], BF16)
                xin_ring = Ring([xin_t[:, k, :] for k in range(2)])
                xbf_ring = Ring([xbf_t[:, k, :] for k in range(2)])
                xT = SB(st, "xT" + tag, [128, 8, 512], BF16)
                r_xT = Res()
                actT = SB(st, "actT" + tag, [128, NFC, 512], BF16)
                r_act = [Res() for _ in range(NFC)]
                sa_t = SB(st, "sa" + tag, [128, 2, 512], F32)
                sa_ring = Ring([sa_t[:, k, :] for k in range(2)])
                z_t = SB(st, "z" + tag, [128, 3, D], F32)
                z_ring = Ring([z_t[:, k, :] for k in range(3)])
                stt_t = SB(st, "stt" + tag, [128, 2, 12], F32)
                mv_t = SB(st, "mv" + tag, [128, 2, 4], F32)
                st_ring = Ring([(stt_t[:, k, :], mv_t[:, k, :]) for k in range(2)])
                st_res2 = [(Res(), Res()) for _ in range(2)]
                ntile = nrows // 512
                load_T(lambda s: src[s * 128:(s + 1) * 128, :], 4, xin_ring, xbf_ring, xT, r_xT, 6)
                for t in range(ntile):
                    r0 = t * 512
                    for f in range(NFC):
                        ba, bg = (0, 1) if f % 2 == 0 else (2, 3)
                        for dd in range(8):
                            S.pe(lambda e, f=f, dd=dd, ba=ba: e.matmul(ps[:, ba, :], wgu[:, dd, f * 128:(f + 1) * 128], xT[:, dd, :], start=(dd == 0), stop=(dd == 7)),
                                 reads=[r_wgu, r_xT], writes=[ps_res[ba]])
                        for dd in range(8):
                            S.pe(lambda e, f=f, dd=dd, bg=bg: e.matmul(ps[:, bg, :], wgu[:, dd, DFF + f * 128:DFF + (f + 1) * 128], xT[:, dd, :], start=(dd == 0), stop=(dd == 7)),
                                 reads=[r_wgu, r_xT], writes=[ps_res[bg]])
                        sa, r_sa = sa_ring.next()
                        S.act(lambda e, sa=sa, ba=ba: e.activation(out=sa, in_=ps[:, ba, :], func=AF.Silu), reads=[ps_res[ba]], writes=[r_sa])
                        S.dve(lambda e, sa=sa, bg=bg, f=f: e.tensor_tensor(out=actT[:, f, :], in0=sa, in1=ps[:, bg, :], op=ALU.mult),
                              reads=[r_sa, ps_res[bg]], writes=[r_act[f]])
                    if t + 1 < ntile:
                        load_T(lambda s, r1=r0 + 512: src[r1 + s * 128:r1 + (s + 1) * 128, :], 4, xin_ring, xbf_ring, xT, r_xT, 6)
                    for s in range(4):
                        z, r_z = z_ring.next()
                        rows = slice(r0 + s * 128, r0 + (s + 1) * 128)
                        S.dma("sp", z, src[rows, :], reads=[r_src], writes=[r_z])
                        S.act(lambda e, z=z: e.mul(out=z, in_=z, mul=ALPHA), reads=[r_z], writes=[r_z])
                        for half in range(2):
                            b = 4 + half
                            for f in range(NFC):
                                S.pe(lambda e, f=f, s=s, half=half, b=b: e.matmul(ps[:, b, :], actT[:, f, s * 128:(s + 1) * 128], wd[:, f, half * 512:(half + 1) * 512], start=(f == 0), stop=(f == NFC - 1)),
                                     reads=[r_act[f], r_wd], writes=[ps_res[b]])
                        S.dve(lambda e, z=z: e.scalar_tensor_tensor(out=z, in0=ps[:, 4:6, :].rearrange("p a b -> p (a b)"), scalar=0.5, in1=z, op0=ALU.mult, op1=ALU.add),
                              reads=[ps_res[4], ps_res[5], r_z], writes=[r_z])
                        (stt, mv), _ = st_ring.next()
                        k = (st_ring.i - 1) % 2
                        layer_norm_store((stt, st_res2[k][0], mv, st_res2[k][1]), z, r_z, gB, bB, r_gb, None)
                        S.dma("pool", dst[rows, :], z, reads=[r_z], writes=[r_dst])

        r_xr = Res()
        ffn_phase("f1", xr, r_xr, R, w_ffn[0][0], w_ffn[0][1], 0, x1, r_x1)

        S.full_barrier()
        with contextlib.ExitStack() as st:
            alloc_pst(st, "p2")
            win, r_win = load_weight(st, "win", w_in[:, 0:COL_G], D, COL_G)
            xin_t = SB(st, "xin2", [128, 2, D], F32)
            xbf_t = SB(st, "xbf2", [128, 2, D], BF16)
            xin_ring = Ring([xin_t[:, k, :] for k in range(2)])
            xbf_ring = Ring([xbf_t[:, k, :] for k in range(2)])
            xT2_t = SB(st, "xT2", [128, 2, 8, 512], BF16)
            xT2_ring = Ring([xT2_t[:, k, :, :] for k in range(2)])
            sg_t = SB(st, "sg2", [128, 2, 512], F32)
            sg_ring = Ring([sg_t[:, k, :] for k in range(2)])
            ho_t = SB(st, "ho2", [128, 3, 512], F32)
            ho_ring = Ring([ho_t[:, k, :] for k in range(3)])
            qk_t = SB(st, "qk2", [128, 3, 512], BF16)
            qk_ring = Ring([qk_t[:, k, :] for k in range(3)])
            v_t = SB(st, "v2", [128, 2, 4, 129], BF16)
            v_ring = Ring([v_t[:, k, :, :] for k in range(2)])
            zt = SB(st, "zero2", [128, 16], F32)
            r_zt = Res()
            S.pool(lambda e: e.memset(zt[:], 0.0), writes=[r_zt])
            S.pool(lambda e: e.memset(v_t[:], 1.0), writes=v_ring.res)
            for c in range(4):
                S.dma("sp", hT[c * 128:(c + 1) * 128, 0:16], zt[:], reads=[r_zt], writes=[r_hT])
                S.dma("sp", hT[c * 128:(c + 1) * 128, 16 + SP:32 + SP], zt[:], reads=[r_zt], writes=[r_hT])
            xT_next = xT2_ring.next()
            load_T(lambda s: x1[s * 128:(s + 1) * 128, :], 4, xin_ring, xbf_ring, xT_next[0], xT_next[1], 6)
            for t in range(R // 512):
                r0 = t * 512
                xT, r_xT = xT_next
                if r0 + 512 < R:
                    xT_next = xT2_ring.next()
                    load_T(lambda s, r1=r0 + 512: x1[r1 + s * 128:r1 + (s + 1) * 128, :], 4, xin_ring, xbf_ring, xT_next[0], xT_next[1], 6)
                samp = r0 >= SP
                rs = r0 - SP
                need_q = (not samp) or rs < NQ
                if not samp:
                    need_u, hcol, ucols, mask = True, 16 + r0, (0, 512), None
                elif rs < NQ:
                    need_u, hcol, ucols, mask = True, HB + 128 + rs, (0, 512), None
                elif rs == NQ:
                    need_u, hcol, ucols, mask = True, HB + 128 + NQ, (0, 128), C_MK + 1
                elif rs == SS - 512:
                    need_u, hcol, ucols, mask = True, HB - 384, (384, 512), C_MK + 0
                else:
                    need_u = False
                if False:
                    dxT = nc.dram_tensor("dbg_xT", [128, 8 * 512], BF16, kind="ExternalOutput").ap()
                    dwin = nc.dram_tensor("dbg_win", [128, 2560], BF16, kind="ExternalOutput").ap()
                    dxin = nc.dram_tensor("dbg_xin", [128, 1024], F32, kind="ExternalOutput").ap()
                    S.dma("sp", dxT[:, :], xT[:].rearrange("p c t -> p (c t)"), reads=[r_xT])
                    S.dma("sp", dwin[:, :], win[:, 0, :], reads=[r_win])
                    S.dma("sp", dxin[:, :], xin_t[:, 1, :], reads=[xin_ring.res[1]])
                nb = [0]

                def bank():
                    b = nb[0] % 4
                    nb[0] += 1
                    return b

                def proj_fm(col0):
                    b = bank()
                    for dd in range(8):
                        S.pe(lambda e, dd=dd, b=b, col0=col0, xT=xT: e.matmul(ps[:, b, :], win[:, dd, col0:col0 + 128], xT[:, dd, :], start=(dd == 0), stop=(dd == 7)),
                             reads=[r_win, r_xT], writes=[ps_res[b]])
                    return b

                if need_u:
                    for j in range(4):
                        ba = proj_fm(j * 128)
                        bg = proj_fm(512 + j * 128)
                        sg, r_sg = sg_ring.next()
                        ho, r_ho = ho_ring.next()
                        S.act(lambda e, sg=sg, bg=bg: e.activation(out=sg, in_=ps[:, bg, :], func=AF.Sigmoid), reads=[ps_res[bg]], writes=[r_sg])
                        S.dve(lambda e, sg=sg, ho=ho, ba=ba: e.tensor_tensor(out=ho, in0=sg, in1=ps[:, ba, :], op=ALU.mult), reads=[r_sg, ps_res[ba]], writes=[r_ho])
                        if mask is not None:
                            S.dve(lambda e, ho=ho, mask=mask: e.tensor_scalar_mul(out=ho, in0=ho, scalar1=colv(mask)), reads=[r_ho, r_colt], writes=[r_ho])
                        u0, u1 = ucols
                        S.dma("pool", hT[j * 128:(j + 1) * 128, hcol + u0:hcol + u1], ho[:, u0:u1], reads=[r_ho], writes=[r_hT])
                for (need, col0, dstT, r_d, cbase) in ((need_q, COL_Q, QT, r_QT, (r0 if not samp else SP + rs)), (True, COL_K, KT, r_KT, r0)):
                    if not need:
                        continue
                    for h in range(4):
                        b = proj_fm(col0 + h * 128)
                        qk, r_qk = qk_ring.next()
                        S.act(lambda e, qk=qk, b=b: e.copy(out=qk, in_=ps[:, b, :]), reads=[ps_res[b]], writes=[r_qk])
                        S.dma("pool", dstT[h, :, cbase:cbase + 512], qk, reads=[r_qk], writes=[r_d])
                for s in range(4):
                    b = bank()
                    for dd in range(8):
                        S.pe(lambda e, dd=dd, b=b, s=s, xT=xT: e.matmul(ps[:, b, :], xT[:, dd, s * 128:(s + 1) * 128], win[:, dd, COL_V:COL_V + 512], start=(dd == 0), stop=(dd == 7)),
                             reads=[r_win, r_xT], writes=[ps_res[b]])
                    vt, r_vt = v_ring.next()
                    S.dve(lambda e, vt=vt, b=b: e.tensor_copy(out=vt[:, :, 0:128], in_=ps[:, b, :].rearrange("p (h e) -> p h e", h=4)), reads=[ps_res[b]], writes=[r_vt])
                    S.dma("pool", Vs[r0 + s * 128:r0 + (s + 1) * 128, :], vt.rearrange("p h e -> p (h e)"), reads=[r_vt], writes=[r_Vs])

        S.full_barrier()
        with contextlib.ExitStack() as st:
            hin_t = SB(st, "hin", [128, 2, 4, 544], F32)
            hin_ring = Ring([hin_t[:, k, :, :] for k in range(2)])
            acc = SB(st, "cacc", [128, 4, 512], F32)
            r_acc = [Res() for _ in range(4)]
            sq = SB(st, "csq", [128, 4, 512], F32)
            r_sq = [Res() for _ in range(4)]
            onesm = SB(st, "onesm", [128, 128], F32)
            r_ones = Res()
            S.pool(lambda e: e.memset(onesm[:], 1.0 / 512.0), writes=[r_ones])
            mean = SB(st, "cmean", [128, 512], F32)
            rstd = SB(st, "crstd", [128, 512], F32)
            r_mean, r_rstd = Res(), Res()
            co_t = SB(st, "cout", [128, 3, 512], BF16)
            co_ring = Ring([co_t[:, k, :] for k in range(3)])
            NPE = 13
            cdiag = SB(st, "cdiag", [128, 4, NPE, 128], F32)
            r_cdiag = Res()
            for c in range(4):
                for j in range(NPE):
                    S.dve(lambda e, c=c, j=j: e.tensor_scalar_mul(out=cdiag[:, c, j, :], in0=idf[:], scalar1=colv(C_CW + c * 31 + j)), reads=[r_idf, r_colt], writes=[r_cdiag])
            tiles = [(16 + t * 512, t * 512) for t in range(SP // 512)] + [(HB + 128 + t * 512, SP + t * 512) for t in range(NQ // 512)]
            for (hc, mrow) in tiles:
                hin, r_hin = hin_ring.next()
                for c in range(4):
                    S.dma("sp", hin[:, c, 0:542], hT[c * 128:(c + 1) * 128, hc - 15:hc + 527], reads=[r_hT], writes=[r_hin])
                for c in range(4):
                    for j in range(NPE):
                        S.pe(lambda e, c=c, j=j, hin=hin: e.matmul(ps[:, c, :], cdiag[:, c, j, :], hin[:, c, j:j + 512], start=(j == 0), stop=(j == NPE - 1)),
                             reads=[r_cdiag, r_hin], writes=[ps_res[c]])
                for j in range(NPE, 31):
                    for c in range(4):
                        if j == NPE:
                            S.dve(lambda e, c=c, j=j, hin=hin: e.tensor_scalar(out=acc[:, c, :], in0=hin[:, c, j:j + 512], scalar1=colv(C_CW + c * 31 + j), scalar2=colv(C_CB + c), op0=ALU.mult, op1=ALU.add),
                                  reads=[r_hin, r_colt], writes=[r_acc[c]])
                        else:
                            S.dve(lambda e, c=c, j=j, hin=hin: e.scalar_tensor_tensor(out=acc[:, c, :], in0=hin[:, c, j:j + 512], scalar=colv(C_CW + c * 31 + j), in1=acc[:, c, :], op0=ALU.mult, op1=ALU.add),
                                  reads=[r_hin, r_colt, r_acc[c]], writes=[r_acc[c]])
                for c in range(4):
                    S.dve(lambda e, c=c: e.tensor_tensor(out=acc[:, c, :], in0=acc[:, c, :], in1=ps[:, c, :], op=ALU.add), reads=[r_acc[c], ps_res[c]], writes=[r_acc[c]])
                for c in range(4):
                    S.pool(lambda e, c=c: e.tensor_tensor(out=sq[:, c, :], in0=acc[:, c, :], in1=acc[:, c, :], op=ALU.mult), reads=[r_acc[c]], writes=[r_sq[c]])
                for c in range(4):
                    S.pe(lambda e, c=c: e.matmul(ps[:, 4, :], onesm[:], acc[:, c, :], start=(c == 0), stop=(c == 3)), reads=[r_ones, r_acc[c]], writes=[ps_res[4]])
                for c in range(4):
                    S.pe(lambda e, c=c: e.matmul(ps[:, 5, :], onesm[:], sq[:, c, :], start=(c == 0), stop=(c == 3)), reads=[r_ones, r_sq[c]], writes=[ps_res[5]])
                S.act(lambda e: e.copy(out=mean[:], in_=ps[:, 4, :]), reads=[ps_res[4]], writes=[r_mean])
                S.pool(lambda e: e.tensor_tensor(out=rstd[:], in0=mean[:], in1=mean[:], op=ALU.mult), reads=[r_mean], writes=[r_rstd])
                S.dve(lambda e: e.tensor_tensor(out=rstd[:], in0=ps[:, 5, :], in1=rstd[:], op=ALU.subtract), reads=[ps_res[5], r_rstd], writes=[r_rstd])
                S.act(lambda e: e.activation(out=rstd[:], in_=rstd[:], func=AF.Sqrt, bias=epsc[:, 0:1], scale=1.0), reads=[r_rstd, r_eps], writes=[r_rstd])
                S.dve(lambda e: e.reciprocal(out=rstd[:], in_=rstd[:]), reads=[r_rstd], writes=[r_rstd])
                for c in range(4):
                    S.pool(lambda e, c=c: e.tensor_tensor(out=acc[:, c, :], in0=acc[:, c, :], in1=mean[:], op=ALU.subtract), reads=[r_acc[c], r_mean], writes=[r_acc[c]])
                    S.dve(lambda e, c=c: e.tensor_tensor(out=acc[:, c, :], in0=acc[:, c, :], in1=rstd[:], op=ALU.mult), reads=[r_acc[c], r_rstd], writes=[r_acc[c]])
                    co, r_co = co_ring.next()
                    S.act(lambda e, c=c, co=co: e.activation(out=co, in_=acc[:, c, :], func=AF.Silu, bias=colv(C_CBE + c), scale=colv(C_CG + c)), reads=[r_acc[c], r_colt], writes=[r_co])
                    S.dma("pool", cactT[c * 128:(c + 1) * 128, mrow:mrow + 512], co, reads=[r_co], writes=[r_cact])

        S.full_barrier()
        with contextlib.ExitStack() as st:
            KMAX = max(SP, SS)
            kt_buf = SB(st, "kt", [128, 2, KMAX], BF16)
            vh_buf = SB(st, "vh", [128, 2, KMAX // 128, 128], BF16)
            kv_res = [(Res(), Res()), (Res(), Res())]
            bt = SB(st, "bt", [128, 8, 512], F32)
            r_bt = Res()
            farb = SB(st, "farb", [128, NBS], F32)
            r_farb = Res()
            wm = SB(st, "wm", [128, NBS], F32)
            r_wm = Res()
            S.dma("sp", wm[:], wrapm[:, :], writes=[r_wm])
            gcol = SB(st, "gcol", [128, 1], F32)
            r_gcol = Res()
            S.dve(lambda e: e.tensor_scalar_mul(out=gcol[:], in0=colv(C_SG), scalar1=1.0 - LAMBDA_INIT), reads=[r_colt], writes=[r_gcol])
            ps7 = st.enter_context(nc.psum_tensor("ps7", [128, 512], F32))
            r_ps7 = Res()
            onesF = SB(st, "onesF", [128, 128], F32)
            onesB = SB(st, "onesB", [128, 128], BF16)
            r_onesF = Res()
            S.pool(lambda e: e.memset(onesF[:], 1.0), writes=[r_onesF])
            S.pool(lambda e: e.memset(onesB[:], 1.0), writes=[r_onesF])
            qt_t = SB(st, "qt", [128, 2, 512], BF16)
            qt_ring = Ring([qt_t[:, k, :] for k in range(2)])
            pt_t = SB(st, "pt", [128, 4, 2, 512], BF16)
            pt_ring = Ring([pt_t[:, k, :, :] for k in range(4)])
            tmp_t = SB(st, "stmp", [128, 2, 2, 512], F32)
            tmp_ring = Ring([tmp_t[:, k, :, :] for k in range(2)])
            dacc = SB(st, "dacc", [128, 1, 2, 512], F32)
            r_dacc = [Res(), Res()]
            rec = SB(st, "rec", [128, 2, 512], F32)
            r_rec = [Res(), Res()]
            osb = SB(st, "osb", [128, 2, 512], F32)
            r_osb = [Res(), Res()]
            aa_ = SB(st, "a4", [128, 512], F32)
            bb_ = SB(st, "b4", [128, 512], F32)
            sq_ = SB(st, "sq4", [128, 512], F32)
            rs_ = SB(st, "rs4", [128, 512], F32)
            r_aa, r_bb, r_sq4, r_rs = Res(), Res(), Res(), Res()
            at_t = SB(st, "at", [128, 2, 512], BF16)
            at_ring = Ring([at_t[:, k, :] for k in range(2)])
            units = [(0, SP, SP, 0, False, h) for h in range(4)] + [(SP, SS, NQ, SP, True, h) for h in range(4)]
            def load_kv(u):
                (krow0, S_k, nq, qcol0, samp, h) = units[u]
                NB = S_k // 128
                kt_u, vh_u = kt_buf[:, u % 2, :], vh_buf[:, u % 2, :, :]
                r_kt_u, r_vh_u = kv_res[u % 2]
                S.dma("sp", kt_u[:, 0:S_k], KT[h, :, krow0:krow0 + S_k], reads=[r_KT], writes=[r_kt_u])
                nvc = max(1, NB // 16)
                for vc in range(nvc):
                    b0, b1 = vc * NB // nvc, (vc + 1) * NB // nvc
                    S.dma("sp", vh_u[:, b0:b1, :],
                          dram_ap(Vs, (krow0 + b0 * 128) * 516 + h * 129, [[516, 128], [128 * 516, b1 - b0], [1, 128]]),
                          reads=[r_Vs], writes=[r_vh_u])

            load_kv(0)
            for u, (krow0, S_k, nq, qcol0, samp, h) in enumerate(units):
                NB = S_k // 128
                nqt = nq // 512
                kt_t, vh_t = kt_buf[:, u % 2, :], vh_buf[:, u % 2, :, :]
                r_kt, r_vh = kv_res[u % 2]
                for kk in range(6):
                    S.dma("sp", bt[:, kk, :], dram_ap(brep, h * 128 * LB + 639 - 128 * (kk - 1), [[LB - 1, 128], [1, 512]]), reads=[r_brep], writes=[r_bt])
                if u + 1 < len(units):
                    load_kv(u + 1)
                cneg, cpos, cdif = cbt[:, h:h + 1], cbt[:, 4 + h:5 + h], cbt[:, 8 + h:9 + h]
                if samp:
                    S.dve(lambda e, cdif=cdif, cpos=cpos: e.tensor_scalar(out=farb[:], in0=wm[:], scalar1=cdif, scalar2=cpos, op0=ALU.mult, op1=ALU.add), reads=[r_wm, r_cbt], writes=[r_farb])
                    for (dst, srck, cc, mcol) in ((6, 0, cpos, C_MK + 2), (7, 5, cneg, C_MK + 3)):
                        S.dve(lambda e, dst=dst, srck=srck, cc=cc, mcol=mcol: e.tensor_scalar(out=bt[:, dst, :], in0=bt[:, srck, :], scalar1=cc, scalar2=colv(mcol), op0=ALU.subtract, op1=ALU.mult),
                              reads=[r_bt, r_cbt, r_colt], writes=[r_bt])
                        S.dve(lambda e, dst=dst, cc=cc: e.tensor_scalar_add(out=bt[:, dst, :], in0=bt[:, dst, :], scalar1=cc), reads=[r_bt, r_cbt], writes=[r_bt])
                for i in range(nqt):
                    qt, r_qt = qt_ring.next()
                    S.dma("sp", qt, QT[h, :, qcol0 + i * 512:qcol0 + (i + 1) * 512], reads=[r_QT], writes=[r_qt])

                    def kind(j):
                        rel = j - 4 * i
                        if samp and i == 0 and j == NB - 1:
                            return ("near", 6)
                        if samp and i == nqt - 1 and rel == 4:
                            return ("near", 7)
                        if -1 <= rel <= 4:
                            return ("near", rel + 1)
                        if rel < -1:
                            return ("far", cneg, r_cbt)
                        if samp:
                            return ("far", farb[:, j:j + 1], r_farb)
                        return ("far", cpos, r_cbt)

                    pend = {}

                    def qk(j):
                        p = 2 * (j % 2)
                        for m in range(2):
                            S.pe(lambda e, j=j, m=m, p=p, qt=qt, kt_t=kt_t: e.matmul(ps[:, p + m, :], kt_t[m * 64:(m + 1) * 64, j * 128:(j + 1) * 128], qt[m * 64:(m + 1) * 64, :], start=True, stop=True),
                                 reads=[r_kt, r_qt], writes=[ps_res[p + m]])
                        kd = kind(j)
                        pt, r_pt = pt_ring.next()
                        if kd[0] == "near":
                            tmp, r_tmp = tmp_ring.next()
                            for m in range(2):
                                S.dve(lambda e, m=m, p=p, tmp=tmp, kk=kd[1]: e.scalar_tensor_tensor(out=tmp[:, m, :], in0=ps[:, p + m, :], scalar=0.125, in1=bt[:, kk, :], op0=ALU.mult, op1=ALU.add),
                                      reads=[ps_res[p + m], r_bt, r_tmp], writes=[r_tmp])
                            S.act(lambda e, pt=pt, tmp=tmp: e.activation(out=pt, in_=tmp, func=AF.Exp), reads=[r_tmp], writes=[r_pt])
                        else:
                            S.act(lambda e, pt=pt, p=p, bc=kd[1]: e.activation(out=pt, in_=ps[:, p:p + 2, :], func=AF.Exp, bias=bc, scale=0.125),
                                  reads=[ps_res[p], ps_res[p + 1], kd[2]], writes=[r_pt])
                        pend[j] = (pt, r_pt)

                    def avmm(j):
                        pt, r_pt = pend.pop(j)
                        for m in range(2):
                            S.pe(lambda e, j=j, m=m, pt=pt, vh_t=vh_t, st_=(j == 0), sp_=(j == NB - 1): e.matmul(ps[:, 4 + m, :], vh_t[:, j, :], pt[:, m, :], start=st_, stop=sp_),
                                 reads=[r_pt, r_vh], writes=[ps_res[4 + m]])
                        if j % 2 == 0:
                            if j == 0:
                                S.dve(lambda e, pt=pt: e.tensor_copy(out=dacc[:, 0, :, :], in_=pt), reads=[r_pt], writes=[r_dacc[0]])
                            else:
                                S.dve(lambda e, pt=pt: e.tensor_tensor(out=dacc[:, 0, :, :], in0=dacc[:, 0, :, :], in1=pt, op=ALU.add), reads=[r_pt, r_dacc[0]], writes=[r_dacc[0]])
                        else:
                            for m in range(2):
                                dst = ps[:, 6, :] if m == 0 else ps7[:, :]
                                S.pe(lambda e, m=m, pt=pt, dst=dst, st_=(j == 1): e.matmul(dst, onesB[:], pt[:, m, :], start=st_, stop=False),
                                     reads=[r_pt, r_onesF], writes=[ps_res[6] if m == 0 else r_ps7])

                    qk(0)
                    qk(1)
                    for j in range(NB):
                        if j + 2 < NB:
                            qk(j + 2)
                        avmm(j)
                    for m in range(2):
                        S.act(lambda e, m=m: e.copy(out=osb[:, m, :], in_=ps[:, 4 + m, :]), reads=[ps_res[4 + m]], writes=[r_osb[m]])
                    for m in range(2):
                        dst = ps[:, 6, :] if m == 0 else ps7[:, :]
                        rd = ps_res[6] if m == 0 else r_ps7
                        S.pe(lambda e, m=m, dst=dst: e.matmul(dst, onesF[:], dacc[:, 0, m, :], start=False, stop=True), reads=[r_onesF, r_dacc[0]], writes=[rd])
                        S.dve(lambda e, m=m, dst=dst: e.reciprocal(out=rec[:, m, :], in_=dst), reads=[rd], writes=[r_rec[m]])
                    S.dve(lambda e: e.tensor_tensor(out=aa_[:], in0=osb[:, 0, :], in1=rec[:, 0, :], op=ALU.mult), reads=[r_osb[0], r_rec[0]], writes=[r_aa])
                    S.dve(lambda e: e.tensor_tensor(out=bb_[:], in0=osb[:, 1, :], in1=rec[:, 1, :], op=ALU.mult), reads=[r_osb[1], r_rec[1]], writes=[r_bb])
                    S.dve(lambda e: e.scalar_tensor_tensor(out=aa_[:], in0=bb_[:], scalar=lam[:, 5:6], in1=aa_[:], op0=ALU.mult, op1=ALU.add), reads=[r_bb, r_aa, r_lam], writes=[r_aa])
                    S.pool(lambda e: e.tensor_tensor(out=sq_[:], in0=aa_[:], in1=aa_[:], op=ALU.mult), reads=[r_aa], writes=[r_sq4])
                    S.pe(lambda e: e.matmul(ps[:, 6, :], onesF[:], sq_[:], start=True, stop=True), reads=[r_onesF, r_sq4], writes=[ps_res[6]])
                    S.act(lambda e: e.activation(out=rs_[:], in_=ps[:, 6, :], func=AF.Sqrt, bias=epsc[:, 0:1], scale=1.0 / 128.0), reads=[ps_res[6], r_eps], writes=[r_rs])
                    S.dve(lambda e: e.reciprocal(out=rs_[:], in_=rs_[:]), reads=[r_rs], writes=[r_rs])
                    at, r_at = at_ring.next()
                    S.dve(lambda e, at=at: e.scalar_tensor_tensor(out=at, in0=aa_[:], scalar=gcol[:, 0:1], in1=rs_[:], op0=ALU.mult, op1=ALU.mult), reads=[r_aa, r_gcol, r_rs], writes=[r_at])
                    S.dma("pool", attnT[h * 128:(h + 1) * 128, qcol0 + i * 512:qcol0 + (i + 1) * 512], at, reads=[r_at], writes=[r_attn])

        S.full_barrier()
        with contextlib.ExitStack() as st:
            alloc_pst(st, "p5")
            wg, r_wg = load_weight(st, "wg", w_in[:, COL_G:PROJ], D, 2 * D)
            wco, r_wco = load_weight(st, "wco", w_co, 512, D)
            wao, r_wao = load_weight(st, "wao", w_ao, 512, D)
            wo, r_wo = load_weight(st, "wo", w_o, D, D)
            gB = SB(st, "gB5", [128, D], F32)
            bB = SB(st, "bB5", [128, D], F32)
            r_gb = Res()
            S.dma("sp", gB[:], bcast_rows(lnv, 2, D), writes=[r_gb])
            S.dma("sp", bB[:], bcast_rows(lnv, 3, D), writes=[r_gb])
            xin_t = SB(st, "xin5", [128, 2, D], F32)
            xbf_t = SB(st, "xbf5", [128, 2, D], BF16)
            xin_ring = Ring([xin_t[:, k, :] for k in range(2)])
            xbf_ring = Ring([xbf_t[:, k, :] for k in range(2)])
            xT5v = SB(st, "xT5", [128, 8, 512], BF16)
            r_xT = Res()
            ca_t = SB(st, "ca5", [128, 2, 4, 512], BF16)
            ca_ring = Ring([ca_t[:, k, :, :] for k in range(2)])
            aa_t = SB(st, "aa5", [128, 2, 4, 512], BF16)
            aa_ring = Ring([aa_t[:, k, :, :] for k in range(2)])
            g_t = SB(st, "g5", [128, 4, 512], F32)
            g_ring = Ring([g_t[:, k, :] for k in range(4)])
            mT = SB(st, "mT5", [128, 8, 512], BF16)
            r_mT = [Res() for _ in range(8)]
            z_t = SB(st, "z5", [128, 3, D], F32)
            z_ring = Ring([z_t[:, k, :] for k in range(3)])
            stt_t = SB(st, "stt5", [128, 2, 12], F32)
            mv_t = SB(st, "mv5", [128, 2, 4], F32)
            st_res2 = [(Res(), Res()) for _ in range(2)]
            nst = 0
            load_T(lambda s: x1[s * 128:(s + 1) * 128, :], 4, xin_ring, xbf_ring, xT5v, r_xT, 6)
            for t in range(NM // 512):
                r0 = t * 512
                xrow = r0
                ca, r_ca = ca_ring.next()
                aa, r_aa = aa_ring.next()
                for c in range(4):
                    S.dma("sp", ca[:, c, :], cactT[c * 128:(c + 1) * 128, r0:r0 + 512], reads=[r_cact], writes=[r_ca])
                    S.dma("sp", aa[:, c, :], attnT[c * 128:(c + 1) * 128, r0:r0 + 512], reads=[r_attn], writes=[r_aa])
                for n in range(8):
                    for (b, col0) in ((0, n * 128), (1, D + n * 128)):
                        for dd in range(8):
                            S.pe(lambda e, b=b, dd=dd, col0=col0: e.matmul(ps[:, b, :], wg[:, dd, col0:col0 + 128], xT5v[:, dd, :], start=(dd == 0), stop=(dd == 7)),
                                 reads=[r_wg, r_xT], writes=[ps_res[b]])
                    for c in range(4):
                        S.pe(lambda e, c=c, n=n, ca=ca: e.matmul(ps[:, 2, :], wco[:, c, n * 128:(n + 1) * 128], ca[:, c, :], start=(c == 0), stop=(c == 3)),
                             reads=[r_wco, r_ca], writes=[ps_res[2]])
                    for c in range(4):
                        S.pe(lambda e, c=c, n=n, aa=aa: e.matmul(ps[:, 3, :], wao[:, c, n * 128:(n + 1) * 128], aa[:, c, :], start=(c == 0), stop=(c == 3)),
                             reads=[r_wao, r_aa], writes=[ps_res[3]])
                    gc, r_gc = g_ring.next()
                    ga, r_ga = g_ring.next()
                    S.act(lambda e, gc=gc, n=n: e.activation(out=gc, in_=ps[:, 0, :], func=AF.Sigmoid, bias=colv(C_BG + n), scale=1.0), reads=[ps_res[0], r_colt], writes=[r_gc])
                    S.act(lambda e, ga=ga, n=n: e.activation(out=ga, in_=ps[:, 1, :], func=AF.Sigmoid, bias=colv(C_BG + 8 + n), scale=1.0), reads=[ps_res[1], r_colt], writes=[r_ga])
                    S.dve(lambda e, gc=gc: e.tensor_tensor(out=gc, in0=gc, in1=ps[:, 2, :], op=ALU.mult), reads=[r_gc, ps_res[2]], writes=[r_gc])
                    S.dve(lambda e, ga=ga: e.tensor_tensor(out=ga, in0=ga, in1=ps[:, 3, :], op=ALU.mult), reads=[r_ga, ps_res[3]], writes=[r_ga])
                    S.pool(lambda e, gc=gc, ga=ga, n=n: e.tensor_tensor(out=mT[:, n, :], in0=gc, in1=ga, op=ALU.add), reads=[r_gc, r_ga], writes=[r_mT[n]])
                if r0 + 512 < NM:
                    load_T(lambda s, r1=r0 + 512: x1[r1 + s * 128:r1 + (s + 1) * 128, :], 4, xin_ring, xbf_ring, xT5v, r_xT, 6)
                for s in range(4):
                    z, r_z = z_ring.next()
                    S.dma("sp", z, x1[xrow + s * 128:xrow + (s + 1) * 128, :], reads=[r_x1], writes=[r_z])
                    S.act(lambda e, z=z: e.mul(out=z, in_=z, mul=ALPHA), reads=[r_z], writes=[r_z])
                    for half in range(2):
                        b = 4 + half
                        for n in range(8):
                            S.pe(lambda e, n=n, s=s, half=half, b=b: e.matmul(ps[:, b, :], mT[:, n, s * 128:(s + 1) * 128], wo[:, n, half * 512:(half + 1) * 512], start=(n == 0), stop=(n == 7)),
                                 reads=[r_mT[n], r_wo], writes=[ps_res[b]])
                    S.dve(lambda e, z=z: e.tensor_tensor(out=z, in0=z, in1=ps[:, 4:6, :].rearrange("p a b -> p (a b)"), op=ALU.add),
                          reads=[ps_res[4], ps_res[5], r_z], writes=[r_z])
                    k = nst % 2
                    nst += 1
                    layer_norm_store((stt_t[:, k, :], st_res2[k][0], mv_t[:, k, :], st_res2[k][1]), z, r_z, gB, bB, r_gb, None)
                    S.dma("pool", x2[r0 + s * 128:r0 + (s + 1) * 128, :], z, reads=[r_z], writes=[r_x2])

        S.full_barrier()
        r_yo = Res()
        ffn_phase("f2", x2, r_x2, NM, w_ffn[1][0], w_ffn[1][1], 4, yo, r_yo)
        stats = S.emit()
    return nc, stats


def _t5_bucket_np(rel):
    nb = 16
    ret = np.where(rel > 0, nb, 0)
    n = np.abs(rel)
    max_exact = 8
    nf = np.maximum(n, 1).astype(np.float32)
    large = max_exact + (np.log(nf / np.float32(max_exact)) / np.float32(math.log(128 / max_exact)) * np.float32(nb - max_exact)).astype(np.int32)
    large = np.minimum(large, nb - 1)
    return ret + np.where(n < max_exact, n, large)


def _onehot():
    i = np.arange(LB)
    b = _t5_bucket_np(639 - i)
    oh = np.zeros((32, LB), np.float32)
    oh[b, i] = 1.0
    return oh


def make_in_maps(inputs, SP, SS, NQ, ncores):
    f = lambda a: np.ascontiguousarray(np.asarray(a, dtype=np.float32))
    xp = f(inputs["x_prompt"])
    xs = f(inputs["x_sample"])[0]
    common = {
        "ffn1_gu": f(inputs["ffn1_w_gu"][0]), "ffn1_d": f(inputs["ffn1_w_down"][0]),
        "ffn2_gu": f(inputs["ffn2_w_gu"][0]), "ffn2_d": f(inputs["ffn2_w_down"][0]),
        "w_in": f(inputs["w_in"][0]), "w_co": f(inputs["w_conv_out"][0]),
        "w_ao": f(inputs["w_attn_out"][0]), "w_o": f(inputs["w_o"][0]),
        "lnv": f(np.stack([inputs["ln1_g"][0], inputs["ln1_b"][0], inputs["ln2_g"][0], inputs["ln2_b"][0], inputs["ln3_g"][0], inputs["ln3_b"][0]])),
        "sublg": f(inputs["subln_g"]).reshape(1, 128),
        "lamv": f(np.concatenate([inputs["lambda_q1"][0], inputs["lambda_k1"][0], inputs["lambda_q2"][0], inputs["lambda_k2"][0]])).reshape(1, 256),
        "tab": f(inputs["rel_bias_table"]),
        "oh": _onehot(),
        "ident": np.eye(128, dtype=np.float32),
    }
    NCOL = 16 + 124 + 12 + 4 + 1
    colbase = np.zeros((128, NCOL), np.float32)
    colbase[:, 0:16] = f(inputs["b_gate"][0]).reshape(16, 128).T
    cw = f(inputs["conv_w_dw"][0])[:, 0, :]
    for c in range(4):
        colbase[:, 16 + c * 31:16 + (c + 1) * 31] = cw[:, c * 128:(c + 1) * 128].T
    colbase[:, 140:144] = f(inputs["conv_b_dw"][0]).reshape(4, 128).T
    colbase[:, 144:148] = f(inputs["conv_ln_g"][0]).reshape(4, 128).T
    colbase[:, 148:152] = f(inputs["conv_ln_b"][0]).reshape(4, 128).T
    colbase[:, 156] = f(inputs["subln_g"]).reshape(128)
    maps = []
    for r in range(ncores):
        col = colbase.copy()
        col[:, 152] = 1.0 if r > 0 else 0.0
        col[:, 153] = 1.0 if r < ncores - 1 else 0.0
        col[:, 154] = 1.0 if r > 0 else 0.0
        col[:, 155] = 1.0 if r < ncores - 1 else 0.0
        rot = np.roll(xs, -NQ * r, axis=0)
        wrap = ((np.arange(SS // 128) * 128 + NQ * r) >= SS).astype(np.float32)
        m = dict(common)
        m["xr"] = np.ascontiguousarray(np.concatenate([xp[r], rot], axis=0))
        m["colp"] = col
        m["wrapm"] = np.ascontiguousarray(np.broadcast_to(wrap[None, :], (128, SS // 128)))
        maps.append(m)
    return maps


_CACHE = {}


def kernel(**inputs):
    SP, SS, NQ, ncores = 8192, 16384, 2048, 8
    if "nc" not in _CACHE:
        _CACHE["nc"] = build(SP, SS, NQ)[0]
    nc = _CACHE["nc"]
    in_maps = make_in_maps(inputs, SP, SS, NQ, ncores)
    res = run_bass_kernel_spmd(nc, in_maps, core_ids=list(range(ncores)))
    y_prompt = np.stack([np.asarray(res.results[r]["yo"][:SP], dtype=np.float32) for r in range(ncores)], axis=0)
    y_sample = np.concatenate([np.asarray(res.results[r]["yo"][SP:], dtype=np.float32) for r in range(ncores)], axis=0)[None]
    return (y_prompt, y_sample)
```

```python
import contextlib
import math
import numpy as np
import concourse.bass as bass
import concourse.mybir as mybir
from concourse.bass_utils import run_bass_kernel_spmd

F32 = mybir.dt.float32
BF16 = mybir.dt.bfloat16
AF = mybir.ActivationFunctionType
ALU = mybir.AluOpType
AX = mybir.AxisListType

D = 1024
DFF = 2816
NFC = DFF // 128
PROJ = 4608
COL_Q, COL_K, COL_V, COL_G = 1024, 1536, 2048, 2560
ALPHA = 2.0 ** 0.25
LAMBDA_INIT = 0.8 - 0.6 * math.exp(0.0)
LN_EPS = 1e-5
LB = 1280
ENGS = ("pe", "act", "dve", "pool", "sp")
NDMASEM = 12


class Res:
    __slots__ = ("name", "writer", "readers")

    def __init__(self, name=""):
        self.name = name
        self.writer = None
        self.readers = {}


class Op:
    __slots__ = ("eng", "fn", "deps", "needed", "is_dma", "sem", "val")

    def __init__(self, eng, fn, is_dma):
        self.eng = eng
        self.fn = fn
        self.deps = []
        self.needed = False
        self.is_dma = is_dma
        self.sem = None
        self.val = None


class Sched:
    def __init__(self, nc):
        self.nc = nc
        self.ops = {e: [] for e in ENGS}
        self.dmas = {e: [] for e in ENGS}
        self.pending = {e: [] for e in ENGS}

    def full_barrier(self):
        deps = []
        for e in ENGS:
            comp = [o for o in self.ops[e][-1:] if not o.is_dma]
            for o in reversed(self.ops[e]):
                if not o.is_dma:
                    comp = [o]
                    break
            deps.extend(comp)
            deps.extend(self.dmas[e][-NDMASEM:])
        for o in deps:
            o.needed = True
        for e in ENGS:
            self.pending[e] = list(deps)

    def _track(self, op, reads, writes):
        deps = []
        if self.pending[op.eng]:
            deps.extend(self.pending[op.eng])
            self.pending[op.eng] = []
        for r in reads:
            if r.writer is not None:
                deps.append(r.writer)
        for w in writes:
            if w.writer is not None:
                deps.append(w.writer)
            for o in w.readers.values():
                if isinstance(o, list):
                    deps.extend(o)
                else:
                    deps.append(o)
        for r in reads:
            if op.is_dma:
                r.readers.setdefault("dma_" + op.eng, []).append(op)
            else:
                r.readers[op.eng] = op
        for w in writes:
            w.writer = op
            w.readers = {}
        keep = []
        for d in deps:
            if d is op:
                continue
            if d.eng == "pe" and op.eng == "pe" and not d.is_dma and not op.is_dma:
                continue
            keep.append(d)
            d.needed = True
        op.deps = keep

    def op(self, eng, fn, reads=(), writes=()):
        o = Op(eng, fn, False)
        self._track(o, reads, writes)
        self.ops[eng].append(o)
        return o

    def dma(self, eng, out, in_, reads=(), writes=(), **kw):
        o = Op(eng, (lambda e, out=out, in_=in_, kw=kw: e.dma_start(out=out, in_=in_, **kw)), True)
        o.needed = True
        self._track(o, reads, writes)
        self.ops[eng].append(o)
        self.dmas[eng].append(o)
        return o

    def pe(self, fn, reads=(), writes=()):
        return self.op("pe", fn, reads, writes)

    def act(self, fn, reads=(), writes=()):
        return self.op("act", fn, reads, writes)

    def dve(self, fn, reads=(), writes=()):
        return self.op("dve", fn, reads, writes)

    def pool(self, fn, reads=(), writes=()):
        return self.op("pool", fn, reads, writes)

    def emit(self):
        nc = self.nc
        csem = {e: nc.alloc_semaphore(name=f"c_{e}") for e in ENGS}
        dsem = {e: [nc.alloc_semaphore(name=f"d_{e}_{j}") for j in range(NDMASEM)] for e in ("sp", "pool")}
        last_on = {}
        for e in ENGS:
            n = 0
            nd = 0
            for o in self.ops[e]:
                if o.is_dma:
                    j = nd % NDMASEM
                    o.sem = dsem[e][j]
                    o.val = 16 * (nd // NDMASEM + 1)
                    if (e, j) in last_on:
                        o.deps.append(last_on[(e, j)])
                    last_on[(e, j)] = o
                    nd += 1
                elif o.needed:
                    n += 1
                    o.sem = csem[e]
                    o.val = n
        finals = list(last_on.values())
        stats = {}
        with nc.Block() as block:
            def body(e):
                def run(eng):
                    seen = {}
                    nw = 0
                    for o in self.ops[e]:
                        for d in o.deps:
                            k = id(d.sem)
                            if seen.get(k, 0) < d.val:
                                eng.wait_ge(d.sem, d.val)
                                seen[k] = d.val
                                nw += 1
                        ins = o.fn(eng)
                        if o.is_dma:
                            ins.then_inc(o.sem, 16)
                        elif o.needed:
                            ins.then_inc(o.sem, 1)
                    if e == "sp":
                        for o in finals:
                            k = id(o.sem)
                            if seen.get(k, 0) < o.val:
                                eng.wait_ge(o.sem, o.val)
                                seen[k] = o.val
                    stats[e] = (len(self.ops[e]), nw)
                return run
            block.tensor(body("pe"))
            block.scalar(body("act"))
            block.vector(body("dve"))
            block.gpsimd(body("pool"))
            block.sync(body("sp"))
        return stats


class Ring:
    def __init__(self, views):
        self.views = views
        self.res = [Res() for _ in views]
        self.i = 0

    def next(self):
        k = self.i % len(self.views)
        self.i += 1
        return self.views[k], self.res[k]


def build(SP, SS, NQ, debug=False):
    assert SP % 512 == 0 and SS % 512 == 0 and NQ % 512 == 0 and SS >= NQ + 1024
    R = SP + SS
    NM = SP + NQ
    NBP, NBS = SP // 128, SS // 128
    HB = 32 + SP
    HC = HB + NQ + 256
    nc = bass.Bass("TRN2", target_bir_lowering=False)
    S = Sched(nc)

    def din(name, shape):
        return nc.dram_tensor(name, shape, F32, kind="ExternalInput").ap()

    def dscr(name, shape, dt):
        if debug:
            return nc.dram_tensor(name, shape, dt, kind="ExternalOutput").ap()
        return nc.dram_tensor(name, shape, dt).ap()

    xr = din("xr", [R, D])
    w_ffn = [(din("ffn1_gu", [D, 2 * DFF]), din("ffn1_d", [DFF, D])),
             (din("ffn2_gu", [D, 2 * DFF]), din("ffn2_d", [DFF, D]))]
    w_in = din("w_in", [D, PROJ])
    w_co = din("w_co", [512, D])
    w_ao = din("w_ao", [512, D])
    w_o = din("w_o", [D, D])
    lnv = din("lnv", [6, D])
    NCOL = 16 + 124 + 12 + 4 + 1
    colp = din("colp", [128, NCOL])
    wrapm = din("wrapm", [128, NBS])
    sublg = din("sublg", [1, 128])
    lamv = din("lamv", [1, 256])
    tab = din("tab", [32, 4])
    oh = din("oh", [32, LB])
    ident = din("ident", [128, 128])
    yo = nc.dram_tensor("yo", [NM, D], F32, kind="ExternalOutput").ap()

    x1 = dscr("x1", [R, D], F32)
    hT = dscr("hT", [512, HC], F32)
    QT = dscr("QT", [4, 128, NM], BF16)
    KT = dscr("KT", [4, 128, R], BF16)
    Vs = dscr("Vs", [R, 4 * 129], BF16)
    brep = dscr("brep", [4, 128, LB], F32)
    cactT = dscr("cactT", [512, NM], BF16)
    attnT = dscr("attnT", [512, NM], BF16)
    x2 = dscr("x2", [NM, D], F32)
    r_x1, r_hT, r_QT, r_KT, r_Vs, r_brep, r_cact, r_attn, r_x2 = (Res() for _ in range(9))

    def dram_ap(t, offset, pattern):
        return bass.AP(t.tensor, offset, pattern)

    def bcast_rows(src, row, n, parts=128):
        return dram_ap(src, row * src.shape[1], [[0, parts], [1, n]])

    with contextlib.ExitStack() as top:
        def SB(st, name, shape, dt):
            return st.enter_context(nc.sbuf_tensor(name, shape, dt))

        ps = top.enter_context(nc.psum_tensor("ps", [128, 7, 512], F32))
        ps_res = [Res(f"ps{b}") for b in range(7)]
        r_pstT = Res("pstT")
        PSX = {}

        def alloc_pst(st, tag):
            PSX["t"] = st.enter_context(nc.psum_tensor("pstT" + tag, [128, 1024], BF16))

        idf = SB(top, "idf", [128, 128], F32)
        idb = SB(top, "idb", [128, 128], BF16)
        colt = SB(top, "colt", [128, NCOL], F32)
        lam = SB(top, "lam", [128, 8], F32)
        r_idf, r_idb, r_colt, r_lam = Res(), Res(), Res(), Res()
        S.dma("sp", idf[:], ident[:, :], writes=[r_idf])
        S.dma("sp", colt[:], colp[:, :], writes=[r_colt])
        S.dve(lambda e: e.tensor_copy(out=idb[:], in_=idf[:]), reads=[r_idf], writes=[r_idb])
        C_BG, C_CW, C_CB, C_CG, C_CBE, C_MK, C_SG = 0, 16, 140, 144, 148, 152, 156

        def colv(c):
            return colt[:, c:c + 1]

        with contextlib.ExitStack() as st:
            lv = SB(st, "lv", [128, 256], F32)
            lp = SB(st, "lp", [128, 128], F32)
            r_lv, r_lp = Res(), Res()
            S.dma("sp", lv[:], bcast_rows(lamv, 0, 256), writes=[r_lv])
            S.dve(lambda e: e.tensor_tensor(out=lp[:, 0:64], in0=lv[:, 0:64], in1=lv[:, 64:128], op=ALU.mult), reads=[r_lv], writes=[r_lp])
            S.dve(lambda e: e.tensor_tensor(out=lp[:, 64:128], in0=lv[:, 128:192], in1=lv[:, 192:256], op=ALU.mult), reads=[r_lv, r_lp], writes=[r_lp])
            S.dve(lambda e: e.reduce_sum(out=lam[:, 0:1], in_=lp[:, 0:64], axis=AX.X), reads=[r_lp], writes=[r_lam])
            S.dve(lambda e: e.reduce_sum(out=lam[:, 1:2], in_=lp[:, 64:128], axis=AX.X), reads=[r_lp, r_lam], writes=[r_lam])
            S.act(lambda e: e.activation(out=lam[:, 2:4], in_=lam[:, 0:2], func=AF.Exp), reads=[r_lam], writes=[r_lam])
            S.dve(lambda e: e.tensor_tensor(out=lam[:, 4:5], in0=lam[:, 3:4], in1=lam[:, 2:3], op=ALU.subtract), reads=[r_lam], writes=[r_lam])
            S.dve(lambda e: e.tensor_scalar_add(out=lam[:, 5:6], in0=lam[:, 4:5], scalar1=-LAMBDA_INIT), reads=[r_lam], writes=[r_lam])
            tb = SB(st, "tb", [32, 4, 128], F32)
            oht = SB(st, "oht", [32, LB], F32)
            gb = SB(st, "gb", [128, LB], F32)
            r_tb, r_oht, r_gb = Res(), Res(), Res()
            S.dma("sp", oht[:], oh[:, :], writes=[r_oht])
            tabt = SB(st, "tabt", [32, 4], F32)
            r_tabt = Res()
            S.dma("sp", tabt[:], tab[:, :], writes=[r_tabt])
            S.pool(lambda e: e.memset(tb[:], 1.0), writes=[r_tb])
            for h in range(4):
                S.dve(lambda e, h=h: e.tensor_scalar_mul(out=tb[:, h, :], in0=tb[:, h, :], scalar1=tabt[:, h:h + 1]), reads=[r_tb, r_tabt], writes=[r_tb])
            for h in range(4):
                for (c0, c1) in ((0, 512), (512, 1024), (1024, LB)):
                    S.pe(lambda e, h=h, c0=c0, c1=c1: e.matmul(ps[:, 0, 0:c1 - c0], tb[:, h, :], oht[:, c0:c1], start=True, stop=True),
                         reads=[r_tb, r_oht], writes=[ps_res[0]])
                    S.dve(lambda e, c0=c0, c1=c1: e.tensor_copy(out=gb[:, c0:c1], in_=ps[:, 0, 0:c1 - c0]), reads=[ps_res[0]], writes=[r_gb])
                S.dma("sp", brep[h], gb[:], reads=[r_gb], writes=[r_brep])

        S.full_barrier()

        def load_weight(st, name, src, K, N, queue="pool"):
            kc = K // 128
            t = SB(st, name, [128, kc, N], BF16)
            r = Res(name)
            for c in range(kc):
                S.dma(queue, t[:, c, :], src[c * 128:(c + 1) * 128, :], writes=[r])
            return t, r

        def load_T(rows_ap_fn, nsub, xin_ring, xbf_ring, xT, r_xT, tbank, r_src=None):
            for s in range(nsub):
                xin, r_xin = xin_ring.next()
                xbf, r_xbf = xbf_ring.next()
                S.dma("sp", xin, rows_ap_fn(s), writes=[r_xin])
                S.act(lambda e, xbf=xbf, xin=xin: e.copy(out=xbf, in_=xin), reads=[r_xin], writes=[r_xbf])
                pst = PSX["t"][:, :]
                for c in range(8):
                    S.pe(lambda e, c=c, xbf=xbf, pst=pst: e.transpose(pst[:, c * 128:(c + 1) * 128], xbf[:, c * 128:(c + 1) * 128], idb[:]),
                         reads=[r_xbf, r_idb], writes=[r_pstT])
                S.dve(lambda e, s=s, pst=pst: e.tensor_copy(out=xT[:, :, s * 128:(s + 1) * 128],
                                                           in_=pst.rearrange("p (c t) -> p c t", c=8)),
                      reads=[r_pstT], writes=[r_xT])

        def layer_norm_store(st_tiles, z, r_z, gB, bB, r_gb, dst_ap):
            stt, r_stt, mv, r_mv = st_tiles
            S.dve(lambda e: e.bn_stats(out=stt[:, 0:6], in_=z[:, 0:512]), reads=[r_z], writes=[r_stt])
            S.dve(lambda e: e.bn_stats(out=stt[:, 6:12], in_=z[:, 512:1024]), reads=[r_z, r_stt], writes=[r_stt])
            S.dve(lambda e: e.bn_aggr(out=mv[:, 0:2], in_=stt[:, 0:12]), reads=[r_stt], writes=[r_mv])
            S.act(lambda e: e.activation(out=mv[:, 2:3], in_=mv[:, 1:2], func=AF.Sqrt, bias=epsc[:, 0:1], scale=1.0), reads=[r_mv, r_eps], writes=[r_mv])
            S.dve(lambda e: e.reciprocal(out=mv[:, 3:4], in_=mv[:, 2:3]), reads=[r_mv], writes=[r_mv])
            S.dve(lambda e: e.tensor_scalar(out=z, in0=z, scalar1=mv[:, 0:1], scalar2=mv[:, 3:4], op0=ALU.subtract, op1=ALU.mult),
                  reads=[r_z, r_mv], writes=[r_z])
            S.pool(lambda e: e.tensor_tensor(out=z, in0=z, in1=gB[:], op=ALU.mult), reads=[r_z, r_gb], writes=[r_z])
            S.pool(lambda e: e.tensor_tensor(out=z, in0=z, in1=bB[:], op=ALU.add), reads=[r_z, r_gb], writes=[r_z])

        cbt = SB(top, "cbt", [128, 12], F32)
        r_cbt = Res()
        S.dma("sp", cbt[:, 0:4], bcast_rows(tab, 15, 4), writes=[r_cbt])
        S.dma("sp", cbt[:, 4:8], bcast_rows(tab, 31, 4), writes=[r_cbt])
        S.dve(lambda e: e.tensor_tensor(out=cbt[:, 8:12], in0=cbt[:, 0:4], in1=cbt[:, 4:8], op=ALU.subtract), reads=[r_cbt], writes=[r_cbt])
        epsc = SB(top, "epsc", [128, 1], F32)
        r_eps = Res()
        S.pool(lambda e: e.memset(epsc[:], LN_EPS), writes=[r_eps])

        def ffn_phase(tag, src, r_src, nrows, wgu_d, wd_d, ln_row, dst, r_dst):
            with contextlib.ExitStack() as st:
                alloc_pst(st, tag)
                wgu, r_wgu = load_weight(st, "wgu" + tag, wgu_d, D, 2 * DFF)
                wd, r_wd = load_weight(st, "wd" + tag, wd_d, DFF, D)
                gB = SB(st, "gB" + tag, [128, D], F32)
                bB = SB(st, "bB" + tag, [128, D], F32)
                r_gb = Res()
                S.dma("sp", gB[:], bcast_rows(lnv, ln_row, D), writes=[r_gb])
                S.dma("sp", bB[:], bcast_rows(lnv, ln_row + 1, D), writes=[r_gb])
                xin_t = SB(st, "xin" + tag, [128, 2, D], F32)
                xbf_t = SB(st, "xbf" + tag, [128, 2, D], BF16)
                xin_ring = Ring([xin_t[:, k, :] for k in range(2)])
                xbf_ring = Ring([xbf_t[:, k, :] for k in range(2)])
                xT = SB(st, "xT" + tag, [128, 8, 512], BF16)
                r_xT = Res()
                actT = SB(st, "actT" + tag, [128, NFC, 512], BF16)
                r_act = [Res() for _ in range(NFC)]
                sa_t = SB(st, "sa" + tag, [128, 2, 512], F32)
                sa_ring = Ring([sa_t[:, k, :] for k in range(2)])
                z_t = SB(st, "z" + tag, [128, 3, D], F32)
                z_ring = Ring([z_t[:, k, :] for k in range(3)])
                stt_t = SB(st, "stt" + tag, [128, 2, 12], F32)
                mv_t = SB(st, "mv" + tag, [128, 2, 4], F32)
                st_ring = Ring([(stt_t[:, k, :], mv_t[:, k, :]) for k in range(2)])
                st_res2 = [(Res(), Res()) for _ in range(2)]
                ntile = nrows // 512
                load_T(lambda s: src[s * 128:(s + 1) * 128, :], 4, xin_ring, xbf_ring, xT, r_xT, 6)
                for t in range(ntile):
                    r0 = t * 512
                    for f in range(NFC):
                        ba, bg = (0, 1) if f % 2 == 0 else (2, 3)
                        for dd in range(8):
                            S.pe(lambda e, f=f, dd=dd, ba=ba: e.matmul(ps[:, ba, :], wgu[:, dd, f * 128:(f + 1) * 128], xT[:, dd, :], start=(dd == 0), stop=(dd == 7)),
                                 reads=[r_wgu, r_xT], writes=[ps_res[ba]])
                        for dd in range(8):
                            S.pe(lambda e, f=f, dd=dd, bg=bg: e.matmul(ps[:, bg, :], wgu[:, dd, DFF + f * 128:DFF + (f + 1) * 128], xT[:, dd, :], start=(dd == 0), stop=(dd == 7)),
                                 reads=[r_wgu, r_xT], writes=[ps_res[bg]])
                        sa, r_sa = sa_ring.next()
                        S.act(lambda e, sa=sa, ba=ba: e.activation(out=sa, in_=ps[:, ba, :], func=AF.Silu), reads=[ps_res[ba]], writes=[r_sa])
                        S.dve(lambda e, sa=sa, bg=bg, f=f: e.tensor_tensor(out=actT[:, f, :], in0=sa, in1=ps[:, bg, :], op=ALU.mult),
                              reads=[r_sa, ps_res[bg]], writes=[r_act[f]])
                    if t + 1 < ntile:
                        load_T(lambda s, r1=r0 + 512: src[r1 + s * 128:r1 + (s + 1) * 128, :], 4, xin_ring, xbf_ring, xT, r_xT, 6)
                    for s in range(4):
                        z, r_z = z_ring.next()
                        rows = slice(r0 + s * 128, r0 + (s + 1) * 128)
                        S.dma("sp", z, src[rows, :], reads=[r_src], writes=[r_z])
                        S.act(lambda e, z=z: e.mul(out=z, in_=z, mul=ALPHA), reads=[r_z], writes=[r_z])
                        for half in range(2):
                            b = 4 + half
                            for f in range(NFC):
                                S.pe(lambda e, f=f, s=s, half=half, b=b: e.matmul(ps[:, b, :], actT[:, f, s * 128:(s + 1) * 128], wd[:, f, half * 512:(half + 1) * 512], start=(f == 0), stop=(f == NFC - 1)),
                                     reads=[r_act[f], r_wd], writes=[ps_res[b]])
                        S.dve(lambda e, z=z: e.scalar_tensor_tensor(out=z, in0=ps[:, 4:6, :].rearrange("p a b -> p (a b)"), scalar=0.5, in1=z, op0=ALU.mult, op1=ALU.add),
                              reads=[ps_res[4], ps_res[5], r_z], writes=[r_z])
                        (stt, mv), _ = st_ring.next()
                        k = (st_ring.i - 1) % 2
                        layer_norm_store((stt, st_res2[k][0], mv, st_res2[k][1]), z, r_z, gB, bB, r_gb, None)
                        S.dma("pool", dst[rows, :], z, reads=[r_z], writes=[r_dst])

        r_xr = Res()
        ffn_phase("f1", xr, r_xr, R, w_ffn[0][0], w_ffn[0][1], 0, x1, r_x1)

        S.full_barrier()
        with contextlib.ExitStack() as st:
            alloc_pst(st, "p2")
            win, r_win = load_weight(st, "win", w_in[:, 0:COL_G], D, COL_G)
            xin_t = SB(st, "xin2", [128, 2, D], F32)
            xbf_t = SB(st, "xbf2", [128, 2, D], BF16)
            xin_ring = Ring([xin_t[:, k, :] for k in range(2)])
            xbf_ring = Ring([xbf_t[:, k, :] for k in range(2)])
            xT2_t = SB(st, "xT2", [128, 2, 8, 512], BF16)
            xT2_ring = Ring([xT2_t[:, k, :, :] for k in range(2)])
            sg_t = SB(st, "sg2", [128, 2, 512], F32)
            sg_ring = Ring([sg_t[:, k, :] for k in range(2)])
            ho_t = SB(st, "ho2", [128, 3, 512], F32)
            ho_ring = Ring([ho_t[:, k, :] for k in range(3)])
            qk_t = SB(st, "qk2", [128, 3, 512], BF16)
            qk_ring = Ring([qk_t[:, k, :] for k in range(3)])
            v_t = SB(st, "v2", [128, 2, 4, 129], BF16)
            v_ring = Ring([v_t[:, k, :, :] for k in range(2)])
            zt = SB(st, "zero2", [128, 16], F32)
            r_zt = Res()
            S.pool(lambda e: e.memset(zt[:], 0.0), writes=[r_zt])
            S.pool(lambda e: e.memset(v_t[:], 1.0), writes=v_ring.res)
            for c in range(4):
                S.dma("sp", hT[c * 128:(c + 1) * 128, 0:16], zt[:], reads=[r_zt], writes=[r_hT])
                S.dma("sp", hT[c * 128:(c + 1) * 128, 16 + SP:32 + SP], zt[:], reads=[r_zt], writes=[r_hT])
            xT_next = xT2_ring.next()
            load_T(lambda s: x1[s * 128:(s + 1) * 128, :], 4, xin_ring, xbf_ring, xT_next[0], xT_next[1], 6)
            for t in range(R // 512):
                r0 = t * 512
                xT, r_xT = xT_next
                if r0 + 512 < R:
                    xT_next = xT2_ring.next()
                    load_T(lambda s, r1=r0 + 512: x1[r1 + s * 128:r1 + (s + 1) * 128, :], 4, xin_ring, xbf_ring, xT_next[0], xT_next[1], 6)
                samp = r0 >= SP
                rs = r0 - SP
                need_q = (not samp) or rs < NQ
                if not samp:
                    need_u, hcol, ucols, mask = True, 16 + r0, (0, 512), None
                elif rs < NQ:
                    need_u, hcol, ucols, mask = True, HB + 128 + rs, (0, 512), None
                elif rs == NQ:
                    need_u, hcol, ucols, mask = True, HB + 128 + NQ, (0, 128), C_MK + 1
                elif rs == SS - 512:
                    need_u, hcol, ucols, mask = True, HB - 384, (384, 512), C_MK + 0
                else:
                    need_u = False
                if False:
                    dxT = nc.dram_tensor("dbg_xT", [128, 8 * 512], BF16, kind="ExternalOutput").ap()
                    dwin = nc.dram_tensor("dbg_win", [128, 2560], BF16, kind="ExternalOutput").ap()
                    dxin = nc.dram_tensor("dbg_xin", [128, 1024], F32, kind="ExternalOutput").ap()
                    S.dma("sp", dxT[:, :], xT[:].rearrange("p c t -> p (c t)"), reads=[r_xT])
                    S.dma("sp", dwin[:, :], win[:, 0, :], reads=[r_win])
                    S.dma("sp", dxin[:, :], xin_t[:, 1, :], reads=[xin_ring.res[1]])
                nb = [0]

                def bank():
                    b = nb[0] % 4
                    nb[0] += 1
                    return b

                def proj_fm(col0):
                    b = bank()
                    for dd in range(8):
                        S.pe(lambda e, dd=dd, b=b, col0=col0, xT=xT: e.matmul(ps[:, b, :], win[:, dd, col0:col0 + 128], xT[:, dd, :], start=(dd == 0), stop=(dd == 7)),
                             reads=[r_win, r_xT], writes=[ps_res[b]])
                    return b

                if need_u:
                    for j in range(4):
                        ba = proj_fm(j * 128)
                        bg = proj_fm(512 + j * 128)
                        sg, r_sg = sg_ring.next()
                        ho, r_ho = ho_ring.next()
                        S.act(lambda e, sg=sg, bg=bg: e.activation(out=sg, in_=ps[:, bg, :], func=AF.Sigmoid), reads=[ps_res[bg]], writes=[r_sg])
                        S.dve(lambda e, sg=sg, ho=ho, ba=ba: e.tensor_tensor(out=ho, in0=sg, in1=ps[:, ba, :], op=ALU.mult), reads=[r_sg, ps_res[ba]], writes=[r_ho])
                        if mask is not None:
                            S.dve(lambda e, ho=ho, mask=mask: e.tensor_scalar_mul(out=ho, in0=ho, scalar1=colv(mask)), reads=[r_ho, r_colt], writes=[r_ho])
                        u0, u1 = ucols
                        S.dma("pool", hT[j * 128:(j + 1) * 128, hcol + u0:hcol + u1], ho[:, u0:u1], reads=[r_ho], writes=[r_hT])
                for (need, col0, dstT, r_d, cbase) in ((need_q, COL_Q, QT, r_QT, (r0 if not samp else SP + rs)), (True, COL_K, KT, r_KT, r0)):
                    if not need:
                        continue
                    for h in range(4):
                        b = proj_fm(col0 + h * 128)
                        qk, r_qk = qk_ring.next()
                        S.act(lambda e, qk=qk, b=b: e.copy(out=qk, in_=ps[:, b, :]), reads=[ps_res[b]], writes=[r_qk])
                        S.dma("pool", dstT[h, :, cbase:cbase + 512], qk, reads=[r_qk], writes=[r_d])
                for s in range(4):
                    b = bank()
                    for dd in range(8):
                        S.pe(lambda e, dd=dd, b=b, s=s, xT=xT: e.matmul(ps[:, b, :], xT[:, dd, s * 128:(s + 1) * 128], win[:, dd, COL_V:COL_V + 512], start=(dd == 0), stop=(dd == 7)),
                             reads=[r_win, r_xT], writes=[ps_res[b]])
                    vt, r_vt = v_ring.next()
                    S.dve(lambda e, vt=vt, b=b: e.tensor_copy(out=vt[:, :, 0:128], in_=ps[:, b, :].rearrange("p (h e) -> p h e", h=4)), reads=[ps_res[b]], writes=[r_vt])
                    S.dma("pool", Vs[r0 + s * 128:r0 + (s + 1) * 128, :], vt.rearrange("p h e -> p (h e)"), reads=[r_vt], writes=[r_Vs])

        S.full_barrier()
        with contextlib.ExitStack() as st:
            hin_t = SB(st, "hin", [128, 2, 4, 544], F32)
            hin_ring = Ring([hin_t[:, k, :, :] for k in range(2)])
            acc = SB(st, "cacc", [128, 4, 512], F32)
            r_acc = [Res() for _ in range(4)]
            sq = SB(st, "csq", [128, 4, 512], F32)
            r_sq = [Res() for _ in range(4)]
            onesm = SB(st, "onesm", [128, 128], F32)
            r_ones = Res()
            S.pool(lambda e: e.memset(onesm[:], 1.0 / 512.0), writes=[r_ones])
            mean = SB(st, "cmean", [128, 512], F32)
            rstd = SB(st, "crstd", [128, 512], F32)
            r_mean, r_rstd = Res(), Res()
            co_t = SB(st, "cout", [128, 3, 512], BF16)
            co_ring = Ring([co_t[:, k, :] for k in range(3)])
            NPE = 17
            cdiag = SB(st, "cdiag", [128, 4, NPE, 128], F32)
            r_cdiag = Res()
            for c in range(4):
                for j in range(NPE):
                    S.dve(lambda e, c=c, j=j: e.tensor_scalar_mul(out=cdiag[:, c, j, :], in0=idf[:], scalar1=colv(C_CW + c * 31 + j)), reads=[r_idf, r_colt], writes=[r_cdiag])
            tiles = [(16 + t * 512, t * 512) for t in range(SP // 512)] + [(HB + 128 + t * 512, SP + t * 512) for t in range(NQ // 512)]
            for (hc, mrow) in tiles:
                hin, r_hin = hin_ring.next()
                for c in range(4):
                    S.dma("sp", hin[:, c, 0:542], hT[c * 128:(c + 1) * 128, hc - 15:hc + 527], reads=[r_hT], writes=[r_hin])
                for c in range(4):
                    for j in range(NPE):
                        S.pe(lambda e, c=c, j=j, hin=hin: e.matmul(ps[:, c, :], cdiag[:, c, j, :], hin[:, c, j:j + 512], start=(j == 0), stop=(j == NPE - 1)),
                             reads=[r_cdiag, r_hin], writes=[ps_res[c]])
                for j in range(NPE, 31):
                    for c in range(4):
                        if j == NPE:
                            S.dve(lambda e, c=c, j=j, hin=hin: e.tensor_scalar(out=acc[:, c, :], in0=hin[:, c, j:j + 512], scalar1=colv(C_CW + c * 31 + j), scalar2=colv(C_CB + c), op0=ALU.mult, op1=ALU.add),
                                  reads=[r_hin, r_colt], writes=[r_acc[c]])
                        else:
                            S.dve(lambda e, c=c, j=j, hin=hin: e.scalar_tensor_tensor(out=acc[:, c, :], in0=hin[:, c, j:j + 512], scalar=colv(C_CW + c * 31 + j), in1=acc[:, c, :], op0=ALU.mult, op1=ALU.add),
                                  reads=[r_hin, r_colt, r_acc[c]], writes=[r_acc[c]])
                for c in range(4):
                    S.dve(lambda e, c=c: e.tensor_tensor(out=acc[:, c, :], in0=acc[:, c, :], in1=ps[:, c, :], op=ALU.add), reads=[r_acc[c], ps_res[c]], writes=[r_acc[c]])
                for c in range(4):
                    S.pool(lambda e, c=c: e.tensor_tensor(out=sq[:, c, :], in0=acc[:, c, :], in1=acc[:, c, :], op=ALU.mult), reads=[r_acc[c]], writes=[r_sq[c]])
                for c in range(4):
                    S.pe(lambda e, c=c: e.matmul(ps[:, 4, :], onesm[:], acc[:, c, :], start=(c == 0), stop=(c == 3)), reads=[r_ones, r_acc[c]], writes=[ps_res[4]])
                for c in range(4):
                    S.pe(lambda e, c=c: e.matmul(ps[:, 5, :], onesm[:], sq[:, c, :], start=(c == 0), stop=(c == 3)), reads=[r_ones, r_sq[c]], writes=[ps_res[5]])
                S.act(lambda e: e.copy(out=mean[:], in_=ps[:, 4, :]), reads=[ps_res[4]], writes=[r_mean])
                S.pool(lambda e: e.tensor_tensor(out=rstd[:], in0=mean[:], in1=mean[:], op=ALU.mult), reads=[r_mean], writes=[r_rstd])
                S.dve(lambda e: e.tensor_tensor(out=rstd[:], in0=ps[:, 5, :], in1=rstd[:], op=ALU.subtract), reads=[ps_res[5], r_rstd], writes=[r_rstd])
                S.act(lambda e: e.activation(out=rstd[:], in_=rstd[:], func=AF.Sqrt, bias=epsc[:, 0:1], scale=1.0), reads=[r_rstd, r_eps], writes=[r_rstd])
                S.dve(lambda e: e.reciprocal(out=rstd[:], in_=rstd[:]), reads=[r_rstd], writes=[r_rstd])
                for c in range(4):
                    S.pool(lambda e, c=c: e.tensor_tensor(out=acc[:, c, :], in0=acc[:, c, :], in1=mean[:], op=ALU.subtract), reads=[r_acc[c], r_mean], writes=[r_acc[c]])
                    S.dve(lambda e, c=c: e.tensor_tensor(out=acc[:, c, :], in0=acc[:, c, :], in1=rstd[:], op=ALU.mult), reads=[r_acc[c], r_rstd], writes=[r_acc[c]])
                    co, r_co = co_ring.next()
                    S.act(lambda e, c=c, co=co: e.activation(out=co, in_=acc[:, c, :], func=AF.Silu, bias=colv(C_CBE + c), scale=colv(C_CG + c)), reads=[r_acc[c], r_colt], writes=[r_co])
                    S.dma("pool", cactT[c * 128:(c + 1) * 128, mrow:mrow + 512], co, reads=[r_co], writes=[r_cact])

        S.full_barrier()
        with contextlib.ExitStack() as st:
            KMAX = max(SP, SS)
            kt_buf = SB(st, "kt", [128, 2, KMAX], BF16)
            vh_buf = SB(st, "vh", [128, 2, KMAX // 128, 128], BF16)
            kv_res = [(Res(), Res()), (Res(), Res())]
            bt = SB(st, "bt", [128, 8, 512], F32)
            r_bt = Res()
            farb = SB(st, "farb", [128, NBS], F32)
            r_farb = Res()
            wm = SB(st, "wm", [128, NBS], F32)
            r_wm = Res()
            S.dma("sp", wm[:], wrapm[:, :], writes=[r_wm])
            gcol = SB(st, "gcol", [128, 1], F32)
            r_gcol = Res()
            S.dve(lambda e: e.tensor_scalar_mul(out=gcol[:], in0=colv(C_SG), scalar1=1.0 - LAMBDA_INIT), reads=[r_colt], writes=[r_gcol])
            ps7 = st.enter_context(nc.psum_tensor("ps7", [128, 512], F32))
            r_ps7 = Res()
            onesF = SB(st, "onesF", [128, 128], F32)
            onesB = SB(st, "onesB", [128, 128], BF16)
            r_onesF = Res()
            S.pool(lambda e: e.memset(onesF[:], 1.0), writes=[r_onesF])
            S.pool(lambda e: e.memset(onesB[:], 1.0), writes=[r_onesF])
            qt_t = SB(st, "qt", [128, 2, 512], BF16)
            qt_ring = Ring([qt_t[:, k, :] for k in range(2)])
            pt_t = SB(st, "pt", [128, 4, 2, 512], BF16)
            pt_ring = Ring([pt_t[:, k, :, :] for k in range(4)])
            tmp_t = SB(st, "stmp", [128, 2, 2, 512], F32)
            tmp_ring = Ring([tmp_t[:, k, :, :] for k in range(2)])
            dacc = SB(st, "dacc", [128, 1, 2, 512], F32)
            r_dacc = [Res(), Res()]
            rec = SB(st, "rec", [128, 2, 512], F32)
            r_rec = [Res(), Res()]
            osb = SB(st, "osb", [128, 2, 512], F32)
            r_osb = [Res(), Res()]
            aa_ = SB(st, "a4", [128, 512], F32)
            bb_ = SB(st, "b4", [128, 512], F32)
            sq_ = SB(st, "sq4", [128, 512], F32)
            rs_ = SB(st, "rs4", [128, 512], F32)
            r_aa, r_bb, r_sq4, r_rs = Res(), Res(), Res(), Res()
            at_t = SB(st, "at", [128, 2, 512], BF16)
            at_ring = Ring([at_t[:, k, :] for k in range(2)])
            units = [(0, SP, SP, 0, False, h) for h in range(4)] + [(SP, SS, NQ, SP, True, h) for h in range(4)]
            def load_kv(u):
                (krow0, S_k, nq, qcol0, samp, h) = units[u]
                NB = S_k // 128
                kt_u, vh_u = kt_buf[:, u % 2, :], vh_buf[:, u % 2, :, :]
                r_kt_u, r_vh_u = kv_res[u % 2]
                S.dma("sp", kt_u[:, 0:S_k], KT[h, :, krow0:krow0 + S_k], reads=[r_KT], writes=[r_kt_u])
                nvc = max(1, NB // 16)
                for vc in range(nvc):
                    b0, b1 = vc * NB // nvc, (vc + 1) * NB // nvc
                    S.dma("sp", vh_u[:, b0:b1, :],
                          dram_ap(Vs, (krow0 + b0 * 128) * 516 + h * 129, [[516, 128], [128 * 516, b1 - b0], [1, 128]]),
                          reads=[r_Vs], writes=[r_vh_u])

            load_kv(0)
            for u, (krow0, S_k, nq, qcol0, samp, h) in enumerate(units):
                NB = S_k // 128
                nqt = nq // 512
                kt_t, vh_t = kt_buf[:, u % 2, :], vh_buf[:, u % 2, :, :]
                r_kt, r_vh = kv_res[u % 2]
                for kk in range(6):
                    S.dma("sp", bt[:, kk, :], dram_ap(brep, h * 128 * LB + 639 - 128 * (kk - 1), [[LB - 1, 128], [1, 512]]), reads=[r_brep], writes=[r_bt])
                if u + 1 < len(units):
                    load_kv(u + 1)
                cneg, cpos, cdif = cbt[:, h:h + 1], cbt[:, 4 + h:5 + h], cbt[:, 8 + h:9 + h]
                if samp:
                    S.dve(lambda e, cdif=cdif, cpos=cpos: e.tensor_scalar(out=farb[:], in0=wm[:], scalar1=cdif, scalar2=cpos, op0=ALU.mult, op1=ALU.add), reads=[r_wm, r_cbt], writes=[r_farb])
                    for (dst, srck, cc, mcol) in ((6, 0, cpos, C_MK + 2), (7, 5, cneg, C_MK + 3)):
                        S.dve(lambda e, dst=dst, srck=srck, cc=cc, mcol=mcol: e.tensor_scalar(out=bt[:, dst, :], in0=bt[:, srck, :], scalar1=cc, scalar2=colv(mcol), op0=ALU.subtract, op1=ALU.mult),
                              reads=[r_bt, r_cbt, r_colt], writes=[r_bt])
                        S.dve(lambda e, dst=dst, cc=cc: e.tensor_scalar_add(out=bt[:, dst, :], in0=bt[:, dst, :], scalar1=cc), reads=[r_bt, r_cbt], writes=[r_bt])
                for i in range(nqt):
                    qt, r_qt = qt_ring.next()
                    S.dma("sp", qt, QT[h, :, qcol0 + i * 512:qcol0 + (i + 1) * 512], reads=[r_QT], writes=[r_qt])

                    def kind(j):
                        rel = j - 4 * i
                        if samp and i == 0 and j == NB - 1:
                            return ("near", 6)
                        if samp and i == nqt - 1 and rel == 4:
                            return ("near", 7)
                        if -1 <= rel <= 4:
                            return ("near", rel + 1)
                        if rel < -1:
                            return ("far", cneg, r_cbt)
                        if samp:
                            return ("far", farb[:, j:j + 1], r_farb)
                        return ("far", cpos, r_cbt)

                    pend = {}

                    def qk(j):
                        p = 2 * (j % 2)
                        for m in range(2):
                            S.pe(lambda e, j=j, m=m, p=p, qt=qt, kt_t=kt_t: e.matmul(ps[:, p + m, :], kt_t[m * 64:(m + 1) * 64, j * 128:(j + 1) * 128], qt[m * 64:(m + 1) * 64, :], start=True, stop=True),
                                 reads=[r_kt, r_qt], writes=[ps_res[p + m]])
                        kd = kind(j)
                        pt, r_pt = pt_ring.next()
                        if kd[0] == "near":
                            tmp, r_tmp = tmp_ring.next()
                            for m in range(2):
                                S.dve(lambda e, m=m, p=p, tmp=tmp, kk=kd[1]: e.scalar_tensor_tensor(out=tmp[:, m, :], in0=ps[:, p + m, :], scalar=0.125, in1=bt[:, kk, :], op0=ALU.mult, op1=ALU.add),
                                      reads=[ps_res[p + m], r_bt, r_tmp], writes=[r_tmp])
                            S.act(lambda e, pt=pt, tmp=tmp: e.activation(out=pt, in_=tmp, func=AF.Exp), reads=[r_tmp], writes=[r_pt])
                        else:
                            S.act(lambda e, pt=pt, p=p, bc=kd[1]: e.activation(out=pt, in_=ps[:, p:p + 2, :], func=AF.Exp, bias=bc, scale=0.125),
                                  reads=[ps_res[p], ps_res[p + 1], kd[2]], writes=[r_pt])
                        pend[j] = (pt, r_pt)

                    def avmm(j):
                        pt, r_pt = pend.pop(j)
                        for m in range(2):
                            S.pe(lambda e, j=j, m=m, pt=pt, vh_t=vh_t, st_=(j == 0), sp_=(j == NB - 1): e.matmul(ps[:, 4 + m, :], vh_t[:, j, :], pt[:, m, :], start=st_, stop=sp_),
                                 reads=[r_pt, r_vh], writes=[ps_res[4 + m]])
                        if j % 2 == 0:
                            if j == 0:
                                S.dve(lambda e, pt=pt: e.tensor_copy(out=dacc[:, 0, :, :], in_=pt), reads=[r_pt], writes=[r_dacc[0]])
                            else:
                                S.dve(lambda e, pt=pt: e.tensor_tensor(out=dacc[:, 0, :, :], in0=dacc[:, 0, :, :], in1=pt, op=ALU.add), reads=[r_pt, r_dacc[0]], writes=[r_dacc[0]])
                        else:
                            for m in range(2):
                                dst = ps[:, 6, :] if m == 0 else ps7[:, :]
                                S.pe(lambda e, m=m, pt=pt, dst=dst, st_=(j == 1): e.matmul(dst, onesB[:], pt[:, m, :], start=st_, stop=False),
                                     reads=[r_pt, r_onesF], writes=[ps_res[6] if m == 0 else r_ps7])

                    qk(0)
                    qk(1)
                    for j in range(NB):
                        if j + 2 < NB:
                            qk(j + 2)
                        avmm(j)
                    for m in range(2):
                        S.act(lambda e, m=m: e.copy(out=osb[:, m, :], in_=ps[:, 4 + m, :]), reads=[ps_res[4 + m]], writes=[r_osb[m]])
                    for m in range(2):
                        dst = ps[:, 6, :] if m == 0 else ps7[:, :]
                        rd = ps_res[6] if m == 0 else r_ps7
                        S.pe(lambda e, m=m, dst=dst: e.matmul(dst, onesF[:], dacc[:, 0, m, :], start=False, stop=True), reads=[r_onesF, r_dacc[0]], writes=[rd])
                        S.dve(lambda e, m=m, dst=dst: e.reciprocal(out=rec[:, m, :], in_=dst), reads=[rd], writes=[r_rec[m]])
                    S.dve(lambda e: e.tensor_tensor(out=aa_[:], in0=osb[:, 0, :], in1=rec[:, 0, :], op=ALU.mult), reads=[r_osb[0], r_rec[0]], writes=[r_aa])
                    S.dve(lambda e: e.tensor_tensor(out=bb_[:], in0=osb[:, 1, :], in1=rec[:, 1, :], op=ALU.mult), reads=[r_osb[1], r_rec[1]], writes=[r_bb])
                    S.dve(lambda e: e.scalar_tensor_tensor(out=aa_[:], in0=bb_[:], scalar=lam[:, 5:6], in1=aa_[:], op0=ALU.mult, op1=ALU.add), reads=[r_bb, r_aa, r_lam], writes=[r_aa])
                    S.pool(lambda e: e.tensor_tensor(out=sq_[:], in0=aa_[:], in1=aa_[:], op=ALU.mult), reads=[r_aa], writes=[r_sq4])
                    S.pe(lambda e: e.matmul(ps[:, 6, :], onesF[:], sq_[:], start=True, stop=True), reads=[r_onesF, r_sq4], writes=[ps_res[6]])
                    S.act(lambda e: e.activation(out=rs_[:], in_=ps[:, 6, :], func=AF.Sqrt, bias=epsc[:, 0:1], scale=1.0 / 128.0), reads=[ps_res[6], r_eps], writes=[r_rs])
                    S.dve(lambda e: e.reciprocal(out=rs_[:], in_=rs_[:]), reads=[r_rs], writes=[r_rs])
                    at, r_at = at_ring.next()
                    S.dve(lambda e, at=at: e.scalar_tensor_tensor(out=at, in0=aa_[:], scalar=gcol[:, 0:1], in1=rs_[:], op0=ALU.mult, op1=ALU.mult), reads=[r_aa, r_gcol, r_rs], writes=[r_at])
                    S.dma("pool", attnT[h * 128:(h + 1) * 128, qcol0 + i * 512:qcol0 + (i + 1) * 512], at, reads=[r_at], writes=[r_attn])

        S.full_barrier()
        with contextlib.ExitStack() as st:
            alloc_pst(st, "p5")
            wg, r_wg = load_weight(st, "wg", w_in[:, COL_G:PROJ], D, 2 * D)
            wco, r_wco = load_weight(st, "wco", w_co, 512, D)
            wao, r_wao = load_weight(st, "wao", w_ao, 512, D)
            wo, r_wo = load_weight(st, "wo", w_o, D, D)
            gB = SB(st, "gB5", [128, D], F32)
            bB = SB(st, "bB5", [128, D], F32)
            r_gb = Res()
            S.dma("sp", gB[:], bcast_rows(lnv, 2, D), writes=[r_gb])
            S.dma("sp", bB[:], bcast_rows(lnv, 3, D), writes=[r_gb])
            xin_t = SB(st, "xin5", [128, 2, D], F32)
            xbf_t = SB(st, "xbf5", [128, 2, D], BF16)
            xin_ring = Ring([xin_t[:, k, :] for k in range(2)])
            xbf_ring = Ring([xbf_t[:, k, :] for k in range(2)])
            xT5v = SB(st, "xT5", [128, 8, 512], BF16)
            r_xT = Res()
            ca_t = SB(st, "ca5", [128, 2, 4, 512], BF16)
            ca_ring = Ring([ca_t[:, k, :, :] for k in range(2)])
            aa_t = SB(st, "aa5", [128, 2, 4, 512], BF16)
            aa_ring = Ring([aa_t[:, k, :, :] for k in range(2)])
            g_t = SB(st, "g5", [128, 4, 512], F32)
            g_ring = Ring([g_t[:, k, :] for k in range(4)])
            mT = SB(st, "mT5", [128, 8, 512], BF16)
            r_mT = [Res() for _ in range(8)]
            z_t = SB(st, "z5", [128, 3, D], F32)
            z_ring = Ring([z_t[:, k, :] for k in range(3)])
            stt_t = SB(st, "stt5", [128, 2, 12], F32)
            mv_t = SB(st, "mv5", [128, 2, 4], F32)
            st_res2 = [(Res(), Res()) for _ in range(2)]
            nst = 0
            load_T(lambda s: x1[s * 128:(s + 1) * 128, :], 4, xin_ring, xbf_ring, xT5v, r_xT, 6)
            for t in range(NM // 512):
                r0 = t * 512
                xrow = r0
                ca, r_ca = ca_ring.next()
                aa, r_aa = aa_ring.next()
                for c in range(4):
                    S.dma("sp", ca[:, c, :], cactT[c * 128:(c + 1) * 128, r0:r0 + 512], reads=[r_cact], writes=[r_ca])
                    S.dma("sp", aa[:, c, :], attnT[c * 128:(c + 1) * 128, r0:r0 + 512], reads=[r_attn], writes=[r_aa])
                for n in range(8):
                    for (b, col0) in ((0, n * 128), (1, D + n * 128)):
                        for dd in range(8):
                            S.pe(lambda e, b=b, dd=dd, col0=col0: e.matmul(ps[:, b, :], wg[:, dd, col0:col0 + 128], xT5v[:, dd, :], start=(dd == 0), stop=(dd == 7)),
                                 reads=[r_wg, r_xT], writes=[ps_res[b]])
                    for c in range(4):
                        S.pe(lambda e, c=c, n=n, ca=ca: e.matmul(ps[:, 2, :], wco[:, c, n * 128:(n + 1) * 128], ca[:, c, :], start=(c == 0), stop=(c == 3)),
                             reads=[r_wco, r_ca], writes=[ps_res[2]])
                    for c in range(4):
                        S.pe(lambda e, c=c, n=n, aa=aa: e.matmul(ps[:, 3, :], wao[:, c, n * 128:(n + 1) * 128], aa[:, c, :], start=(c == 0), stop=(c == 3)),
                             reads=[r_wao, r_aa], writes=[ps_res[3]])
                    gc, r_gc = g_ring.next()
                    ga, r_ga = g_ring.next()
                    S.act(lambda e, gc=gc, n=n: e.activation(out=gc, in_=ps[:, 0, :], func=AF.Sigmoid, bias=colv(C_BG + n), scale=1.0), reads=[ps_res[0], r_colt], writes=[r_gc])
                    S.act(lambda e, ga=ga, n=n: e.activation(out=ga, in_=ps[:, 1, :], func=AF.Sigmoid, bias=colv(C_BG + 8 + n), scale=1.0), reads=[ps_res[1], r_colt], writes=[r_ga])
                    S.dve(lambda e, gc=gc: e.tensor_tensor(out=gc, in0=gc, in1=ps[:, 2, :], op=ALU.mult), reads=[r_gc, ps_res[2]], writes=[r_gc])
                    S.dve(lambda e, ga=ga: e.tensor_tensor(out=ga, in0=ga, in1=ps[:, 3, :], op=ALU.mult), reads=[r_ga, ps_res[3]], writes=[r_ga])
                    S.pool(lambda e, gc=gc, ga=ga, n=n: e.tensor_tensor(out=mT[:, n, :], in0=gc, in1=ga, op=ALU.add), reads=[r_gc, r_ga], writes=[r_mT[n]])
                if r0 + 512 < NM:
                    load_T(lambda s, r1=r0 + 512: x1[r1 + s * 128:r1 + (s + 1) * 128, :], 4, xin_ring, xbf_ring, xT5v, r_xT, 6)
                for s in range(4):
                    z, r_z = z_ring.next()
                    S.dma("sp", z, x1[xrow + s * 128:xrow + (s + 1) * 128, :], reads=[r_x1], writes=[r_z])
                    S.act(lambda e, z=z: e.mul(out=z, in_=z, mul=ALPHA), reads=[r_z], writes=[r_z])
                    for half in range(2):
                        b = 4 + half
                        for n in range(8):
                            S.pe(lambda e, n=n, s=s, half=half, b=b: e.matmul(ps[:, b, :], mT[:, n, s * 128:(s + 1) * 128], wo[:, n, half * 512:(half + 1) * 512], start=(n == 0), stop=(n == 7)),
                                 reads=[r_mT[n], r_wo], writes=[ps_res[b]])
                    S.dve(lambda e, z=z: e.tensor_tensor(out=z, in0=z, in1=ps[:, 4:6, :].rearrange("p a b -> p (a b)"), op=ALU.add),
                          reads=[ps_res[4], ps_res[5], r_z], writes=[r_z])
                    k = nst % 2
                    nst += 1
                    layer_norm_store((stt_t[:, k, :], st_res2[k][0], mv_t[:, k, :], st_res2[k][1]), z, r_z, gB, bB, r_gb, None)
                    S.dma("pool", x2[r0 + s * 128:r0 + (s + 1) * 128, :], z, reads=[r_z], writes=[r_x2])

        S.full_barrier()
        r_yo = Res()
        ffn_phase("f2", x2, r_x2, NM, w_ffn[1][0], w_ffn[1][1], 4, yo, r_yo)
        stats = S.emit()
    return nc, stats


def _t5_bucket_np(rel):
    nb = 16
    ret = np.where(rel > 0, nb, 0)
    n = np.abs(rel)
    max_exact = 8
    nf = np.maximum(n, 1).astype(np.float32)
    large = max_exact + (np.log(nf / np.float32(max_exact)) / np.float32(math.log(128 / max_exact)) * np.float32(nb - max_exact)).astype(np.int32)
    large = np.minimum(large, nb - 1)
    return ret + np.where(n < max_exact, n, large)


def _onehot():
    i = np.arange(LB)
    b = _t5_bucket_np(639 - i)
    oh = np.zeros((32, LB), np.float32)
    oh[b, i] = 1.0
    return oh


def make_in_maps(inputs, SP, SS, NQ, ncores):
    f = lambda a: np.ascontiguousarray(np.asarray(a, dtype=np.float32))
    xp = f(inputs["x_prompt"])
    xs = f(inputs["x_sample"])[0]
    common = {
        "ffn1_gu": f(inputs["ffn1_w_gu"][0]), "ffn1_d": f(inputs["ffn1_w_down"][0]),
        "ffn2_gu": f(inputs["ffn2_w_gu"][0]), "ffn2_d": f(inputs["ffn2_w_down"][0]),
        "w_in": f(inputs["w_in"][0]), "w_co": f(inputs["w_conv_out"][0]),
        "w_ao": f(inputs["w_attn_out"][0]), "w_o": f(inputs["w_o"][0]),
        "lnv": f(np.stack([inputs["ln1_g"][0], inputs["ln1_b"][0], inputs["ln2_g"][0], inputs["ln2_b"][0], inputs["ln3_g"][0], inputs["ln3_b"][0]])),
        "sublg": f(inputs["subln_g"]).reshape(1, 128),
        "lamv": f(np.concatenate([inputs["lambda_q1"][0], inputs["lambda_k1"][0], inputs["lambda_q2"][0], inputs["lambda_k2"][0]])).reshape(1, 256),
        "tab": f(inputs["rel_bias_table"]),
        "oh": _onehot(),
        "ident": np.eye(128, dtype=np.float32),
    }
    NCOL = 16 + 124 + 12 + 4 + 1
    colbase = np.zeros((128, NCOL), np.float32)
    colbase[:, 0:16] = f(inputs["b_gate"][0]).reshape(16, 128).T
    cw = f(inputs["conv_w_dw"][0])[:, 0, :]
    for c in range(4):
        colbase[:, 16 + c * 31:16 + (c + 1) * 31] = cw[:, c * 128:(c + 1) * 128].T
    colbase[:, 140:144] = f(inputs["conv_b_dw"][0]).reshape(4, 128).T
    colbase[:, 144:148] = f(inputs["conv_ln_g"][0]).reshape(4, 128).T
    colbase[:, 148:152] = f(inputs["conv_ln_b"][0]).reshape(4, 128).T
    colbase[:, 156] = f(inputs["subln_g"]).reshape(128)
    maps = []
    for r in range(ncores):
        col = colbase.copy()
        col[:, 152] = 1.0 if r > 0 else 0.0
        col[:, 153] = 1.0 if r < ncores - 1 else 0.0
        col[:, 154] = 1.0 if r > 0 else 0.0
        col[:, 155] = 1.0 if r < ncores - 1 else 0.0
        rot = np.roll(xs, -NQ * r, axis=0)
        wrap = ((np.arange(SS // 128) * 128 + NQ * r) >= SS).astype(np.float32)
        m = dict(common)
        m["xr"] = np.ascontiguousarray(np.concatenate([xp[r], rot], axis=0))
        m["colp"] = col
        m["wrapm"] = np.ascontiguousarray(np.broadcast_to(wrap[None, :], (128, SS // 128)))
        maps.append(m)
    return maps


_CACHE = {}


def kernel(**inputs):
    SP, SS, NQ, ncores = 8192, 16384, 2048, 8
    if "nc" not in _CACHE:
        _CACHE["nc"] = build(SP, SS, NQ)[0]
    nc = _CACHE["nc"]
    in_maps = make_in_maps(inputs, SP, SS, NQ, ncores)
    res = run_bass_kernel_spmd(nc, in_maps, core_ids=list(range(ncores)))
    y_prompt = np.stack([np.asarray(res.results[r]["yo"][:SP], dtype=np.float32) for r in range(ncores)], axis=0)
    y_sample = np.concatenate([np.asarray(res.results[r]["yo"][SP:], dtype=np.float32) for r in range(ncores)], axis=0)[None]
    return (y_prompt, y_sample)
```
